# Optimizing a Trainium2 kernel written in Bass

```python
import jax, jax.numpy as jnp
from jax import lax
import numpy as np

D_MODEL = 1024
BATCH = 4
SEQ = 8192
DEPTH = 1

PLE_DIM = 256
ML_HEADS = 4
ML_DQK = 128
ML_DV = 256
ML_CONV = 4
ML_CHUNK = 64
SW_Q_HEADS = 16
SW_KV_HEADS = 4
SW_HEAD_DIM = 64
SW_WINDOW = 128
D_FF = 4 * D_MODEL
EPS = 1e-6

ML_QK_W = ML_HEADS * ML_DQK
ML_V_W = ML_HEADS * ML_DV
SW_Q_W = SW_Q_HEADS * SW_HEAD_DIM
SW_KV_W = SW_KV_HEADS * SW_HEAD_DIM
SPLIT_SIZES = (2 * ML_QK_W, ML_V_W, ML_V_W, 2 * ML_HEADS, SW_Q_W, SW_KV_W, SW_KV_W, D_MODEL, D_MODEL)
N_IN = sum(SPLIT_SIZES)

kernel_name = "hybrid_mlstm_swa_sink_parallel_block"


def rmsnorm(x, g):
    xf = x.astype(jnp.float32)
    xf = xf * lax.rsqrt(jnp.mean(xf * xf, axis=-1, keepdims=True) + EPS)
    return xf.astype(x.dtype) * g.astype(x.dtype)


def causal_depthwise_conv(x, w):
    c = x.shape[-1]
    return lax.conv_general_dilated(
        x, w.astype(x.dtype)[:, None, :], window_strides=(1,), padding=((ML_CONV - 1, 0),),
        dimension_numbers=("NWC", "WIO", "NWC"), feature_group_count=c)


def mlstm_chunkwise(q, k, v, ig, lf):
    B, S = q.shape[:2]
    L = ML_CHUNK
    nc = S // L

    def to_chunks(t):
        t = t.astype(jnp.float32).reshape((B, nc, L) + t.shape[2:])
        return jnp.moveaxis(jnp.moveaxis(t, 1, 0), 3, 2)

    xs = (to_chunks(q), to_chunks(k), to_chunks(v), to_chunks(ig), to_chunks(lf))
    causal = jnp.tril(jnp.ones((L, L), dtype=bool))

    def step(carry, chunk):
        C, n, m = carry
        qc, kc, vc, ic, fc = chunk
        b = jnp.cumsum(fc, axis=-1)
        log_d = jnp.where(causal, b[..., :, None] - b[..., None, :] + ic[..., None, :], -jnp.inf)
        inter = b + m[..., None]
        m_t = jnp.maximum(inter, jnp.max(log_d, axis=-1))
        scores = jnp.einsum("bhtk,bhsk->bhts", qc, kc) * jnp.exp(log_d - m_t[..., None])
        w_inter = jnp.exp(inter - m_t)
        num = (w_inter[..., None] * jnp.einsum("bhvk,bhtk->bhtv", C, qc)
               + jnp.einsum("bhts,bhsv->bhtv", scores, vc))
        den = w_inter * jnp.einsum("bhk,bhtk->bht", n, qc) + jnp.sum(scores, axis=-1)
        h = num / jnp.maximum(jnp.abs(den), jnp.exp(-m_t))[..., None]
        b_last = b[..., -1]
        log_w = b_last[..., None] - b + ic
        m_new = jnp.maximum(b_last + m, jnp.max(log_w, axis=-1))
        w = jnp.exp(log_w - m_new[..., None])
        decay = jnp.exp(b_last + m - m_new)
        C_new = decay[..., None, None] * C + jnp.einsum("bhs,bhsv,bhsk->bhvk", w, vc, kc)
        n_new = decay[..., None] * n + jnp.einsum("bhs,bhsk->bhk", w, kc)
        return (C_new, n_new, m_new), h

    H, dk, dv = q.shape[2], q.shape[3], v.shape[3]
    init = (jnp.zeros((B, H, dv, dk), jnp.float32), jnp.zeros((B, H, dk), jnp.float32),
            jnp.zeros((B, H), jnp.float32))
    _, h = lax.scan(step, init, xs)
    h = jnp.moveaxis(jnp.moveaxis(h, 2, 3), 0, 1)
    return h.reshape(B, S, H, dv)


def swa_with_sinks(q, k, v, sinks):
    B, S = q.shape[:2]
    W = SW_WINDOW
    nb = S // W
    G = SW_Q_HEADS // SW_KV_HEADS
    qb = q.reshape(B, nb, W, SW_KV_HEADS, G, SW_HEAD_DIM)

    def band(t):
        tb = t.reshape(B, nb, W, SW_KV_HEADS, SW_HEAD_DIM)
        prev = jnp.pad(tb, ((0, 0), (1, 0), (0, 0), (0, 0), (0, 0)))[:, :-1]
        return jnp.concatenate([prev, tb], axis=2)

    kb, vb = band(k), band(v)
    logits = jnp.einsum("bnqhgd,bnkhd->bnhgqk", qb, kb).astype(jnp.float32) * (SW_HEAD_DIM ** -0.5)
    qi = jnp.arange(W)[:, None]
    ki = jnp.arange(2 * W)[None, :]
    diff = qi + W - ki
    band_mask = (diff >= 0) & (diff < W)
    valid = band_mask[None] & ((jnp.arange(nb)[:, None, None] > 0) | (ki >= W)[None])
    logits = jnp.where(valid[None, :, None, None], logits, -jnp.inf)
    sink = sinks.astype(jnp.float32).reshape(SW_KV_HEADS, G)[None, None, :, :, None, None]
    m = jnp.maximum(jnp.max(logits, axis=-1, keepdims=True), sink)
    pexp = jnp.exp(logits - m)
    probs = pexp / (jnp.sum(pexp, axis=-1, keepdims=True) + jnp.exp(sink - m))
    out = jnp.einsum("bnhgqk,bnkhd->bnqhgd", probs.astype(v.dtype), vb)
    return out.reshape(B, S, SW_Q_W)


def setup_inputs(seed: int = 0) -> dict:
    key = jax.random.key(seed)
    ks = jax.random.split(key, 20)
    f32 = jnp.float32
    nrm = lambda k, shape, s: jax.random.normal(k, shape, f32) * s
    gain = lambda k, shape: 1.0 + 0.05 * jax.random.normal(k, shape, f32)
    b_if = jnp.concatenate([
        0.1 * jax.random.normal(ks[3], (DEPTH, ML_HEADS), f32),
        3.0 + 0.5 * jax.random.normal(ks[4], (DEPTH, ML_HEADS), f32),
    ], axis=-1)
    return {
        "x": jax.random.normal(ks[0], (BATCH, SEQ, D_MODEL), f32),
        "p": jax.random.normal(ks[1], (DEPTH, BATCH, SEQ, PLE_DIM), f32),
        "norm_mix_g": gain(ks[2], (DEPTH, D_MODEL)),
        "w_in": nrm(ks[5], (DEPTH, D_MODEL, N_IN), D_MODEL ** -0.5),
        "conv_qk": nrm(ks[6], (DEPTH, ML_CONV, 2 * ML_QK_W), ML_CONV ** -0.5),
        "b_if": b_if,
        "mlstm_norm_g": gain(ks[7], (DEPTH, ML_V_W)),
        "sinks": nrm(ks[8], (DEPTH, SW_Q_HEADS), 0.5),
        "w_branch_a": nrm(ks[9], (DEPTH, ML_V_W, D_MODEL), ML_V_W ** -0.5),
        "w_branch_b": nrm(ks[10], (DEPTH, SW_Q_W, D_MODEL), SW_Q_W ** -0.5),
        "w_out": nrm(ks[11], (DEPTH, D_MODEL, D_MODEL), D_MODEL ** -0.5),
        "norm_mlp_g": gain(ks[12], (DEPTH, D_MODEL)),
        "w_up": nrm(ks[13], (DEPTH, D_MODEL, D_FF), D_MODEL ** -0.5),
        "w_down": nrm(ks[14], (DEPTH, D_FF, D_MODEL), D_FF ** -0.5),
        "norm_ple_g": gain(ks[15], (DEPTH, D_MODEL)),
        "w_ple_gate": nrm(ks[16], (DEPTH, D_MODEL, D_MODEL), D_MODEL ** -0.5),
        "w_ple_proj": nrm(ks[17], (DEPTH, PLE_DIM, D_MODEL), PLE_DIM ** -0.5),
        "final_norm_g": gain(ks[18], (D_MODEL,)),
    }


def reference(x, p, norm_mix_g, w_in, conv_qk, b_if, mlstm_norm_g, sinks, w_branch_a, w_branch_b,
              w_out, norm_mlp_g, w_up, w_down, norm_ple_g, w_ple_gate, w_ple_proj, final_norm_g):
    B, S, _ = x.shape
    split_idx = [int(v) for v in np.cumsum(SPLIT_SIZES)[:-1]]
    for i in range(DEPTH):
        h = rmsnorm(x, norm_mix_g[i])
        proj = h @ w_in[i]
        qk_ml, v_ml, o_ml, if_pre, q_sw, k_sw, v_sw, g_a, g_b = jnp.split(proj, split_idx, axis=-1)

        qk_ml = jax.nn.silu(causal_depthwise_conv(qk_ml, conv_qk[i]))
        q_ml, k_ml = jnp.split(qk_ml, 2, axis=-1)
        q_ml = q_ml.reshape(B, S, ML_HEADS, ML_DQK) * (ML_DQK ** -0.5)
        k_ml = k_ml.reshape(B, S, ML_HEADS, ML_DQK)
        gates = (if_pre + b_if[i].astype(if_pre.dtype)).astype(jnp.float32)
        ig = gates[..., :ML_HEADS]
        lf = jax.nn.log_sigmoid(gates[..., ML_HEADS:])
        h_ml = mlstm_chunkwise(q_ml, k_ml, v_ml.reshape(B, S, ML_HEADS, ML_DV), ig, lf)
        h_ml = h_ml * lax.rsqrt(jnp.mean(h_ml * h_ml, axis=-1, keepdims=True) + EPS)
        h_ml = h_ml.reshape(B, S, ML_V_W).astype(x.dtype) * mlstm_norm_g[i].astype(x.dtype)
        y_a = jax.nn.sigmoid(o_ml) * h_ml

        y_b = swa_with_sinks(q_sw.reshape(B, S, SW_Q_HEADS, SW_HEAD_DIM),
                             k_sw.reshape(B, S, SW_KV_HEADS, SW_HEAD_DIM),
                             v_sw.reshape(B, S, SW_KV_HEADS, SW_HEAD_DIM), sinks[i])

        merged = jax.nn.sigmoid(g_a) * (y_a @ w_branch_a[i]) + jax.nn.sigmoid(g_b) * (y_b @ w_branch_b[i])
        x = x + merged @ w_out[i]

        u = rmsnorm(x, norm_mlp_g[i]) @ w_up[i]
        x = x + jnp.square(jax.nn.relu(u)) @ w_down[i]

        gate = jax.nn.sigmoid(rmsnorm(x, norm_ple_g[i]) @ w_ple_gate[i])
        x = x + gate * (p[i].astype(x.dtype) @ w_ple_proj[i])
    return rmsnorm(x, final_norm_g)
```

```python
from contextlib import ExitStack
import numpy as np
import concourse.bass as bass
import concourse.mybir as mybir
from concourse.bass_utils import run_bass_kernel_spmd

F32 = mybir.dt.float32
BF16 = mybir.dt.bfloat16
AF = mybir.ActivationFunctionType
ALU = mybir.AluOpType

ENGS = ("pe", "act", "dve", "pool", "sp")
NDMASEM = 8


class Buf:
    __slots__ = ("name", "writers", "readers", "excl")

    def __init__(self, name, excl=False):
        self.name = name
        self.writers = {}
        self.readers = {}
        self.excl = excl


class Op:
    __slots__ = ("eng", "fn", "deps", "signal", "count", "idx", "is_dma", "dma_slot", "dma_val", "waits")

    def __init__(self, eng, fn, is_dma):
        self.eng = eng
        self.fn = fn
        self.deps = []
        self.signal = False
        self.count = 0
        self.is_dma = is_dma
        self.dma_slot = None
        self.dma_val = 0
        self.waits = []


class Sched:
    def __init__(self, nc, same_engine_sync=True):
        self.nc = nc
        self.ops = {e: [] for e in ENGS}
        self.all_ops = []
        self.same_engine_sync = same_engine_sync

    def _add(self, eng, fn, reads, writes, is_dma):
        op = Op(eng, fn, is_dma)
        op.idx = len(self.all_ops)
        deps = {}
        for b in reads:
            for w in b.writers.values():
                deps[id(w)] = w
            if b.excl:
                for r in b.readers.values():
                    if r.eng != eng:
                        deps[id(r)] = r
        for b in writes:
            for w in b.writers.values():
                deps[id(w)] = w
            for r in b.readers.values():
                deps[id(r)] = r
        op.deps = list(deps.values())
        key = eng if not is_dma else ("dma", op.idx)
        for b in reads:
            b.readers[key] = op
        for b in writes:
            b.writers = {key: op}
            b.readers = {}
        self.ops[eng].append(op)
        self.all_ops.append(op)
        return op

    def op(self, eng, fn, reads=(), writes=()):
        return self._add(eng, fn, reads, writes, False)

    def dma(self, eng, fn, reads=(), writes=()):
        return self._add(eng, fn, reads, writes, True)

    def _skip_same(self, d, op):
        return (d.eng == op.eng and not op.is_dma and not d.is_dma
                and (d.eng in ("pe", "sp") or not self.same_engine_sync))

    def finalize(self):
        dma_i = {e: 0 for e in ENGS}
        for op in self.all_ops:
            if op.is_dma:
                i = dma_i[op.eng]
                dma_i[op.eng] += 1
                op.dma_slot = (op.eng, i % NDMASEM)
                op.dma_val = 16 * (i // NDMASEM + 1)
        for op in self.all_ops:
            for d in op.deps:
                if d.is_dma or self._skip_same(d, op):
                    continue
                d.signal = True
        cnt = {e: 0 for e in ENGS}
        for op in self.all_ops:
            if op.signal and not op.is_dma:
                cnt[op.eng] += 1
                op.count = cnt[op.eng]
        waited = {e: {} for e in ENGS}
        prev_dma_on_slot = {}
        for op in self.all_ops:
            w = waited[op.eng]
            need = {}
            for d in op.deps:
                if d.is_dma:
                    key = ("dma",) + d.dma_slot
                    val = d.dma_val
                else:
                    if self._skip_same(d, op):
                        continue
                    key = ("eng", d.eng)
                    val = d.count
                if w.get(key, 0) >= val:
                    continue
                if need.get(key, 0) < val:
                    need[key] = val
            if op.is_dma:
                p = prev_dma_on_slot.get(op.dma_slot)
                if p is not None:
                    key = ("dma",) + p.dma_slot
                    if w.get(key, 0) < p.dma_val and need.get(key, 0) < p.dma_val:
                        need[key] = p.dma_val
                prev_dma_on_slot[op.dma_slot] = op
            for key, val in need.items():
                w[key] = val
            op.waits = list(need.items())
        self.final_dma = dict(prev_dma_on_slot)

    def emit(self):
        nc = self.nc
        self.finalize()
        with ExitStack() as es:
            esem = {e: es.enter_context(nc.semaphore("s_" + e)) for e in ENGS}
            dsem = {}
            for e in ENGS:
                if any(o.is_dma for o in self.ops[e]):
                    for k in range(NDMASEM):
                        dsem[(e, k)] = es.enter_context(nc.semaphore("d_%s%d" % (e, k)))
            block = es.enter_context(nc.Block())

            def run(eng_name, eng):
                for op in self.ops[eng_name]:
                    for key, val in op.waits:
                        if key[0] == "eng":
                            eng.wait_ge(esem[key[1]], val)
                        else:
                            eng.wait_ge(dsem[(key[1], key[2])], val)
                    ins = op.fn(eng)
                    if op.is_dma:
                        ins.then_inc(dsem[op.dma_slot], 16)
                    elif op.signal:
                        ins.then_inc(esem[eng_name], 1)
                if eng_name == "sp":
                    for slot, p in self.final_dma.items():
                        eng.wait_ge(dsem[slot], p.dma_val)

            @block.tensor
            def _(e):
                run("pe", e)

            @block.scalar
            def _(e):
                run("act", e)

            @block.vector
            def _(e):
                run("dve", e)

            @block.gpsimd
            def _(e):
                run("pool", e)

            @block.sync
            def _(e):
                run("sp", e)


D = 1024
KC = 8
MT = 512
NSUB = 4
EPS = 1e-6
DKS = 128 ** -0.5
NG = 39
GSZ = 4096
NWB = 4
STOP = None
DBG = 9

C_QK, C_V, C_O, C_IF, C_QS, C_KS, C_VS, C_GA, C_GB = 0, 1024, 2048, 3072, 3080, 4104, 4360, 4616, 5640

G_KML, G_V0, G_V1, G_SMALL, G_QML, G_KSW, G_QS0, G_QS1, G_O0, G_O1 = range(10)
G_GA0, G_GA1, G_GB0, G_GB1, G_WA0, G_WA1, G_WB0, G_WB1, G_WO0, G_WO1 = range(10, 20)
G_UP0 = 20
G_DN0 = 28
G_PG0, G_PG1, G_PP = 36, 37, 38

GN_MIX, GN_MIX8, GN_ML05, GN_ONE, GN_HALF, GN_MLP, GN_PLE = range(7)


def group_gain(g):
    if g in (G_QS0, G_QS1):
        return GN_MIX8
    if g <= G_GB1:
        return GN_MIX
    if g in (G_WA0, G_WA1):
        return GN_ML05
    if g in (G_WB0, G_WB1):
        return GN_ONE
    if g in (G_WO0, G_WO1):
        return GN_HALF
    if G_UP0 <= g < G_DN0:
        return GN_MLP
    if G_DN0 <= g < G_PG0:
        return GN_ONE
    if g in (G_PG0, G_PG1):
        return GN_PLE
    return GN_HALF


PRE_GROUPS = [G_SMALL, G_KML, G_V0, G_V1]
PRE_LAST_GROUPS = [G_SMALL, G_KML, G_V0, G_V1, G_QML, G_KSW]
MAIN_GROUPS = ([G_KML, G_V0, G_V1, G_SMALL, G_QML, G_KSW, G_QS0, G_QS1, G_O0, G_O1, G_GA0, G_GA1, G_GB0, G_GB1,
                G_WA0, G_WB0, G_WA1, G_WB1, G_WO0, G_WO1] + list(range(G_UP0, G_UP0 + 8))
               + list(range(G_DN0, G_DN0 + 8)) + [G_PG0, G_PP, G_PG1])
CONV_ORDER = PRE_LAST_GROUPS + [g for g in MAIN_GROUPS if g not in PRE_LAST_GROUPS]


class Prog:
    def __init__(self, npre, nmain, same_engine_sync=True):
        self.npre, self.nmain = npre, nmain
        nc = bass.Bass("TRN2", target_bir_lowering=False)
        self.nc = nc
        self.S = Sched(nc, same_engine_sync)
        self.es = ExitStack()
        self.ring_i = 0
        self.cnt = {}
        self.cvt_queue = []
        self.ps_reserved = set()
        self.pending_out = []
        self.copy_x_pending = False
        self.marks = []
        self._alloc()
        self._build()
        self.S.emit()
        self.es.close()

    def sb(self, name, shape, dt):
        return self.es.enter_context(self.nc.sbuf_tensor("sb_" + name, shape, dt))

    def rr(self, name, n):
        i = self.cnt.get(name, 0)
        self.cnt[name] = i + 1
        return i % n

    def mark(self, name):
        self.marks.append((name, len(self.S.ops['pe'])))

    def ps_next(self):
        while True:
            i = self.ring_i % 8
            self.ring_i += 1
            if i not in self.ps_reserved:
                return self.ps[i], self.Bps[i]

    def ps_reserve(self):
        ps, Bp = self.ps_next()
        self.ps_reserved.add(self.Bps.index(Bp))
        return ps, Bp

    def ps_release(self, Bp):
        self.ps_reserved.discard(self.Bps.index(Bp))

    def op(self, eng, fn, reads=(), writes=()):
        return self.S.op(eng, fn, reads, writes)

    def act(self, out, in_, func, reads, writes, **kw):
        self.S.op("act", lambda e: e.activation(out=out, in_=in_, func=func, **kw), reads, writes)

    def tsc(self, eng, out, in0, s1, s2, op0, op1, reads, writes):
        if op1 is None:
            self.S.op(eng, lambda e: e.tensor_scalar(out=out, in0=in0, scalar1=s1, scalar2=None, op0=op0), reads, writes)
        else:
            self.S.op(eng, lambda e: e.tensor_scalar(out=out, in0=in0, scalar1=s1, scalar2=s2, op0=op0, op1=op1), reads, writes)

    def stt(self, out, in0, scalar, in1, op0, op1, reads, writes):
        self.S.op("dve", lambda e: e.scalar_tensor_tensor(out=out, in0=in0, scalar=scalar, in1=in1, op0=op0, op1=op1), reads, writes)

    def tt(self, eng, out, in0, in1, op, reads, writes):
        self.S.op(eng, lambda e: e.tensor_tensor(out=out, in0=in0, in1=in1, op=op), reads, writes)

    def cp(self, eng, out, in_, reads, writes):
        if eng == "act":
            self.act(out, in_, AF.Copy, reads, writes)
        else:
            self.S.op(eng, lambda e: e.tensor_copy(out=out, in_=in_), reads, writes)

    def mm(self, out, lhsT, rhs, start, stop, reads, writes):
        self.S.op("pe", lambda e: e.matmul(out, lhsT=lhsT, rhs=rhs, start=start, stop=stop), reads, writes)

    def tr(self, out, in_, reads, writes):
        idb = self.identb
        self.S.op("pe", lambda e: e.transpose(out=out, in_=in_, identity=idb[:]), list(reads) + [self.Bconst], writes)

    def dma(self, out, in_, reads, writes, eng="sp"):
        self.S.dma(eng, lambda e: e.dma_start(out=out, in_=in_), reads, writes)

    def _alloc(self):
        nc = self.nc
        npre, nmain = self.npre, self.nmain
        dt_in = lambda name, shape: nc.dram_tensor(name, shape, F32, kind="ExternalInput").ap()
        self.x_pre = dt_in("x_pre", [npre * MT, D])
        self.x_main = dt_in("x_main", [nmain * MT, D])
        self.p_main = dt_in("p_main", [nmain * MT, 256])
        self.wg = dt_in("wg", [NG, 128, GSZ])
        self.gains_d = dt_in("gains", [128, 4, 8])
        self.convw_d = dt_in("convw", [128, 8, 4])
        self.bif_d = dt_in("bif", [128, 8])
        self.sinks_d = dt_in("sinks", [128, 16])
        self.gfin_d = dt_in("gfin", [128, D])
        self.gmlb_d = dt_in("gmlb", [128, D])
        self.cmat_d = dt_in("cmat", [128, 3, 128])
        self.mpair_d = dt_in("mpair", [128, 2, 2, 128])
        self.out_d = nc.dram_tensor("out", [nmain * MT, D], F32, kind="ExternalOutput").ap()
        self.scr = nc.dram_tensor("wscr", [NG, 128, GSZ], BF16, kind="Internal").ap()
        self.Bscr = [Buf("scr%d" % g) for g in range(NG)]

        sb = self.sb
        B = lambda n: Buf(n)
        self.cmat = sb("cmat", [128, 3, 128], F32)
        self.identb = sb("identb", [128, 128], BF16)
        self.maskb = sb("maskb", [128, 128], BF16)
        self.mpair = sb("mpair", [128, 2, 2, 128], BF16)
        self.gains = sb("gains", [128, 4, 8], F32)
        self.convw = sb("convw", [128, 8, 4], F32)
        self.bif = sb("bif", [128, 8], F32)
        self.sinks = sb("sinks", [128, 16], F32)
        self.esink = sb("esink", [128, 16], F32)
        self.gfin = sb("gfin", [128, D], F32)
        self.mhalf = sb("mhalf", [128, 16], F32)
        self.onecol = sb("onecol", [128, 1], BF16)
        self.Bconst = B("const")
        self.xs = sb("xs", [128, NSUB, D], F32)
        self.Bx = [B("x%d" % i) for i in range(NSUB)]
        self.hT = sb("hT", [128, KC, MT], BF16)
        self.BhT = [B("hT%d" % i) for i in range(NSUB)]
        self.hn = sb("hn", [128, 2, D], BF16)
        self.Bhn = [B("hn0"), B("hn1")]
        self.junk = sb("junk", [128, 2, 256], BF16)
        self.Bjunk = [B("junk0"), B("junk1")]
        self.nsc = sb("nsc", [128, 2 * NSUB, 4], F32)
        self.Bnsc = [B("nsc%d" % i) for i in range(2 * NSUB)]
        self.gmlb = sb("gmlb", [128, D], F32)
        self.raw = sb("raw", [128, 4, MT + 4], BF16)
        self.Braw = [B("raw%d" % i) for i in range(4)]
        self.dg = sb("dg", [128, 8, 4, 128], BF16)
        self.halo = sb("halo", [128, 8, 3], BF16)
        self.Bhalo = [B("halo%d" % i) for i in range(8)]
        self.acc = sb("acc", [128, 2, MT], F32)
        self.Bacc = [B("acc0"), B("acc1")]
        self.th = sb("th", [128, 2, MT], F32)
        self.Bth = [B("th0"), B("th1")]
        self.qkT = sb("qkT", [128, 8, MT], BF16)
        self.Bqk = [B("qk%d" % i) for i in range(8)]
        self.vml = sb("vml", [128, NSUB, D], BF16)
        self.Bv = [B("v%d" % i) for i in range(NSUB)]
        self.pld = self.vml[:, 0:2, :].bitcast(F32).rearrange("p a b -> p (a b)").rearrange("p (s f) -> p s f", s=NSUB)
        self.Bpld = [self.Bv[0], self.Bv[1]]
        self.to1 = sb("to1", [128, NSUB, D], BF16)
        self.Bto = [B("to%d" % i) for i in range(NSUB)]
        self.gsb = sb("gsb", [128, NSUB, 8], F32)
        self.Bgsb = B("gsb")
        self.gsc = sb("gsc", [128, 16, 16], F32)
        self.Bgsc = B("gsc")
        self.qsT = sb("qsT", [128, 8, MT], BF16)
        self.Bqs = [B("qs%d" % i) for i in range(8)]
        self.ksT = sb("ksT", [128, 4, MT + 128], BF16)
        self.Bks = [B("ks%d" % i) for i in range(4)]
        self.vs = sb("vs", [128, 5, 4, 65], BF16)
        self.Bvs = [B("vs%d" % i) for i in range(5)]
        self.tg = sb("tgab", [128, 16, MT], BF16)
        self.Btg = [B("tg%d" % i) for i in range(16)]
        self.yT = sb("yT", [128, 16, MT], BF16)
        self.ByT = [[B("yaT%d" % i) for i in range(NSUB)], [B("ybT%d" % i) for i in range(NSUB)]]
        self.xtmp = self.yT[:].bitcast(F32).rearrange("p a b -> p (a b)").rearrange("p (s f) -> p s f", s=NSUB)
        self.Bxtmp = [self.ByT[0], self.ByT[0], self.ByT[1], self.ByT[1]]
        self.ya = sb("ya", [128, 2, D], BF16)
        self.Bya = [B("ya0"), B("ya1")]
        self.yb = sb("yb", [128, 2, D], BF16)
        self.Byb = [B("yb0"), B("yb1")]
        self.pT = sb("pT", [128, 2, 4, 128], BF16)
        self.BpT = [B("pT0"), B("pT1")]
        self.kw = sb("kw", [128, 2, 4, 128], BF16)
        self.Bkw = [B("kw0"), B("kw1")]
        self.E = sb("E", [128, 4, 2, MT], BF16)
        self.BE = [[B("E%d0" % i), B("E%d1" % i)] for i in range(4)]
        self.tmpA = sb("tmpA", [128, 2, MT], F32)
        self.BtA = [B("tA0"), B("tA1")]
        self.tmpB = sb("tmpB", [128, 2, MT], F32)
        self.BtB = [B("tB0"), B("tB1")]
        self.ot = [self.tmpA[:].rearrange("p a b -> p (a b)"), self.tmpB[:].rearrange("p a b -> p (a b)")]
        self.mpair_f = self.ot[0][:, 0:512].rearrange("p (a b c) -> p a b c", a=2, b=2)
        self.Bot = [self.BtA, self.BtB]
        self.rl = sb("rl", [128, 2, MT], F32)
        self.Brl = [B("rl0"), B("rl1")]

        self.C = sb("C", [128, 4, 256], F32)
        self.BC = [B("C%d" % i) for i in range(4)]
        self.nst = sb("nst", [128, 8], F32)
        self.Bn = B("n")
        self.Cin = sb("Cin", [128, 4, 256], BF16)
        self.BCin = [B("Cin%d" % i) for i in range(4)]
        self.nin = sb("nin", [128, 4], BF16)
        self.Bnin = B("nin")
        self.msc = sb("msc", [128, 2, 8, 4], F32)
        self.Bmsc = [B("msc0"), B("msc1")]
        self.ssc = sb("ssc", [128, 2, 2, 4], F32)
        self.Bssc = [B("ssc0"), B("ssc1")]
        self.pbf = sb("pbf", [128, NSUB, 256], BF16)
        self.Bpbf = B("pbf")
        self.ppT = sb("ppT", [128, 2, MT], BF16)
        self.BppT = B("ppT")
        self.tgp, self.Btgp = self.acc, self.Bacc
        self.ot += [self.acc[:].rearrange("p a b -> p (a b)"), self.rl[:].rearrange("p a b -> p (a b)")]
        self.Bot += [self.Bacc, self.Brl]
        self.ptmp, self.Bptmp = self.th, self.Bth
        self.wb = [sb("wb%d" % i, [128, GSZ], BF16) for i in range(NWB)]
        self.Bwb = [B("wb%d" % i) for i in range(NWB)]
        self.Bout = B("outd")
        self.ps = [self.es.enter_context(nc.psum_tensor("ps%d" % i, [128, 512], F32)) for i in range(8)]
        self.Bps = [Buf("ps%d" % i, excl=True) for i in range(8)]

    def w_init(self):
        seq = []
        for t in range(self.npre):
            seq += PRE_LAST_GROUPS if t == self.npre - 1 else PRE_GROUPS
        for t in range(self.nmain):
            seq += MAIN_GROUPS
        self.wseq = seq
        self.w_i = 0
        self.w_issued = 0

    def w_next(self, g, prev_live=False):
        assert self.wseq[self.w_i] == g, (self.w_i, self.wseq[self.w_i], g)
        while self.w_issued < min(len(self.wseq), self.w_i + NWB - (1 if prev_live else 0)):
            k = self.w_issued
            gg = self.wseq[k]
            b = k % NWB
            self.dma(self.wb[b][:], self.scr[gg], [self.Bscr[gg]], [self.Bwb[b]])
            self.w_issued += 1
        b = self.w_i % NWB
        self.w_i += 1
        return self.wb[b], self.Bwb[b]

    def setup(self):
        n0 = len(self.S.all_ops)
        Bcm, Bgn, Bcw, Bbf, Bsk, Bgf, Bgm, Bes = (Buf(n) for n in ("c_cm", "c_gn", "c_cw", "c_bf", "c_sk", "c_gf", "c_gm", "c_es"))
        self.dma(self.cmat[:], self.cmat_d, [], [Bcm])
        self.dma(self.convw[:], self.convw_d, [], [Bcw])
        self.dma(self.mpair_f, self.mpair_d, [], self.BtA)
        self.dma(self.gains[:], self.gains_d, [], [Bgn])
        self.dma(self.bif[:], self.bif_d, [], [Bbf])
        self.dma(self.sinks[:], self.sinks_d, [], [Bsk])
        self.dma(self.gfin[:], self.gfin_d, [], [Bgf])
        self.dma(self.gmlb[:], self.gmlb_d, [], [Bgm])
        cm = self.cmat
        self.cp("dve", self.identb[:], cm[:, 0, :], [Bcm], [Buf("c_id")])
        self.cp("dve", self.maskb[:], cm[:, 1, :], [Bcm], [Buf("c_mk")])
        self.op("dve", lambda e: e.memset(self.mhalf[:], -0.5), [], [Buf("c_mh")])
        self.op("dve", lambda e: e.memset(self.onecol[:], 1.0), [], [Buf("c_oc")])
        self.tsc("dve", self.convw[:], self.convw[:], 0.5, None, ALU.mult, None, [Bcw], [Bcw])
        for rc in range(8):
            for j in range(4):
                self.tsc("dve", self.dg[:, rc, j, :], cm[:, 0, :], self.convw[:, rc, j:j + 1], None, ALU.mult, None, [Bcm, Bcw], [Buf("c_dg")])
        self.cp("dve", self.mpair[:], self.mpair_f, self.BtA, [Buf("c_mp")])
        self.tsc("dve", self.gmlb[:], self.gmlb[:], 0.5, None, ALU.mult, None, [Bgm], [Bgm])
        self.act(self.esink[:], self.sinks[:], AF.Exp, [Bsk], [Bes])
        self.Bconst.writers = {("setup", i): o for i, o in enumerate(self.S.all_ops[n0:])}
        for h in range(4):
            self.op("pool", lambda e, h=h: e.memset(self.C[:, h, :], 0.0), [], [self.BC[h]])
        self.op("pool", lambda e: e.memset(self.nst[:], 0.0), [], [self.Bn])
        for c in range(8):
            self.op("pool", lambda e, c=c: e.memset(self.halo[:, c, :], 0.0), [], [self.Bhalo[c]])
        for b in range(5):
            self.op("pool", lambda e, b=b: e.memset(self.vs[:, b, :, 64:65], 1.0), [], [self.Bvs[b]])

    def convert_weights(self, glist):
        for g in glist:
            self.dma(self.scr[g], self.wg[g], [], [self.Bscr[g]], eng="pool")

    def cvt_one(self):
        if self.cvt_queue:
            self.convert_weights([self.cvt_queue.pop(0)])

    def xsrc(self, sub, tmp=False):
        if tmp:
            return self.xtmp[:, sub, :], self.Bxtmp[sub]
        return self.xs[:, sub, :], [self.Bx[sub]]

    def norm_stats(self, sub, tmp=False):
        x, Bxl = self.xsrc(sub, tmp)
        sl = sub + (NSUB if tmp else 0)
        ns, Bn = self.nsc, self.Bnsc[sl]
        self.act(self.hn[:, sub % 2, :], x, AF.Square, Bxl, [Bn, self.Bhn[sub % 2]], accum_out=ns[:, sl, 0:1])
        self.tsc("dve", ns[:, sl, 1:2], ns[:, sl, 0:1], 1.0 / D, EPS, ALU.mult, ALU.add, [Bn], [Bn])
        self.tt("pool", ns[:, sl, 2:3], ns[:, sl, 1:2], self.mhalf[:, 0:1], ALU.pow, [Bn, self.Bconst], [Bn])
        self.cvt_one()

    def norm_apply(self, sub, gi, tmp=False):
        x, Bxl = self.xsrc(sub, tmp)
        sl = sub + (NSUB if tmp else 0)
        ns, Bn = self.nsc, self.Bnsc[sl]
        k = self.rr("hn", 2)
        self.act(self.hn[:, k, :], x, AF.Copy, Bxl + [Bn], [self.Bhn[k]], scale=ns[:, sl, 2:3])
        ps, Bp = self.ps_next()
        psb = ps[:].bitcast(BF16)
        for kc in range(KC):
            self.tr(psb[:, kc * 128:(kc + 1) * 128], self.hn[:, k, kc * 128:(kc + 1) * 128], [self.Bhn[k]], [Bp])
        self.tt("dve", self.hT[:, :, sub * 128:(sub + 1) * 128], psb.rearrange("p (k t) -> p k t", k=KC),
                self.gains[:, gi, :].unsqueeze(2).to_broadcast([128, KC, 128]), ALU.mult, [Bp, self.Bconst], [self.BhT[sub]])

    def norm_stage(self, gi, stats_done=False, tmp=False):
        if not stats_done:
            for sub in range(NSUB):
                self.norm_stats(sub, tmp)
        for sub in range(NSUB):
            self.norm_apply(sub, gi, tmp)

    def special_s0(self):
        F = F32
        tgf = self.tg[:].bitcast(F).rearrange("p a b -> p (a b)")
        ytf = self.yT[:].bitcast(F).rearrange("p a b -> p (a b)")
        hn32 = tgf[:, 0:1024]
        hT32 = tgf[:, 1024:2048].rearrange("p (k t) -> p k t", k=KC)
        w32 = [tgf[:, 2048:3072].rearrange("p (k n) -> p k n", k=KC), tgf[:, 3072:4096].rearrange("p (k n) -> p k n", k=KC)]
        qk32 = ytf[:, 0:1024].rearrange("p (c t) -> p c t", c=8)
        raw32 = [ytf[:, 1024:1024 + 131], ytf[:, 1280:1280 + 131]]
        acc32 = [ytf[:, 1536:1664], ytf[:, 1664:1792]]
        th32 = [ytf[:, 1792:1920], ytf[:, 1920:2048]]
        self.S32 = ytf[:, 2048:2560]
        Rt, Ry = list(self.Btg), self.ByT[0] + self.ByT[1]
        Bh, BhT, Bw, Bqk = Buf("hn32"), Buf("hT32"), [Buf("w32a"), Buf("w32b")], [Buf("qk32_%d" % i) for i in range(8)]
        Braw, Bacc, Bth, BS = [Buf("raw32a"), Buf("raw32b")], [Buf("acc32a"), Buf("acc32b")], [Buf("th32a"), Buf("th32b")], Buf("S32")
        self.BS32 = BS
        ns = self.nsc
        self.act(hn32, self.xs[:, 0, :], AF.Copy, [self.Bx[0], self.Bnsc[0]] + Rt, [Bh], scale=ns[:, 0, 2:3])
        idf = self.cmat[:, 0, :]
        for half in range(2):
            ps, Bp = self.ps_next()
            for q in range(4):
                kc = half * 4 + q
                self.S.op("pe", lambda e, o=ps[:, q * 128:(q + 1) * 128], i=hn32[:, kc * 128:(kc + 1) * 128]: e.transpose(out=o, in_=i, identity=idf),
                          [Bh, self.Bconst] + Rt, [Bp])
            self.tt("dve", hT32[:, half * 4:(half + 1) * 4, :], ps[:].rearrange("p (k t) -> p k t", k=4),
                    self.gains[:, 0, half * 4:(half + 1) * 4].unsqueeze(2).to_broadcast([128, 4, 128]), ALU.mult, [Bp, self.Bconst] + Rt, [BhT])
        cw = self.convw
        for rc in range(8):
            g, c = (G_QML, rc) if rc < 4 else (G_KML, rc - 4)
            k = rc % 2
            src = self.wg[g].rearrange("p (k n) -> p k n", k=KC)[:, :, c * 128:(c + 1) * 128]
            self.dma(w32[k], src, Rt, [Bw[k]])
            ps, Bp = self.ps_next()
            for kc in range(KC):
                self.mm(ps[:, 0:128], w32[k][:, kc, :], hT32[:, kc, :], kc == 0, kc == KC - 1, [Bw[k], BhT] + Rt, [Bp])
            self.cp("dve", raw32[k][:, 0:3], self.halo[:, rc, :], [self.Bhalo[rc]] + Ry, [Braw[k]])
            self.cp("act", raw32[k][:, 3:131], ps[:, 0:128], [Bp] + Ry, [Braw[k]])
            self.tsc("dve", acc32[k], raw32[k][:, 3:131], cw[:, rc, 3:4], None, ALU.mult, None, [Braw[k], self.Bconst] + Ry, [Bacc[k]])
            for j in (2, 1, 0):
                self.stt(acc32[k], raw32[k][:, j:j + 128], cw[:, rc, j:j + 1], acc32[k], ALU.mult, ALU.add, [Braw[k], Bacc[k], self.Bconst] + Ry, [Bacc[k]])
            self.act(th32[k], acc32[k], AF.Tanh, [Bacc[k]] + Ry, [Bth[k]])
            self.stt(qk32[:, rc, :], th32[k], 1.0, acc32[k], ALU.add, ALU.mult, [Bth[k], Bacc[k]] + Ry, [Bqk[rc]])
        ps, Bp = self.ps_next()
        for h in range(4):
            self.mm(ps[:, h * 128:(h + 1) * 128], qk32[:, 4 + h, :], qk32[:, h, :], True, True, [Bqk[4 + h], Bqk[h]] + Ry, [Bp])
        self.cp("act", self.S32, ps[:], [Bp] + Ry, [BS])

    def load_x(self, src, t):
        for sub in range(NSUB):
            r0 = t * MT + sub * 128
            self.dma(self.xs[:, sub, :], src[r0:r0 + 128, :], [], [self.Bx[sub]])

    def proj_fm(self, g, consumer):
        w, Bw = self.w_next(g)
        wv = w[:].rearrange("p (k n) -> p k n", k=KC)
        for c in range(4):
            ps, Bp = self.ps_next()
            for kc in range(KC):
                self.mm(ps[:], wv[:, kc, c * 128:(c + 1) * 128], self.hT[:, kc, :], kc == 0, kc == KC - 1,
                        [Bw] + self.BhT, [Bp])
            consumer(c, ps, Bp)

    def proj_tm(self, g, consumer, ncols=512, col0=0):
        w, Bw = self.w_next(g)
        wv = w[:].rearrange("p (k n) -> p k n", k=KC)
        for sub in range(NSUB):
            ps, Bp = self.ps_next()
            for kc in range(KC):
                self.mm(ps[:, 0:ncols], self.hT[:, kc, sub * 128:(sub + 1) * 128], wv[:, kc, col0:col0 + ncols], kc == 0, kc == KC - 1,
                        [Bw, self.BhT[sub]], [Bp])
            consumer(sub, ps, Bp)

    def conv_a(self, rc, ps, Bp):
        k = self.rr("raw", 4)
        raw, Br = self.raw[:, k, :], self.Braw[k]
        self.cp("pool", raw[:, 0:3], self.halo[:, rc, :], [self.Bhalo[rc]], [Br])
        self.cp("act", raw[:, 3:MT + 3], ps[:], [Bp], [Br])
        self.cp("pool", self.halo[:, rc, :], raw[:, MT:MT + 3], [Br], [self.Bhalo[rc]])
        return k

    def conv_b(self, rc, k):
        raw, Br = self.raw[:, k, :], self.Braw[k]
        ps, Bp = self.ps_next()
        for j in range(4):
            self.mm(ps[:], self.dg[:, rc, j, :], raw[:, j:j + MT], j == 0, j == 3, [Br, self.Bconst], [Bp])
        kt = self.rr("th", 2)
        th, Bt = self.th[:, kt, :], self.Bth[kt]
        self.act(th, ps[:], AF.Tanh, [Bp], [Bt])
        self.stt(self.qkT[:, rc, :], th, 1.0, ps[:], ALU.add, ALU.mult, [Bt, Bp], [self.Bqk[rc]])

    def proj_conv(self, g, base):
        ks = []
        self.proj_fm(g, lambda c, ps, Bp: ks.append(self.conv_a(base + c, ps, Bp)))
        for c in range(4):
            self.conv_b(base + c, ks[c])

    def cons_v(self, gi):
        def f(sub, ps, Bp):
            self.cp("act", self.vml[:, sub, gi * 512:(gi + 1) * 512], ps[:], [Bp], [self.Bv[sub]])
        return f

    def cons_small(self, sub, ps, Bp):
        self.cp("dve", self.vs[:, sub + 1, :, 0:64], ps[:, 0:256].rearrange("p (j d) -> p j d", j=4), [Bp], [self.Bvs[sub + 1]])
        self.cp("dve", self.gsb[:, sub, :], ps[:, 256:264], [Bp], [self.Bgsb])

    def cons_gates_only(self, sub, ps, Bp):
        self.cp("dve", self.gsb[:, sub, :], ps[:, 0:8], [Bp], [self.Bgsb])

    def cons_ksw(self, c, ps, Bp):
        self.cp("act", self.ksT[:, c, 128:MT + 128], ps[:], [Bp], [self.Bks[c]])

    def cons_qs(self, gi):
        def f(c, ps, Bp):
            cc = gi * 4 + c
            self.act(self.qsT[:, cc, :], ps[:], AF.Copy, [Bp], [self.Bqs[cc]], scale=0.125)
        return f

    def cons_o(self, gi):
        def f(sub, ps, Bp):
            o = self.to1[:, sub, gi * 512:(gi + 1) * 512]
            self.act(o, ps[:], AF.Tanh, [Bp], [self.Bto[sub]], scale=0.5)
            self.tsc("pool", o, o, 1.0, 1.0, ALU.add, ALU.mult, [self.Bto[sub]], [self.Bto[sub]])
            self.tt("pool", o, o, self.gmlb[:, gi * 512:(gi + 1) * 512], ALU.mult, [self.Bto[sub], self.Bconst], [self.Bto[sub]])
        return f

    def cons_g(self, base):
        def f(c, ps, Bp):
            cc = base + c
            self.act(self.tg[:, cc, :], ps[:], AF.Tanh, [Bp], [self.Btg[cc]], scale=0.5)
        return f

    def gates(self):
        G = self.gsc
        Bg = self.Bgsc
        gp = G[:, 0:2, :].rearrange("p a b -> p (a b)").rearrange("p (s e) -> p s e", s=4)
        self.tt("dve", gp, self.gsb[:], self.bif[:].unsqueeze(1).to_broadcast([128, 4, 8]), ALU.add,
                [self.Bgsb, self.Bconst], [Bg])
        tf = G[:, 2, :].rearrange("p (s h) -> p s h", s=4)
        self.act(tf, gp[:, :, 4:8], AF.Tanh, [Bg], [Bg], scale=0.5)
        lf = G[:, 3, :]
        self.act(lf, G[:, 2, :], AF.Ln, [Bg], [Bg], scale=0.5, bias=0.5)
        ps, Bp = self.ps_next()
        cm = self.cmat
        self.mm(ps[:, 0:16], cm[:, 1, :], lf, True, True, [Bg, self.Bconst], [Bp])
        self.mm(ps[:, 16:32], cm[:, 2, :], lf, True, True, [Bg, self.Bconst], [Bp])
        bt = G[:, 4:6, :].rearrange("p a b -> p (a b)")
        self.cp("dve", bt, ps[:, 0:32], [Bp], [Bg])
        b_, tot = G[:, 4, :], G[:, 5, :]
        bmb = G[:, 6, :]
        self.stt(bmb, tot, -0.5, b_, ALU.mult, ALU.add, [Bg], [Bg])
        ap_ = G[:, 7, :]
        self.tt("dve", ap_.rearrange("p (s h) -> p s h", s=4), gp[:, :, 0:4], bmb.rearrange("p (s h) -> p s h", s=4),
                ALU.subtract, [Bg], [Bg])
        e1, dlow, ebeta, e2b = G[:, 8, :], G[:, 9, :], G[:, 10, :], G[:, 11, :]
        self.act(e1, ap_, AF.Exp, [Bg], [Bg])
        self.act(dlow, bmb, AF.Exp, [Bg], [Bg], scale=-2.0)
        self.act(ebeta, tot, AF.Exp, [Bg], [Bg], scale=0.5)
        self.act(e2b, tot, AF.Exp, [Bg], [Bg])
        self.tsc("dve", G[:, 12, :], e1, DKS, None, ALU.mult, None, [Bg], [Bg])
        self.tt("dve", G[:, 13, :], e1, ebeta, ALU.mult, [Bg], [Bg])
        self.tsc("dve", G[:, 14, :], ebeta, DKS, None, ALU.mult, None, [Bg], [Bg])

    def mlstm_kt(self, sub):
        kk = self.rr("kw", 2)
        ps, Bp = self.ps_next()
        psb = ps[:].bitcast(BF16)
        for h in range(4):
            self.tr(psb[:, h * 128:(h + 1) * 128], self.qkT[:, 4 + h, sub * 128:(sub + 1) * 128], [self.Bqk[4 + h]], [Bp])
        return kk, ps, Bp

    def mlstm_dc(self, sub, kk, ps, Bp):
        G, Bg = self.gsc, self.Bgsc
        psb = ps[:].bitcast(BF16)
        wsc = G[:, 13, sub * 4:(sub + 1) * 4]
        self.tt("dve", self.kw[:, kk, :, :], psb[:, 0:512].rearrange("p (h d) -> p h d", h=4),
                wsc.unsqueeze(2).to_broadcast([128, 4, 128]), ALU.mult, [Bp, Bg], [self.Bkw[kk]])
        dC = []
        for hp in range(2):
            ps2, Bp2 = self.ps_next()
            for hh in range(2):
                h = hp * 2 + hh
                self.mm(ps2[:, hh * 256:(hh + 1) * 256], self.kw[:, kk, h, :], self.vml[:, sub, h * 256:(h + 1) * 256], True, True,
                        [self.Bkw[kk], self.Bv[sub]], [Bp2])
            dC.append((ps2, Bp2))
        psn, Bpn = self.ps_next()
        for h in range(4):
            self.mm(psn[:, h:h + 1], self.kw[:, kk, h, :], self.onecol[:], True, True, [self.Bkw[kk], self.Bconst], [Bpn])
        return dC, (psn, Bpn)

    def mlstm_update(self, sub, dC, dn):
        G, Bg = self.gsc, self.Bgsc
        for h in range(4):
            ps2, Bp2 = dC[h // 2]
            hh = h % 2
            idx = sub * 4 + h
            self.stt(self.C[:, h, :], self.C[:, h, :], G[:, 11, idx:idx + 1], ps2[:, hh * 256:(hh + 1) * 256], ALU.mult, ALU.add,
                     [self.BC[h], Bg, Bp2], [self.BC[h]])
        psn, Bpn = dn
        n, tmp = self.nst[:, 0:4], self.nst[:, 4:8]
        self.tt("dve", tmp, n, G[:, 11, sub * 4:(sub + 1) * 4], ALU.mult, [self.Bn, Bg], [self.Bn])
        self.tt("dve", n, tmp, psn[:, 0:4], ALU.add, [self.Bn, Bpn], [self.Bn])

    def mlstm_p1(self, sub, use_s32=False):
        G, Bg = self.gsc, self.Bgsc
        tsl = slice(sub * 128, (sub + 1) * 128)
        for h in range(4):
            idx = sub * 4 + h
            self.act(self.Cin[:, h, :], self.C[:, h, :], AF.Copy, [self.BC[h], Bg], [self.BCin[h]], scale=G[:, 14, idx:idx + 1])
        self.tt("dve", self.nin[:], self.nst[:, 0:4], G[:, 14, sub * 4:(sub + 1) * 4], ALU.mult, [self.Bn, Bg], [self.Bnin])
        if use_s32:
            Ssrc, BSl = self.S32, [self.BS32] + self.ByT[0] + self.ByT[1]
        else:
            psS, BpS = self.ps_next()
            for h in range(4):
                self.mm(psS[:, h * 128:(h + 1) * 128], self.qkT[:, 4 + h, tsl], self.qkT[:, h, tsl], True, True,
                        [self.Bqk[4 + h], self.Bqk[h]], [BpS])
            Ssrc, BSl = psS, [BpS]
        k = self.rr("pT", 2)
        for h in range(4):
            idx = sub * 4 + h
            self.stt(self.pT[:, k, h, :], Ssrc[:, h * 128:(h + 1) * 128], G[:, 12, idx:idx + 1], self.maskb[:], ALU.mult, ALU.mult,
                     BSl + [Bg, self.Bconst], [self.BpT[k]])
        kk = self.rr("kw", 2)
        ps, Bp = self.ps_next()
        psb = ps[:].bitcast(BF16)
        for h in range(4):
            self.tr(psb[:, h * 128:(h + 1) * 128], self.qkT[:, 4 + h, tsl], [self.Bqk[4 + h]], [Bp])
        wsc = G[:, 13, sub * 4:(sub + 1) * 4]
        self.tt("dve", self.kw[:, kk, :, :], psb[:, 0:512].rearrange("p (h d) -> p h d", h=4),
                wsc.unsqueeze(2).to_broadcast([128, 4, 128]), ALU.mult, [Bp, Bg], [self.Bkw[kk]])
        return dict(sub=sub, k=k, kk=kk)

    def mlstm_p2(self, cx):
        G, Bg = self.gsc, self.Bgsc
        sub, k, kk = cx["sub"], cx["k"], cx["kk"]
        tsl = slice(sub * 128, (sub + 1) * 128)
        nums = []
        for hp in range(2):
            ps2, Bp2 = self.ps_reserve()
            for hh in range(2):
                h = hp * 2 + hh
                o = ps2[:, hh * 256:(hh + 1) * 256]
                self.mm(o, self.pT[:, k, h, :], self.vml[:, sub, h * 256:(h + 1) * 256], True, False, [self.BpT[k], self.Bv[sub]], [Bp2])
                self.mm(o, self.qkT[:, h, tsl], self.Cin[:, h, :], False, True, [self.Bqk[h], self.BCin[h]], [Bp2])
            nums.append((ps2, Bp2))
        psd, Bpd = self.ps_next()
        for h in range(4):
            self.mm(psd[:, h:h + 1], self.pT[:, k, h, :], self.onecol[:], True, False, [self.BpT[k], self.Bconst], [Bpd])
            self.mm(psd[:, h:h + 1], self.qkT[:, h, tsl], self.nin[:, h:h + 1], False, True, [self.Bqk[h], self.Bnin], [Bpd])
        dC = []
        for hp in range(2):
            ps2, Bp2 = self.ps_next()
            for hh in range(2):
                h = hp * 2 + hh
                self.mm(ps2[:, hh * 256:(hh + 1) * 256], self.kw[:, kk, h, :], self.vml[:, sub, h * 256:(h + 1) * 256], True, True,
                        [self.Bkw[kk], self.Bv[sub]], [Bp2])
            dC.append((ps2, Bp2))
        psn, Bpn = self.ps_next()
        for h in range(4):
            self.mm(psn[:, h:h + 1], self.kw[:, kk, h, :], self.onecol[:], True, True, [self.Bkw[kk], self.Bconst], [Bpn])
        m = self.rr("msc", 2)
        M, Bm = self.msc, self.Bmsc[m]
        for h in range(4):
            ps2, Bp2 = nums[h // 2]
            hh = h % 2
            self.act(self.junk[:, m, :], ps2[:, hh * 256:(hh + 1) * 256], AF.Square, [Bp2], [self.Bjunk[m], Bm],
                     accum_out=M[:, m, 0, h:h + 1], scale=1.0 / 16)
        self.act(M[:, m, 1, :], psd[:, 0:4], AF.Square, [Bpd], [Bm])
        self.tt("dve", M[:, m, 2, :], M[:, m, 1, :], G[:, 9, sub * 4:(sub + 1) * 4], ALU.max, [Bm, Bg], [Bm])
        self.stt(M[:, m, 3, :], M[:, m, 2, :], EPS, M[:, m, 0, :], ALU.mult, ALU.add, [Bm], [Bm])
        self.tt("pool", M[:, m, 6, :], M[:, m, 3, :], self.mhalf[:, 0:4], ALU.pow, [Bm, self.Bconst], [Bm])
        self.mlstm_update(sub, dC, (psn, Bpn))
        cx["nums"], cx["m"] = nums, m
        return cx

    def mlstm_p3(self, cx):
        sub, nums, m = cx["sub"], cx["nums"], cx["m"]
        M, Bm = self.msc, self.Bmsc[m]
        ky = self.rr("ya", 2)
        for h in range(4):
            ps2, Bp2 = nums[h // 2]
            hh = h % 2
            self.stt(self.ya[:, ky, h * 256:(h + 1) * 256], ps2[:, hh * 256:(hh + 1) * 256], M[:, m, 6, h:h + 1],
                     self.to1[:, sub, h * 256:(h + 1) * 256], ALU.mult, ALU.mult, [Bp2, Bm, self.Bto[sub]], [self.Bya[ky]])
        for _, Bp2 in nums:
            self.ps_release(Bp2)
        cx["ky"] = ky

    def mlstm_p3b(self, cx):
        ky = cx["ky"]
        self.transpose_to(self.ya[:, ky, :], self.Bya[ky], 0, cx["sub"])

    def transpose_to(self, src, Bsrc, which, sub):
        ps, Bp = self.ps_next()
        psb = ps[:].bitcast(BF16)
        for kc in range(KC):
            self.tr(psb[:, kc * 128:(kc + 1) * 128], src[:, kc * 128:(kc + 1) * 128], [Bsrc], [Bp])
        self.cp("act", self.yT[:, which * 8:(which + 1) * 8, sub * 128:(sub + 1) * 128], psb.rearrange("p (k t) -> p k t", k=KC),
                [Bp], [self.ByT[which][sub]])

    def swa_p1(self, sub, j, first_block):
        mp = self.mpair[:, 0 if first_block else 1, :, :]
        mpb = mp.unsqueeze(2).to_broadcast([128, 2, 2, 128])
        ke = j
        banks = [self.ps_next(), self.ps_next()]
        for kb in range(2):
            ks = slice((sub + kb) * 128, (sub + kb + 1) * 128)
            for r in range(2):
                ps, Bp = banks[r]
                rs = slice(r * 64, (r + 1) * 64)
                self.mm(ps[:, kb * 256:(kb + 1) * 256].rearrange("p (c q) -> p c q", c=2),
                        self.ksT[rs, j, ks], self.qsT[rs, 2 * j:2 * j + 2, sub * 128:(sub + 1) * 128], True, True,
                        [self.Bks[j], self.Bqs[2 * j], self.Bqs[2 * j + 1]], [Bp])
        for r in range(2):
            ps, Bp = banks[r]
            Er = self.E[:, ke, r, :]
            self.act(Er, ps[:], AF.Exp, [Bp], [self.BE[ke][r]])
            Er4 = Er.rearrange("p (b c q) -> p b c q", b=2, c=2)
            self.tt("pool", Er4, Er4, mpb, ALU.mult, [self.BE[ke][r], self.Bconst], [self.BE[ke][r]])

    def swa_p2(self, sub, j, ky):
        ke = j
        pso, Bpo = self.ps_next()
        for g in range(4):
            c, r = g // 2, g % 2
            for kb in range(2):
                lhsT = self.E[:, ke, r, :].rearrange("p (b c q) -> p b c q", b=2, c=2)[:, kb, c, :]
                self.mm(pso[:, g * 65:(g + 1) * 65], lhsT, self.vs[:, sub + kb, j, :], kb == 0, kb == 1,
                        [self.BE[ke][r], self.Bvs[sub + kb]], [Bpo])
        s_ = self.rr("ssc", 2)
        SS, Bs = self.ssc, self.Bssc[s_]
        o3 = pso[:, 0:260].rearrange("p (g d) -> p g d", g=4)
        self.tt("dve", SS[:, s_, 0, :].unsqueeze(2), o3[:, :, 64:65], self.esink[:, 4 * j:4 * j + 4].unsqueeze(2), ALU.add,
                [Bpo, self.Bconst], [Bs])
        self.op("dve", lambda e, s_=s_: e.reciprocal(out=SS[:, s_, 1, :], in_=SS[:, s_, 0, :]), [Bs], [Bs])
        self.tt("dve", self.yb[:, ky, j * 256:(j + 1) * 256].rearrange("p (g d) -> p g d", g=4), o3[:, :, 0:64],
                SS[:, s_, 1, :].unsqueeze(2).to_broadcast([128, 4, 64]), ALU.mult, [Bpo, Bs], [self.Byb[ky]])

    def attention(self, sub, first_block, deferred):
        ky = self.rr("yb", 2)
        for f in deferred[:1]:
            f()
        cx = self.mlstm_p1(sub, use_s32=first_block)
        self.swa_p1(sub, 0, first_block)
        self.swa_p1(sub, 1, first_block)
        for f in deferred[1:]:
            f()
        del deferred[:]
        cx = self.mlstm_p2(cx)
        self.swa_p2(sub, 0, ky)
        self.swa_p2(sub, 1, ky)
        self.swa_p1(sub, 2, first_block)
        self.swa_p1(sub, 3, first_block)
        self.swa_p2(sub, 2, ky)
        self.swa_p2(sub, 3, ky)
        deferred.append(lambda: self.mlstm_p3(cx))
        deferred.append(lambda: self.mlstm_p3b(cx))
        deferred.append(lambda: self.transpose_to(self.yb[:, ky, :], self.Byb[ky], 1, sub))

    def swa_shift(self):
        for j in range(4):
            self.cp("pool", self.ksT[:, j, 0:128], self.ksT[:, j, MT:MT + 128], [self.Bks[j]], [self.Bks[j]])
        self.cp("pool", self.vs[:, 0, :, 0:64], self.vs[:, 4, :, 0:64], [self.Bvs[4]], [self.Bvs[0]])

    def merge(self):
        for half in range(2):
            wa, Bwa = self.w_next(G_WA0 + half)
            wbb, Bwb = self.w_next(G_WB0 + half, prev_live=True)
            wav = wa[:].rearrange("p (k n) -> p k n", k=KC)
            wbv = wbb[:].rearrange("p (k n) -> p k n", k=KC)
            for c in range(4):
                oc = half * 4 + c
                k = self.rr("tmpAB", 2)
                psa, Bpa = self.ps_next()
                for kc in range(KC):
                    self.mm(psa[:], wav[:, kc, c * 128:(c + 1) * 128], self.yT[:, kc, :], kc == 0, kc == KC - 1, [Bwa] + self.ByT[0], [Bpa])
                psb_, Bpb = self.ps_next()
                for kc in range(KC):
                    self.mm(psb_[:], wbv[:, kc, c * 128:(c + 1) * 128], self.yT[:, 8 + kc, :], kc == 0, kc == KC - 1, [Bwb] + self.ByT[1], [Bpb])
                self.stt(self.tmpA[:, k, :], self.tg[:, oc, :], 1.0, psa[:], ALU.add, ALU.mult, [self.Btg[oc], Bpa], [self.BtA[k]])
                self.stt(self.tmpB[:, k, :], self.tg[:, 8 + oc, :], 1.0, psb_[:], ALU.add, ALU.mult, [self.Btg[8 + oc], Bpb], [self.BtB[k]])
                self.tt("pool", self.hT[:, oc, :], self.tmpA[:, k, :], self.tmpB[:, k, :], ALU.add, [self.BtA[k], self.BtB[k]], self.BhT)

    def tm_residual(self, groups, lhs_src, Blhs_fn, nk, scale=1.0, after_sub=None):
        ws = [self.w_next(groups[0]), self.w_next(groups[1], prev_live=True)]
        for sub in range(NSUB):
            for half in range(2):
                w, Bw = ws[half]
                wv = w[:].rearrange("p (k n) -> p k n", k=nk)
                ps, Bp = self.ps_next()
                for kc in range(nk):
                    self.mm(ps[:], lhs_src[:, kc, sub * 128:(sub + 1) * 128], wv[:, kc, :], kc == 0, kc == nk - 1, [Bw] + Blhs_fn(sub), [Bp])
                xs = self.xs[:, sub, half * 512:(half + 1) * 512]
                self.stt(xs, ps[:], scale, xs, ALU.mult, ALU.add, [self.Bx[sub], Bp], [self.Bx[sub]])
            if after_sub is not None:
                after_sub(sub)

    def after_wout(self, sub):
        self.norm_stats(sub)
        if sub >= 2:
            self.norm_apply(sub - 2, 2)
        if sub == NSUB - 1:
            self.norm_apply(sub - 1, 2)
            self.norm_apply(sub, 2)

    def mlp(self):
        uT = []
        for c in range(8):
            uT.append((self.qsT[:, c, :], self.Bqs[c]))
        for c in range(8):
            uT.append((self.qkT[:, c, :], self.Bqk[c]))
        for c in range(16):
            uT.append((self.tg[:, c, :], self.Btg[c]))
        self.uT = uT
        for g in range(8):
            def cons(c, ps, Bp, g=g):
                cc = g * 4 + c
                k = self.rr("rl", 2)
                self.act(self.rl[:, k, :], ps[:], AF.Relu, [Bp], [self.Brl[k]])
                u, Bu = uT[cc]
                self.tt("pool", u, self.rl[:, k, :], self.rl[:, k, :], ALU.mult, [self.Brl[k]], [Bu])
            self.proj_fm(G_UP0 + g, cons)
        for half in range(2):
            pss = [self.ps_next() for _ in range(NSUB)]
            for kg in range(4):
                w, Bw = self.w_next(G_DN0 + half * 4 + kg)
                wv = w[:].rearrange("p (k n) -> p k n", k=KC)
                for sub in range(NSUB):
                    ps, Bp = pss[sub]
                    for kc in range(KC):
                        u, Bu = uT[kg * 8 + kc]
                        self.mm(ps[:], u[:, sub * 128:(sub + 1) * 128], wv[:, kc, :], kg == 0 and kc == 0, kg == 3 and kc == KC - 1, [Bw, Bu], [Bp])
            for sub in range(NSUB):
                ps, Bp = pss[sub]
                xs = self.xs[:, sub, half * 512:(half + 1) * 512]
                self.tt("dve", xs, xs, ps[:], ALU.add, [self.Bx[sub], Bp], [self.Bx[sub]])
                if half == 1:
                    self.norm_stats(sub)

    def ple_p(self, t):
        for sub in range(NSUB):
            r0 = t * MT + sub * 128
            self.dma(self.pld[:, sub, :], self.p_main[r0:r0 + 128, :], [], self.Bpld)
        self.tsc("pool", self.pbf[:], self.pld, 0.5, 0.0, ALU.mult, ALU.add, self.Bpld, [self.Bpbf])
        ps, Bp = self.ps_next()
        psb = ps[:].bitcast(BF16)
        for sub in range(NSUB):
            for c in range(2):
                self.tr(psb[:, (c * 4 + sub) * 128:(c * 4 + sub + 1) * 128], self.pbf[:, sub, c * 128:(c + 1) * 128], [self.Bpbf], [Bp])
        self.cp("act", self.ppT[:].rearrange("p c t -> p (c t)"), psb[:, 0:1024], [Bp], [self.BppT])

    def ple(self, t, nxt):
        self.norm_stage(3, stats_done=True)
        if nxt:
            for sub in range(NSUB):
                self.norm_stats(sub, tmp=True)
        wp = None
        for half in range(2):
            w, Bw = self.w_next(G_PG0 + half, prev_live=(half == 1))
            wv = w[:].rearrange("p (k n) -> p k n", k=KC)
            if half == 0:
                wp, Bwp = self.w_next(G_PP, prev_live=True)
                wpv = wp[:, 0:2048].rearrange("p (k n) -> p k n", k=2)
            for sub in range(NSUB):
                psg, Bpg = self.ps_next()
                for kc in range(KC):
                    self.mm(psg[:], self.hT[:, kc, sub * 128:(sub + 1) * 128], wv[:, kc, :], kc == 0, kc == KC - 1, [Bw, self.BhT[sub]], [Bpg])
                psp, Bpp = self.ps_next()
                for kc in range(2):
                    self.mm(psp[:], self.ppT[:, kc, sub * 128:(sub + 1) * 128], wpv[:, kc, half * 512:(half + 1) * 512], kc == 0, kc == 1,
                            [Bwp, self.BppT], [Bpp])
                k = self.rr("tgp", 2)
                self.act(self.tgp[:, k, :], psg[:], AF.Tanh, [Bpg], [self.Btgp[k]], scale=0.5)
                self.stt(self.ptmp[:, k, :], self.tgp[:, k, :], 1.0, psp[:], ALU.add, ALU.mult, [self.Btgp[k], Bpp], [self.Bptmp[k]])
                xs = self.xs[:, sub, half * 512:(half + 1) * 512]
                self.tt("pool", xs, xs, self.ptmp[:, k, :], ALU.add, [self.Bx[sub], self.Bptmp[k]], [self.Bx[sub]])

    def flush_out(self, n=None):
        while self.pending_out and (n is None or n > 0):
            o, src, Bs = self.pending_out.pop(0)
            self.dma(o, src, Bs, [self.Bout])
            if n is not None:
                n -= 1

    def final(self, t):
        for sub in range(NSUB):
            self.norm_stats(sub)
        for sub in range(NSUB):
            ko = self.rr("ot", 4)
            self.stt(self.ot[ko], self.xs[:, sub, :], self.nsc[:, sub, 2:3], self.gfin[:], ALU.mult, ALU.mult,
                     [self.Bx[sub], self.Bnsc[sub], self.Bconst], self.Bot[ko])
            r0 = t * MT + sub * 128
            self.pending_out.append((self.out_d[r0:r0 + 128, :], self.ot[ko], self.Bot[ko]))

    def prefix_proj(self, t, last):
        self.mark('prefix')
        if last:
            self.proj_tm(G_SMALL, self.cons_small, ncols=264)
        else:
            self.proj_tm(G_SMALL, self.cons_gates_only, ncols=8, col0=256)
        self.proj_conv(G_KML, 4)
        if not last:
            self.load_x(self.x_pre, t + 1)
        else:
            self.load_x(self.x_main, 0)
        self.gates()
        self.proj_tm(G_V0, self.cons_v(0))
        self.proj_tm(G_V1, self.cons_v(1))
        if last:
            self.proj_conv(G_QML, 0)
            self.proj_fm(G_KSW, self.cons_ksw)

    def prefix_state(self, t, last):
        for pair in range(2):
            st = [self.mlstm_kt(sub) for sub in (2 * pair, 2 * pair + 1)]
            for sub, (kk, ps, Bp) in zip((2 * pair, 2 * pair + 1), st):
                dC, dn = self.mlstm_dc(sub, kk, ps, Bp)
                self.mlstm_update(sub, dC, dn)
        if last:
            self.swa_shift()

    def main_tile(self, t):
        if t == 0:
            self.special_s0()
        self.mark('proj')
        self.proj_conv(G_KML, 4)
        self.proj_tm(G_V0, self.cons_v(0))
        self.proj_tm(G_V1, self.cons_v(1))
        self.proj_tm(G_SMALL, self.cons_small, ncols=264)
        self.proj_conv(G_QML, 0)
        self.proj_fm(G_KSW, self.cons_ksw)
        self.gates()
        self.proj_fm(G_QS0, self.cons_qs(0))
        self.proj_fm(G_QS1, self.cons_qs(1))
        self.proj_tm(G_O0, self.cons_o(0))
        self.proj_tm(G_O1, self.cons_o(1))
        self.proj_fm(G_GA0, self.cons_g(0))
        self.proj_fm(G_GA1, self.cons_g(4))
        self.proj_fm(G_GB0, self.cons_g(8))
        self.proj_fm(G_GB1, self.cons_g(12))
        if STOP == "proj":
            return
        self.flush_out()
        if self.copy_x_pending:
            for sub in range(NSUB):
                self.dma(self.xs[:, sub, :], self.xtmp[:, sub, :], self.Bxtmp[sub], [self.Bx[sub]])
            self.copy_x_pending = False
        self.mark('attn')
        deferred = []
        for sub in range(NSUB):
            self.attention(sub, (t == 0 and sub == 0), deferred)
        for f in deferred:
            f()
        self.swa_shift()
        if STOP == "mix":
            return
        self.mark('merge')
        self.ple_p(t)
        self.merge()
        if STOP == "merge":
            return
        nxt = t + 1 < self.nmain
        if nxt:
            for sub in range(NSUB):
                r0 = (t + 1) * MT + sub * 128
                self.dma(self.xtmp[:, sub, :], self.x_main[r0:r0 + 128, :], [], self.Bxtmp[sub])
        self.mark('wout')
        self.tm_residual([G_WO0, G_WO1], self.hT, lambda sub: [self.BhT[sub]], KC, scale=0.5, after_sub=self.after_wout)
        if STOP == "wout":
            return
        self.mark('mlp')
        self.mlp()
        if STOP == "mlp":
            return
        self.mark('ple')
        self.ple(t, nxt)
        if STOP == "ple":
            return
        if nxt:
            for sub in range(NSUB):
                self.norm_apply(sub, 0, tmp=True)
        self.mark('final')
        self.final(t)
        self.copy_x_pending = nxt

    def _build(self):
        self.w_init()
        self.setup()
        if STOP == "setup":
            return
        nfirst = len(PRE_LAST_GROUPS)
        self.convert_weights(CONV_ORDER[:nfirst])
        if STOP == "cvt0":
            return
        self.cvt_queue = list(CONV_ORDER[nfirst:])
        self.load_x(self.x_pre, 0)
        self.norm_stage(0)
        for t in range(self.npre):
            last = t == self.npre - 1
            self.prefix_proj(t, last)
            self.norm_stage(0)
            self.prefix_state(t, last)
            self.cvt_one()
        while self.cvt_queue:
            self.cvt_one()
        if STOP == "cvt":
            return
        for t in range(self.nmain):
            self.main_tile(t)
        self.flush_out()


def _grp(w2d, nk=KC):
    ncols = w2d.shape[1]
    a = w2d.reshape(nk, 128, ncols).transpose(1, 0, 2).reshape(128, nk * ncols)
    if a.shape[1] < GSZ:
        a = np.concatenate([a, np.zeros((128, GSZ - a.shape[1]), np.float32)], axis=1)
    return a


def pack_weights(inp):
    w_in = np.asarray(inp["w_in"][0], np.float32)
    wg = np.zeros((NG, 128, GSZ), np.float32)
    wg[G_KML] = _grp(w_in[:, C_QK + 512:C_QK + 1024])
    wg[G_V0] = _grp(w_in[:, C_V:C_V + 512])
    wg[G_V1] = _grp(w_in[:, C_V + 512:C_V + 1024])
    small = np.zeros((D, 512), np.float32)
    small[:, 0:256] = w_in[:, C_VS:C_VS + 256]
    small[:, 256:264] = w_in[:, C_IF:C_IF + 8]
    wg[G_SMALL] = _grp(small)
    wg[G_QML] = _grp(w_in[:, C_QK:C_QK + 512])
    ksd = np.zeros((D, 512), np.float32)
    for j in range(4):
        kj = w_in[:, C_KS + j * 64:C_KS + (j + 1) * 64]
        ksd[:, j * 128:j * 128 + 64] = kj
        ksd[:, j * 128 + 64:(j + 1) * 128] = kj
    wg[G_KSW] = _grp(ksd)
    for i in range(2):
        wg[G_QS0 + i] = _grp(w_in[:, C_QS + i * 512:C_QS + (i + 1) * 512])
        wg[G_O0 + i] = _grp(w_in[:, C_O + i * 512:C_O + (i + 1) * 512])
        wg[G_GA0 + i] = _grp(w_in[:, C_GA + i * 512:C_GA + (i + 1) * 512])
        wg[G_GB0 + i] = _grp(w_in[:, C_GB + i * 512:C_GB + (i + 1) * 512])
        wg[G_WA0 + i] = _grp(np.asarray(inp["w_branch_a"][0])[:, i * 512:(i + 1) * 512])
        wg[G_WB0 + i] = _grp(np.asarray(inp["w_branch_b"][0])[:, i * 512:(i + 1) * 512])
        wg[G_WO0 + i] = _grp(np.asarray(inp["w_out"][0])[:, i * 512:(i + 1) * 512])
        wg[G_PG0 + i] = _grp(np.asarray(inp["w_ple_gate"][0])[:, i * 512:(i + 1) * 512])
    w_up = np.asarray(inp["w_up"][0])
    for g in range(8):
        wg[G_UP0 + g] = _grp(w_up[:, g * 512:(g + 1) * 512])
    w_dn = np.asarray(inp["w_down"][0])
    for half in range(2):
        for kg in range(4):
            wg[G_DN0 + half * 4 + kg] = _grp(w_dn[kg * 1024:(kg + 1) * 1024, half * 512:(half + 1) * 512])
    wg[G_PP] = _grp(np.asarray(inp["w_ple_proj"][0]), nk=2)
    return wg


def pack_common(inp):
    col = lambda v: np.ascontiguousarray(np.asarray(v, np.float32).reshape(KC, 128).T)
    gains = np.stack([col(inp["norm_mix_g"][0]), col(inp["mlstm_norm_g"][0]), col(inp["norm_mlp_g"][0]), col(inp["norm_ple_g"][0])], axis=1)
    convw = np.ascontiguousarray(np.asarray(inp["conv_qk"][0], np.float32).reshape(4, KC, 128).transpose(2, 1, 0))
    bif = np.ascontiguousarray(np.broadcast_to(np.asarray(inp["b_if"][0], np.float32)[None, :], (128, 8)))
    sinks = np.ascontiguousarray(np.broadcast_to(np.asarray(inp["sinks"][0], np.float32)[None, :], (128, 16)))
    gfin = np.ascontiguousarray(np.broadcast_to(np.asarray(inp["final_norm_g"], np.float32)[None, :], (128, D)))
    gmlb = np.ascontiguousarray(np.broadcast_to(np.asarray(inp["mlstm_norm_g"][0], np.float32)[None, :], (128, D)))
    ii = np.arange(128)
    mask = (ii[:, None] <= ii[None, :]).astype(np.float32)
    cmat = np.stack([np.eye(128, dtype=np.float32), mask, np.ones((128, 128), np.float32)], axis=1)
    return dict(wg=pack_weights(inp), gains=np.ascontiguousarray(gains), convw=convw, bif=bif, sinks=sinks, gfin=gfin, gmlb=gmlb,
                cmat=np.ascontiguousarray(cmat)), mask


def mpair_for(mask, first_half):
    mp = np.zeros((128, 2, 2, 128), np.float32)
    mp[:, 1, 0, :] = 1.0 - mask
    mp[:, 1, 1, :] = mask
    mp[:, 0, 1, :] = mask
    mp[:, 0, 0, :] = 0.0 if first_half else (1.0 - mask)
    return mp


_PROG_CACHE = {}


def get_prog(npre, nmain):
    key = (npre, nmain)
    if key not in _PROG_CACHE:
        _PROG_CACHE[key] = Prog(npre, nmain)
    return _PROG_CACHE[key]


def kernel(**inputs):
    x = np.asarray(inputs["x"], np.float32)
    p = np.asarray(inputs["p"], np.float32)[0]
    Bsz, S, _ = x.shape
    half = S // 2
    npre = nmain = half // MT
    common, mask = pack_common(inputs)
    prog = get_prog(npre, nmain)
    in_maps = []
    for c in range(8):
        b, h = c // 2, c % 2
        m = dict(common)
        m["x_pre"] = np.zeros((half, D), np.float32) if h == 0 else np.ascontiguousarray(x[b, 0:half])
        m["x_main"] = np.ascontiguousarray(x[b, h * half:(h + 1) * half])
        m["p_main"] = np.ascontiguousarray(p[b, h * half:(h + 1) * half])
        m["mpair"] = mpair_for(mask, h == 0)
        in_maps.append(m)
    res = run_bass_kernel_spmd(prog.nc, in_maps, core_ids=list(range(8)))
    out = np.empty((Bsz, S, D), np.float32)
    for c in range(8):
        b, h = c // 2, c % 2
        out[b, h * half:(h + 1) * half] = res.results[c]["out"]
    return out
```

```python
from contextlib import ExitStack
import numpy as np
import concourse.bass as bass
import concourse.mybir as mybir
from concourse.bass_utils import run_bass_kernel_spmd

F32 = mybir.dt.float32
BF16 = mybir.dt.bfloat16
AF = mybir.ActivationFunctionType
ALU = mybir.AluOpType

ENGS = ("pe", "act", "dve", "pool", "sp")
NDMASEM = 8


class Buf:
    __slots__ = ("name", "writers", "readers", "excl")

    def __init__(self, name, excl=False):
        self.name = name
        self.writers = {}
        self.readers = {}
        self.excl = excl


class Op:
    __slots__ = ("eng", "fn", "deps", "signal", "count", "idx", "is_dma", "dma_slot", "dma_val", "waits")

    def __init__(self, eng, fn, is_dma):
        self.eng = eng
        self.fn = fn
        self.deps = []
        self.signal = False
        self.count = 0
        self.is_dma = is_dma
        self.dma_slot = None
        self.dma_val = 0
        self.waits = []


class Sched:
    def __init__(self, nc, same_engine_sync=True):
        self.nc = nc
        self.ops = {e: [] for e in ENGS}
        self.all_ops = []
        self.same_engine_sync = same_engine_sync

    def _add(self, eng, fn, reads, writes, is_dma):
        op = Op(eng, fn, is_dma)
        op.idx = len(self.all_ops)
        deps = {}
        for b in reads:
            for w in b.writers.values():
                deps[id(w)] = w
            if b.excl:
                for r in b.readers.values():
                    if r.eng != eng:
                        deps[id(r)] = r
        for b in writes:
            for w in b.writers.values():
                deps[id(w)] = w
            for r in b.readers.values():
                deps[id(r)] = r
        op.deps = list(deps.values())
        key = eng if not is_dma else ("dma", op.idx)
        for b in reads:
            b.readers[key] = op
        for b in writes:
            b.writers = {key: op}
            b.readers = {}
        self.ops[eng].append(op)
        self.all_ops.append(op)
        return op

    def op(self, eng, fn, reads=(), writes=()):
        return self._add(eng, fn, reads, writes, False)

    def dma(self, eng, fn, reads=(), writes=()):
        return self._add(eng, fn, reads, writes, True)

    def _skip_same(self, d, op):
        return (d.eng == op.eng and not op.is_dma and not d.is_dma
                and (d.eng in ("pe", "sp") or not self.same_engine_sync))

    def finalize(self):
        dma_i = {e: 0 for e in ENGS}
        for op in self.all_ops:
            if op.is_dma:
                i = dma_i[op.eng]
                dma_i[op.eng] += 1
                op.dma_slot = (op.eng, i % NDMASEM)
                op.dma_val = 16 * (i // NDMASEM + 1)
        for op in self.all_ops:
            for d in op.deps:
                if d.is_dma or self._skip_same(d, op):
                    continue
                d.signal = True
        cnt = {e: 0 for e in ENGS}
        for op in self.all_ops:
            if op.signal and not op.is_dma:
                cnt[op.eng] += 1
                op.count = cnt[op.eng]
        waited = {e: {} for e in ENGS}
        prev_dma_on_slot = {}
        for op in self.all_ops:
            w = waited[op.eng]
            need = {}
            for d in op.deps:
                if d.is_dma:
                    key = ("dma",) + d.dma_slot
                    val = d.dma_val
                else:
                    if self._skip_same(d, op):
                        continue
                    key = ("eng", d.eng)
                    val = d.count
                if w.get(key, 0) >= val:
                    continue
                if need.get(key, 0) < val:
                    need[key] = val
            if op.is_dma:
                p = prev_dma_on_slot.get(op.dma_slot)
                if p is not None:
                    key = ("dma",) + p.dma_slot
                    if w.get(key, 0) < p.dma_val and need.get(key, 0) < p.dma_val:
                        need[key] = p.dma_val
                prev_dma_on_slot[op.dma_slot] = op
            for key, val in need.items():
                w[key] = val
            op.waits = list(need.items())
        self.final_dma = dict(prev_dma_on_slot)

    def emit(self):
        nc = self.nc
        self.finalize()
        with ExitStack() as es:
            esem = {e: es.enter_context(nc.semaphore("s_" + e)) for e in ENGS}
            dsem = {}
            for e in ENGS:
                if any(o.is_dma for o in self.ops[e]):
                    for k in range(NDMASEM):
                        dsem[(e, k)] = es.enter_context(nc.semaphore("d_%s%d" % (e, k)))
            block = es.enter_context(nc.Block())

            def run(eng_name, eng):
                for op in self.ops[eng_name]:
                    for key, val in op.waits:
                        if key[0] == "eng":
                            eng.wait_ge(esem[key[1]], val)
                        else:
                            eng.wait_ge(dsem[(key[1], key[2])], val)
                    ins = op.fn(eng)
                    if op.is_dma:
                        ins.then_inc(dsem[op.dma_slot], 16)
                    elif op.signal:
                        ins.then_inc(esem[eng_name], 1)
                if eng_name == "sp":
                    for slot, p in self.final_dma.items():
                        eng.wait_ge(dsem[slot], p.dma_val)

            @block.tensor
            def _(e):
                run("pe", e)

            @block.scalar
            def _(e):
                run("act", e)

            @block.vector
            def _(e):
                run("dve", e)

            @block.gpsimd
            def _(e):
                run("pool", e)

            @block.sync
            def _(e):
                run("sp", e)


D = 1024
KC = 8
MT = 512
NSUB = 4
EPS = 1e-6
DKS = 128 ** -0.5
NG = 39
GSZ = 4096
NWB = 4
STOP = None
DBG = 9

C_QK, C_V, C_O, C_IF, C_QS, C_KS, C_VS, C_GA, C_GB = 0, 1024, 2048, 3072, 3080, 4104, 4360, 4616, 5640

G_KML, G_V0, G_V1, G_SMALL, G_QML, G_KSW, G_QS0, G_QS1, G_O0, G_O1 = range(10)
G_GA0, G_GA1, G_GB0, G_GB1, G_WA0, G_WA1, G_WB0, G_WB1, G_WO0, G_WO1 = range(10, 20)
G_UP0 = 20
G_DN0 = 28
G_PG0, G_PG1, G_PP = 36, 37, 38

GN_MIX, GN_MIX8, GN_ML05, GN_ONE, GN_HALF, GN_MLP, GN_PLE = range(7)


def group_gain(g):
    if g in (G_QS0, G_QS1):
        return GN_MIX8
    if g <= G_GB1:
        return GN_MIX
    if g in (G_WA0, G_WA1):
        return GN_ML05
    if g in (G_WB0, G_WB1):
        return GN_ONE
    if g in (G_WO0, G_WO1):
        return GN_HALF
    if G_UP0 <= g < G_DN0:
        return GN_MLP
    if G_DN0 <= g < G_PG0:
        return GN_ONE
    if g in (G_PG0, G_PG1):
        return GN_PLE
    return GN_HALF


PRE_GROUPS = [G_SMALL, G_KML, G_V0, G_V1]
PRE_LAST_GROUPS = [G_SMALL, G_KML, G_V0, G_V1, G_QML, G_KSW]
MAIN_GROUPS = ([G_KML, G_V0, G_V1, G_SMALL, G_QML, G_KSW, G_QS0, G_QS1, G_O0, G_O1, G_GA0, G_GA1, G_GB0, G_GB1,
                G_WA0, G_WB0, G_WA1, G_WB1, G_WO0, G_WO1] + list(range(G_UP0, G_UP0 + 8))
               + list(range(G_DN0, G_DN0 + 8)) + [G_PG0, G_PP, G_PG1])
CONV_ORDER = PRE_LAST_GROUPS + [g for g in MAIN_GROUPS if g not in PRE_LAST_GROUPS]


class Prog:
    def __init__(self, npre, nmain, same_engine_sync=True):
        self.npre, self.nmain = npre, nmain
        nc = bass.Bass("TRN2", target_bir_lowering=False)
        self.nc = nc
        self.S = Sched(nc, same_engine_sync)
        self.es = ExitStack()
        self.ring_i = 0
        self.cnt = {}
        self.cvt_queue = []
        self.ps_reserved = set()
        self.pending_out = []
        self.copy_x_pending = False
        self.marks = []
        self._alloc()
        self._build()
        self.S.emit()
        self.es.close()

    def sb(self, name, shape, dt):
        return self.es.enter_context(self.nc.sbuf_tensor("sb_" + name, shape, dt))

    def rr(self, name, n):
        i = self.cnt.get(name, 0)
        self.cnt[name] = i + 1
        return i % n

    def mark(self, name):
        self.marks.append((name, len(self.S.ops['pe'])))

    def ps_next(self):
        while True:
            i = self.ring_i % 8
            self.ring_i += 1
            if i not in self.ps_reserved:
                return self.ps[i], self.Bps[i]

    def ps_reserve(self):
        ps, Bp = self.ps_next()
        self.ps_reserved.add(self.Bps.index(Bp))
        return ps, Bp

    def ps_release(self, Bp):
        self.ps_reserved.discard(self.Bps.index(Bp))

    def op(self, eng, fn, reads=(), writes=()):
        return self.S.op(eng, fn, reads, writes)

    def act(self, out, in_, func, reads, writes, **kw):
        self.S.op("act", lambda e: e.activation(out=out, in_=in_, func=func, **kw), reads, writes)

    def tsc(self, eng, out, in0, s1, s2, op0, op1, reads, writes):
        if op1 is None:
            self.S.op(eng, lambda e: e.tensor_scalar(out=out, in0=in0, scalar1=s1, scalar2=None, op0=op0), reads, writes)
        else:
            self.S.op(eng, lambda e: e.tensor_scalar(out=out, in0=in0, scalar1=s1, scalar2=s2, op0=op0, op1=op1), reads, writes)

    def stt(self, out, in0, scalar, in1, op0, op1, reads, writes):
        self.S.op("dve", lambda e: e.scalar_tensor_tensor(out=out, in0=in0, scalar=scalar, in1=in1, op0=op0, op1=op1), reads, writes)

    def tt(self, eng, out, in0, in1, op, reads, writes):
        self.S.op(eng, lambda e: e.tensor_tensor(out=out, in0=in0, in1=in1, op=op), reads, writes)

    def cp(self, eng, out, in_, reads, writes):
        if eng == "act":
            self.act(out, in_, AF.Copy, reads, writes)
        else:
            self.S.op(eng, lambda e: e.tensor_copy(out=out, in_=in_), reads, writes)

    def mm(self, out, lhsT, rhs, start, stop, reads, writes):
        self.S.op("pe", lambda e: e.matmul(out, lhsT=lhsT, rhs=rhs, start=start, stop=stop), reads, writes)

    def tr(self, out, in_, reads, writes):
        idb = self.identb
        self.S.op("pe", lambda e: e.transpose(out=out, in_=in_, identity=idb[:]), list(reads) + [self.Bconst], writes)

    def dma(self, out, in_, reads, writes, eng="sp"):
        self.S.dma(eng, lambda e: e.dma_start(out=out, in_=in_), reads, writes)

    def _alloc(self):
        nc = self.nc
        npre, nmain = self.npre, self.nmain
        dt_in = lambda name, shape: nc.dram_tensor(name, shape, F32, kind="ExternalInput").ap()
        self.x_pre = dt_in("x_pre", [npre * MT, D])
        self.x_main = dt_in("x_main", [nmain * MT, D])
        self.p_main = dt_in("p_main", [nmain * MT, 256])
        self.wg = dt_in("wg", [NG, 128, GSZ])
        self.gains_d = dt_in("gains", [128, 4, 8])
        self.convw_d = dt_in("convw", [128, 8, 4])
        self.bif_d = dt_in("bif", [128, 8])
        self.sinks_d = dt_in("sinks", [128, 16])
        self.gfin_d = dt_in("gfin", [128, D])
        self.gmlb_d = dt_in("gmlb", [128, D])
        self.cmat_d = dt_in("cmat", [128, 3, 128])
        self.mpair_d = dt_in("mpair", [128, 2, 2, 128])
        self.out_d = nc.dram_tensor("out", [nmain * MT, D], F32, kind="ExternalOutput").ap()
        self.scr = nc.dram_tensor("wscr", [NG, 128, GSZ], BF16, kind="Internal").ap()
        self.Bscr = [Buf("scr%d" % g) for g in range(NG)]

        sb = self.sb
        B = lambda n: Buf(n)
        self.cmat = sb("cmat", [128, 3, 128], F32)
        self.identb = sb("identb", [128, 128], BF16)
        self.maskb = sb("maskb", [128, 128], BF16)
        self.mpair = sb("mpair", [128, 2, 2, 128], BF16)
        self.gains = sb("gains", [128, 4, 8], F32)
        self.convw = sb("convw", [128, 8, 4], F32)
        self.bif = sb("bif", [128, 8], F32)
        self.sinks = sb("sinks", [128, 16], F32)
        self.esink = sb("esink", [128, 16], F32)
        self.gfin = sb("gfin", [128, D], F32)
        self.mhalf = sb("mhalf", [128, 16], F32)
        self.onecol = sb("onecol", [128, 1], BF16)
        self.Bconst = B("const")
        self.xs = sb("xs", [128, NSUB, D], F32)
        self.Bx = [B("x%d" % i) for i in range(NSUB)]
        self.hT = sb("hT", [128, KC, MT], BF16)
        self.BhT = [B("hT%d" % i) for i in range(NSUB)]
        self.hn = sb("hn", [128, 2, D], BF16)
        self.Bhn = [B("hn0"), B("hn1")]
        self.junk = sb("junk", [128, 2, 256], BF16)
        self.Bjunk = [B("junk0"), B("junk1")]
        self.nsc = sb("nsc", [128, 2 * NSUB, 4], F32)
        self.Bnsc = [B("nsc%d" % i) for i in range(2 * NSUB)]
        self.gmlb = sb("gmlb", [128, D], F32)
        self.raw = sb("raw", [128, 4, MT + 4], BF16)
        self.Braw = [B("raw%d" % i) for i in range(4)]
        self.dg = sb("dg", [128, 8, 4, 128], BF16)
        self.halo = sb("halo", [128, 8, 3], BF16)
        self.Bhalo = [B("halo%d" % i) for i in range(8)]
        self.acc = sb("acc", [128, 2, MT], F32)
        self.Bacc = [B("acc0"), B("acc1")]
        self.th = sb("th", [128, 2, MT], F32)
        self.Bth = [B("th0"), B("th1")]
        self.qkT = sb("qkT", [128, 8, MT], BF16)
        self.Bqk = [B("qk%d" % i) for i in range(8)]
        self.vml = sb("vml", [128, NSUB, D], BF16)
        self.Bv = [B("v%d" % i) for i in range(NSUB)]
        self.pld = self.vml[:, 0:2, :].bitcast(F32).rearrange("p a b -> p (a b)").rearrange("p (s f) -> p s f", s=NSUB)
        self.Bpld = [self.Bv[0], self.Bv[1]]
        self.to1 = sb("to1", [128, NSUB, D], BF16)
        self.Bto = [B("to%d" % i) for i in range(NSUB)]
        self.gsb = sb("gsb", [128, NSUB, 8], F32)
        self.Bgsb = B("gsb")
        self.gsc = sb("gsc", [128, 16, 16], F32)
        self.Bgsc = B("gsc")
        self.qsT = sb("qsT", [128, 8, MT], BF16)
        self.Bqs = [B("qs%d" % i) for i in range(8)]
        self.ksT = sb("ksT", [128, 4, MT + 128], BF16)
        self.Bks = [B("ks%d" % i) for i in range(4)]
        self.vs = sb("vs", [128, 5, 4, 65], BF16)
        self.Bvs = [B("vs%d" % i) for i in range(5)]
        self.tg = sb("tgab", [128, 16, MT], BF16)
        self.Btg = [B("tg%d" % i) for i in range(16)]
        self.yT = sb("yT", [128, 16, MT], BF16)
        self.ByT = [[B("yaT%d" % i) for i in range(NSUB)], [B("ybT%d" % i) for i in range(NSUB)]]
        self.xtmp = self.yT[:].bitcast(F32).rearrange("p a b -> p (a b)").rearrange("p (s f) -> p s f", s=NSUB)
        self.Bxtmp = [self.ByT[0], self.ByT[0], self.ByT[1], self.ByT[1]]
        self.ya = sb("ya", [128, 2, D], BF16)
        self.Bya = [B("ya0"), B("ya1")]
        self.yb = sb("yb", [128, 2, D], BF16)
        self.Byb = [B("yb0"), B("yb1")]
        self.pT = sb("pT", [128, 2, 4, 128], BF16)
        self.BpT = [B("pT0"), B("pT1")]
        self.kw = sb("kw", [128, 2, 4, 128], BF16)
        self.Bkw = [B("kw0"), B("kw1")]
        self.E = sb("E", [128, 4, 2, MT], BF16)
        self.BE = [[B("E%d0" % i), B("E%d1" % i)] for i in range(4)]
        self.tmpA = sb("tmpA", [128, 2, MT], F32)
        self.BtA = [B("tA0"), B("tA1")]
        self.tmpB = sb("tmpB", [128, 2, MT], F32)
        self.BtB = [B("tB0"), B("tB1")]
        self.ot = [self.tmpA[:].rearrange("p a b -> p (a b)"), self.tmpB[:].rearrange("p a b -> p (a b)")]
        self.mpair_f = self.ot[0][:, 0:512].rearrange("p (a b c) -> p a b c", a=2, b=2)
        self.Bot = [self.BtA, self.BtB]
        self.rl = sb("rl", [128, 2, MT], F32)
        self.Brl = [B("rl0"), B("rl1")]

        self.C = sb("C", [128, 4, 256], F32)
        self.BC = [B("C%d" % i) for i in range(4)]
        self.nst = sb("nst", [128, 8], F32)
        self.Bn = B("n")
        self.Cin = sb("Cin", [128, 4, 256], BF16)
        self.BCin = [B("Cin%d" % i) for i in range(4)]
        self.nin = sb("nin", [128, 4], BF16)
        self.Bnin = B("nin")
        self.msc = sb("msc", [128, 2, 8, 4], F32)
        self.Bmsc = [B("msc0"), B("msc1")]
        self.ssc = sb("ssc", [128, 2, 2, 4], F32)
        self.Bssc = [B("ssc0"), B("ssc1")]
        self.pbf = sb("pbf", [128, NSUB, 256], BF16)
        self.Bpbf = B("pbf")
        self.ppT = sb("ppT", [128, 2, MT], BF16)
        self.BppT = B("ppT")
        self.tgp, self.Btgp = self.acc, self.Bacc
        self.ot += [self.acc[:].rearrange("p a b -> p (a b)"), self.rl[:].rearrange("p a b -> p (a b)")]
        self.Bot += [self.Bacc, self.Brl]
        self.ptmp, self.Bptmp = self.th, self.Bth
        self.wb = [sb("wb%d" % i, [128, GSZ], BF16) for i in range(NWB)]
        self.Bwb = [B("wb%d" % i) for i in range(NWB)]
        self.Bout = B("outd")
        self.ps = [self.es.enter_context(nc.psum_tensor("ps%d" % i, [128, 512], F32)) for i in range(8)]
        self.Bps = [Buf("ps%d" % i, excl=True) for i in range(8)]

    def w_init(self):
        seq = []
        for t in range(self.npre):
            seq += PRE_LAST_GROUPS if t == self.npre - 1 else PRE_GROUPS
        for t in range(self.nmain):
            seq += MAIN_GROUPS
        self.wseq = seq
        self.w_i = 0
        self.w_issued = 0

    def w_next(self, g, prev_live=False):
        assert self.wseq[self.w_i] == g, (self.w_i, self.wseq[self.w_i], g)
        while self.w_issued < min(len(self.wseq), self.w_i + NWB - (1 if prev_live else 0)):
            k = self.w_issued
            gg = self.wseq[k]
            b = k % NWB
            self.dma(self.wb[b][:], self.scr[gg], [self.Bscr[gg]], [self.Bwb[b]])
            self.w_issued += 1
        b = self.w_i % NWB
        self.w_i += 1
        return self.wb[b], self.Bwb[b]

    def setup(self):
        n0 = len(self.S.all_ops)
        Bcm, Bgn, Bcw, Bbf, Bsk, Bgf, Bgm, Bes = (Buf(n) for n in ("c_cm", "c_gn", "c_cw", "c_bf", "c_sk", "c_gf", "c_gm", "c_es"))
        self.dma(self.cmat[:], self.cmat_d, [], [Bcm])
        self.dma(self.convw[:], self.convw_d, [], [Bcw])
        self.dma(self.mpair_f, self.mpair_d, [], self.BtA)
        self.dma(self.gains[:], self.gains_d, [], [Bgn])
        self.dma(self.bif[:], self.bif_d, [], [Bbf])
        self.dma(self.sinks[:], self.sinks_d, [], [Bsk])
        self.dma(self.gfin[:], self.gfin_d, [], [Bgf])
        self.dma(self.gmlb[:], self.gmlb_d, [], [Bgm])
        cm = self.cmat
        self.cp("dve", self.identb[:], cm[:, 0, :], [Bcm], [Buf("c_id")])
        self.cp("dve", self.maskb[:], cm[:, 1, :], [Bcm], [Buf("c_mk")])
        self.op("dve", lambda e: e.memset(self.mhalf[:], -0.5), [], [Buf("c_mh")])
        self.op("dve", lambda e: e.memset(self.onecol[:], 1.0), [], [Buf("c_oc")])
        self.tsc("dve", self.convw[:], self.convw[:], 0.5, None, ALU.mult, None, [Bcw], [Bcw])
        for rc in range(8):
            for j in range(4):
                self.tsc("dve", self.dg[:, rc, j, :], cm[:, 0, :], self.convw[:, rc, j:j + 1], None, ALU.mult, None, [Bcm, Bcw], [Buf("c_dg")])
        self.cp("dve", self.mpair[:], self.mpair_f, self.BtA, [Buf("c_mp")])
        self.tsc("dve", self.gmlb[:], self.gmlb[:], 0.5, None, ALU.mult, None, [Bgm], [Bgm])
        self.act(self.esink[:], self.sinks[:], AF.Exp, [Bsk], [Bes])
        self.Bconst.writers = {("setup", i): o for i, o in enumerate(self.S.all_ops[n0:])}
        for h in range(4):
            self.op("pool", lambda e, h=h: e.memset(self.C[:, h, :], 0.0), [], [self.BC[h]])
        self.op("pool", lambda e: e.memset(self.nst[:], 0.0), [], [self.Bn])
        for c in range(8):
            self.op("pool", lambda e, c=c: e.memset(self.halo[:, c, :], 0.0), [], [self.Bhalo[c]])
        for b in range(5):
            self.op("pool", lambda e, b=b: e.memset(self.vs[:, b, :, 64:65], 1.0), [], [self.Bvs[b]])

    def convert_weights(self, glist):
        for g in glist:
            self.dma(self.scr[g], self.wg[g], [], [self.Bscr[g]], eng="pool")

    def cvt_one(self):
        if self.cvt_queue:
            self.convert_weights([self.cvt_queue.pop(0)])

    def xsrc(self, sub, tmp=False):
        if tmp:
            return self.xtmp[:, sub, :], self.Bxtmp[sub]
        return self.xs[:, sub, :], [self.Bx[sub]]

    def norm_stats(self, sub, tmp=False):
        x, Bxl = self.xsrc(sub, tmp)
        sl = sub + (NSUB if tmp else 0)
        ns, Bn = self.nsc, self.Bnsc[sl]
        self.act(self.hn[:, sub % 2, :], x, AF.Square, Bxl, [Bn, self.Bhn[sub % 2]], accum_out=ns[:, sl, 0:1])
        self.tsc("dve", ns[:, sl, 1:2], ns[:, sl, 0:1], 1.0 / D, EPS, ALU.mult, ALU.add, [Bn], [Bn])
        self.tt("pool", ns[:, sl, 2:3], ns[:, sl, 1:2], self.mhalf[:, 0:1], ALU.pow, [Bn, self.Bconst], [Bn])
        self.cvt_one()

    def norm_apply(self, sub, gi, tmp=False):
        x, Bxl = self.xsrc(sub, tmp)
        sl = sub + (NSUB if tmp else 0)
        ns, Bn = self.nsc, self.Bnsc[sl]
        k = self.rr("hn", 2)
        self.act(self.hn[:, k, :], x, AF.Copy, Bxl + [Bn], [self.Bhn[k]], scale=ns[:, sl, 2:3])
        ps, Bp = self.ps_next()
        psb = ps[:].bitcast(BF16)
        for kc in range(KC):
            self.tr(psb[:, kc * 128:(kc + 1) * 128], self.hn[:, k, kc * 128:(kc + 1) * 128], [self.Bhn[k]], [Bp])
        self.tt("dve", self.hT[:, :, sub * 128:(sub + 1) * 128], psb.rearrange("p (k t) -> p k t", k=KC),
                self.gains[:, gi, :].unsqueeze(2).to_broadcast([128, KC, 128]), ALU.mult, [Bp, self.Bconst], [self.BhT[sub]])

    def norm_stage(self, gi, stats_done=False, tmp=False):
        if not stats_done:
            for sub in range(NSUB):
                self.norm_stats(sub, tmp)
        for sub in range(NSUB):
            self.norm_apply(sub, gi, tmp)

    def special_s0(self):
        F = F32
        tgf = self.tg[:].bitcast(F).rearrange("p a b -> p (a b)")
        ytf = self.yT[:].bitcast(F).rearrange("p a b -> p (a b)")
        hn32 = tgf[:, 0:1024]
        hT32 = tgf[:, 1024:2048].rearrange("p (k t) -> p k t", k=KC)
        w32 = [tgf[:, 2048:3072].rearrange("p (k n) -> p k n", k=KC), tgf[:, 3072:4096].rearrange("p (k n) -> p k n", k=KC)]
        qk32 = ytf[:, 0:1024].rearrange("p (c t) -> p c t", c=8)
        raw32 = [ytf[:, 1024:1024 + 131], ytf[:, 1280:1280 + 131]]
        acc32 = [ytf[:, 1536:1664], ytf[:, 1664:1792]]
        th32 = [ytf[:, 1792:1920], ytf[:, 1920:2048]]
        self.S32 = ytf[:, 2048:2560]
        Rt, Ry = list(self.Btg), self.ByT[0] + self.ByT[1]
        Bh, BhT, Bw, Bqk = Buf("hn32"), Buf("hT32"), [Buf("w32a"), Buf("w32b")], [Buf("qk32_%d" % i) for i in range(8)]
        Braw, Bacc, Bth, BS = [Buf("raw32a"), Buf("raw32b")], [Buf("acc32a"), Buf("acc32b")], [Buf("th32a"), Buf("th32b")], Buf("S32")
        self.BS32 = BS
        ns = self.nsc
        self.act(hn32, self.xs[:, 0, :], AF.Copy, [self.Bx[0], self.Bnsc[0]] + Rt, [Bh], scale=ns[:, 0, 2:3])
        idf = self.cmat[:, 0, :]
        for half in range(2):
            ps, Bp = self.ps_next()
            for q in range(4):
                kc = half * 4 + q
                self.S.op("pe", lambda e, o=ps[:, q * 128:(q + 1) * 128], i=hn32[:, kc * 128:(kc + 1) * 128]: e.transpose(out=o, in_=i, identity=idf),
                          [Bh, self.Bconst] + Rt, [Bp])
            self.tt("dve", hT32[:, half * 4:(half + 1) * 4, :], ps[:].rearrange("p (k t) -> p k t", k=4),
                    self.gains[:, 0, half * 4:(half + 1) * 4].unsqueeze(2).to_broadcast([128, 4, 128]), ALU.mult, [Bp, self.Bconst] + Rt, [BhT])
        cw = self.convw
        for rc in range(8):
            g, c = (G_QML, rc) if rc < 4 else (G_KML, rc - 4)
            k = rc % 2
            src = self.wg[g].rearrange("p (k n) -> p k n", k=KC)[:, :, c * 128:(c + 1) * 128]
            self.dma(w32[k], src, Rt, [Bw[k]])
            ps, Bp = self.ps_next()
            for kc in range(KC):
                self.mm(ps[:, 0:128], w32[k][:, kc, :], hT32[:, kc, :], kc == 0, kc == KC - 1, [Bw[k], BhT] + Rt, [Bp])
            self.cp("dve", raw32[k][:, 0:3], self.halo[:, rc, :], [self.Bhalo[rc]] + Ry, [Braw[k]])
            self.cp("act", raw32[k][:, 3:131], ps[:, 0:128], [Bp] + Ry, [Braw[k]])
            self.tsc("dve", acc32[k], raw32[k][:, 3:131], cw[:, rc, 3:4], None, ALU.mult, None, [Braw[k], self.Bconst] + Ry, [Bacc[k]])
            for j in (2, 1, 0):
                self.stt(acc32[k], raw32[k][:, j:j + 128], cw[:, rc, j:j + 1], acc32[k], ALU.mult, ALU.add, [Braw[k], Bacc[k], self.Bconst] + Ry, [Bacc[k]])
            self.act(th32[k], acc32[k], AF.Tanh, [Bacc[k]] + Ry, [Bth[k]])
            self.stt(qk32[:, rc, :], th32[k], 1.0, acc32[k], ALU.add, ALU.mult, [Bth[k], Bacc[k]] + Ry, [Bqk[rc]])
        ps, Bp = self.ps_next()
        for h in range(4):
            self.mm(ps[:, h * 128:(h + 1) * 128], qk32[:, 4 + h, :], qk32[:, h, :], True, True, [Bqk[4 + h], Bqk[h]] + Ry, [Bp])
        self.cp("act", self.S32, ps[:], [Bp] + Ry, [BS])

    def load_x(self, src, t):
        for sub in range(NSUB):
            r0 = t * MT + sub * 128
            self.dma(self.xs[:, sub, :], src[r0:r0 + 128, :], [], [self.Bx[sub]])

    def proj_fm(self, g, consumer):
        w, Bw = self.w_next(g)
        wv = w[:].rearrange("p (k n) -> p k n", k=KC)
        for c in range(4):
            ps, Bp = self.ps_next()
            for kc in range(KC):
                self.mm(ps[:], wv[:, kc, c * 128:(c + 1) * 128], self.hT[:, kc, :], kc == 0, kc == KC - 1,
                        [Bw] + self.BhT, [Bp])
            consumer(c, ps, Bp)

    def proj_tm(self, g, consumer, ncols=512, col0=0):
        w, Bw = self.w_next(g)
        wv = w[:].rearrange("p (k n) -> p k n", k=KC)
        for sub in range(NSUB):
            ps, Bp = self.ps_next()
            for kc in range(KC):
                self.mm(ps[:, 0:ncols], self.hT[:, kc, sub * 128:(sub + 1) * 128], wv[:, kc, col0:col0 + ncols], kc == 0, kc == KC - 1,
                        [Bw, self.BhT[sub]], [Bp])
            consumer(sub, ps, Bp)

    def conv_a(self, rc, ps, Bp):
        k = self.rr("raw", 4)
        raw, Br = self.raw[:, k, :], self.Braw[k]
        self.cp("pool", raw[:, 0:3], self.halo[:, rc, :], [self.Bhalo[rc]], [Br])
        self.cp("act", raw[:, 3:MT + 3], ps[:], [Bp], [Br])
        self.cp("pool", self.halo[:, rc, :], raw[:, MT:MT + 3], [Br], [self.Bhalo[rc]])
        return k

    def conv_b(self, rc, k):
        raw, Br = self.raw[:, k, :], self.Braw[k]
        ps, Bp = self.ps_next()
        for j in range(4):
            self.mm(ps[:], self.dg[:, rc, j, :], raw[:, j:j + MT], j == 0, j == 3, [Br, self.Bconst], [Bp])
        kt = self.rr("th", 2)
        th, Bt = self.th[:, kt, :], self.Bth[kt]
        self.act(th, ps[:], AF.Tanh, [Bp], [Bt])
        self.stt(self.qkT[:, rc, :], th, 1.0, ps[:], ALU.add, ALU.mult, [Bt, Bp], [self.Bqk[rc]])

    def proj_conv(self, g, base):
        ks = []
        self.proj_fm(g, lambda c, ps, Bp: ks.append(self.conv_a(base + c, ps, Bp)))
        for c in range(4):
            self.conv_b(base + c, ks[c])

    def cons_v(self, gi):
        def f(sub, ps, Bp):
            self.cp("act", self.vml[:, sub, gi * 512:(gi + 1) * 512], ps[:], [Bp], [self.Bv[sub]])
        return f

    def cons_small(self, sub, ps, Bp):
        self.cp("dve", self.vs[:, sub + 1, :, 0:64], ps[:, 0:256].rearrange("p (j d) -> p j d", j=4), [Bp], [self.Bvs[sub + 1]])
        self.cp("dve", self.gsb[:, sub, :], ps[:, 256:264], [Bp], [self.Bgsb])

    def cons_gates_only(self, sub, ps, Bp):
        self.cp("dve", self.gsb[:, sub, :], ps[:, 0:8], [Bp], [self.Bgsb])

    def cons_ksw(self, c, ps, Bp):
        self.cp("act", self.ksT[:, c, 128:MT + 128], ps[:], [Bp], [self.Bks[c]])

    def cons_qs(self, gi):
        def f(c, ps, Bp):
            cc = gi * 4 + c
            self.act(self.qsT[:, cc, :], ps[:], AF.Copy, [Bp], [self.Bqs[cc]], scale=0.125)
        return f

    def cons_o(self, gi):
        def f(sub, ps, Bp):
            o = self.to1[:, sub, gi * 512:(gi + 1) * 512]
            self.act(o, ps[:], AF.Tanh, [Bp], [self.Bto[sub]], scale=0.5)
            self.tsc("pool", o, o, 1.0, 1.0, ALU.add, ALU.mult, [self.Bto[sub]], [self.Bto[sub]])
            self.tt("pool", o, o, self.gmlb[:, gi * 512:(gi + 1) * 512], ALU.mult, [self.Bto[sub], self.Bconst], [self.Bto[sub]])
        return f

    def cons_g(self, base):
        def f(c, ps, Bp):
            cc = base + c
            self.act(self.tg[:, cc, :], ps[:], AF.Tanh, [Bp], [self.Btg[cc]], scale=0.5)
        return f

    def gates(self):
        G = self.gsc
        Bg = self.Bgsc
        gp = G[:, 0:2, :].rearrange("p a b -> p (a b)").rearrange("p (s e) -> p s e", s=4)
        self.tt("dve", gp, self.gsb[:], self.bif[:].unsqueeze(1).to_broadcast([128, 4, 8]), ALU.add,
                [self.Bgsb, self.Bconst], [Bg])
        tf = G[:, 2, :].rearrange("p (s h) -> p s h", s=4)
        self.act(tf, gp[:, :, 4:8], AF.Tanh, [Bg], [Bg], scale=0.5)
        lf = G[:, 3, :]
        self.act(lf, G[:, 2, :], AF.Ln, [Bg], [Bg], scale=0.5, bias=0.5)
        ps, Bp = self.ps_next()
        cm = self.cmat
        self.mm(ps[:, 0:16], cm[:, 1, :], lf, True, True, [Bg, self.Bconst], [Bp])
        self.mm(ps[:, 16:32], cm[:, 2, :], lf, True, True, [Bg, self.Bconst], [Bp])
        bt = G[:, 4:6, :].rearrange("p a b -> p (a b)")
        self.cp("dve", bt, ps[:, 0:32], [Bp], [Bg])
        b_, tot = G[:, 4, :], G[:, 5, :]
        bmb = G[:, 6, :]
        self.stt(bmb, tot, -0.5, b_, ALU.mult, ALU.add, [Bg], [Bg])
        ap_ = G[:, 7, :]
        self.tt("dve", ap_.rearrange("p (s h) -> p s h", s=4), gp[:, :, 0:4], bmb.rearrange("p (s h) -> p s h", s=4),
                ALU.subtract, [Bg], [Bg])
        e1, dlow, ebeta, e2b = G[:, 8, :], G[:, 9, :], G[:, 10, :], G[:, 11, :]
        self.act(e1, ap_, AF.Exp, [Bg], [Bg])
        self.act(dlow, bmb, AF.Exp, [Bg], [Bg], scale=-2.0)
        self.act(ebeta, tot, AF.Exp, [Bg], [Bg], scale=0.5)
        self.act(e2b, tot, AF.Exp, [Bg], [Bg])
        self.tsc("dve", G[:, 12, :], e1, DKS, None, ALU.mult, None, [Bg], [Bg])
        self.tt("dve", G[:, 13, :], e1, ebeta, ALU.mult, [Bg], [Bg])
        self.tsc("dve", G[:, 14, :], ebeta, DKS, None, ALU.mult, None, [Bg], [Bg])

    def mlstm_kt(self, sub):
        kk = self.rr("kw", 2)
        ps, Bp = self.ps_next()
        psb = ps[:].bitcast(BF16)
        for h in range(4):
            self.tr(psb[:, h * 128:(h + 1) * 128], self.qkT[:, 4 + h, sub * 128:(sub + 1) * 128], [self.Bqk[4 + h]], [Bp])
        return kk, ps, Bp

    def mlstm_dc(self, sub, kk, ps, Bp):
        G, Bg = self.gsc, self.Bgsc
        psb = ps[:].bitcast(BF16)
        wsc = G[:, 13, sub * 4:(sub + 1) * 4]
        self.tt("dve", self.kw[:, kk, :, :], psb[:, 0:512].rearrange("p (h d) -> p h d", h=4),
                wsc.unsqueeze(2).to_broadcast([128, 4, 128]), ALU.mult, [Bp, Bg], [self.Bkw[kk]])
        dC = []
        for hp in range(2):
            ps2, Bp2 = self.ps_next()
            for hh in range(2):
                h = hp * 2 + hh
                self.mm(ps2[:, hh * 256:(hh + 1) * 256], self.kw[:, kk, h, :], self.vml[:, sub, h * 256:(h + 1) * 256], True, True,
                        [self.Bkw[kk], self.Bv[sub]], [Bp2])
            dC.append((ps2, Bp2))
        psn, Bpn = self.ps_next()
        for h in range(4):
            self.mm(psn[:, h:h + 1], self.kw[:, kk, h, :], self.onecol[:], True, True, [self.Bkw[kk], self.Bconst], [Bpn])
        return dC, (psn, Bpn)

    def mlstm_update(self, sub, dC, dn):
        G, Bg = self.gsc, self.Bgsc
        for h in range(4):
            ps2, Bp2 = dC[h // 2]
            hh = h % 2
            idx = sub * 4 + h
            self.stt(self.C[:, h, :], self.C[:, h, :], G[:, 11, idx:idx + 1], ps2[:, hh * 256:(hh + 1) * 256], ALU.mult, ALU.add,
                     [self.BC[h], Bg, Bp2], [self.BC[h]])
        psn, Bpn = dn
        n, tmp = self.nst[:, 0:4], self.nst[:, 4:8]
        self.tt("dve", tmp, n, G[:, 11, sub * 4:(sub + 1) * 4], ALU.mult, [self.Bn, Bg], [self.Bn])
        self.tt("dve", n, tmp, psn[:, 0:4], ALU.add, [self.Bn, Bpn], [self.Bn])

    def mlstm_p1(self, sub, use_s32=False):
        G, Bg = self.gsc, self.Bgsc
        tsl = slice(sub * 128, (sub + 1) * 128)
        for h in range(4):
            idx = sub * 4 + h
            self.act(self.Cin[:, h, :], self.C[:, h, :], AF.Copy, [self.BC[h], Bg], [self.BCin[h]], scale=G[:, 14, idx:idx + 1])
        self.tt("dve", self.nin[:], self.nst[:, 0:4], G[:, 14, sub * 4:(sub + 1) * 4], ALU.mult, [self.Bn, Bg], [self.Bnin])
        if use_s32:
            Ssrc, BSl = self.S32, [self.BS32] + self.ByT[0] + self.ByT[1]
        else:
            psS, BpS = self.ps_next()
            for h in range(4):
                self.mm(psS[:, h * 128:(h + 1) * 128], self.qkT[:, 4 + h, tsl], self.qkT[:, h, tsl], True, True,
                        [self.Bqk[4 + h], self.Bqk[h]], [BpS])
            Ssrc, BSl = psS, [BpS]
        k = self.rr("pT", 2)
        for h in range(4):
            idx = sub * 4 + h
            self.stt(self.pT[:, k, h, :], Ssrc[:, h * 128:(h + 1) * 128], G[:, 12, idx:idx + 1], self.maskb[:], ALU.mult, ALU.mult,
                     BSl + [Bg, self.Bconst], [self.BpT[k]])
        kk = self.rr("kw", 2)
        ps, Bp = self.ps_next()
        psb = ps[:].bitcast(BF16)
        for h in range(4):
            self.tr(psb[:, h * 128:(h + 1) * 128], self.qkT[:, 4 + h, tsl], [self.Bqk[4 + h]], [Bp])
        wsc = G[:, 13, sub * 4:(sub + 1) * 4]
        self.tt("dve", self.kw[:, kk, :, :], psb[:, 0:512].rearrange("p (h d) -> p h d", h=4),
                wsc.unsqueeze(2).to_broadcast([128, 4, 128]), ALU.mult, [Bp, Bg], [self.Bkw[kk]])
        return dict(sub=sub, k=k, kk=kk)

    def mlstm_p2(self, cx):
        G, Bg = self.gsc, self.Bgsc
        sub, k, kk = cx["sub"], cx["k"], cx["kk"]
        tsl = slice(sub * 128, (sub + 1) * 128)
        nums = []
        for hp in range(2):
            ps2, Bp2 = self.ps_reserve()
            for hh in range(2):
                h = hp * 2 + hh
                o = ps2[:, hh * 256:(hh + 1) * 256]
                self.mm(o, self.pT[:, k, h, :], self.vml[:, sub, h * 256:(h + 1) * 256], True, False, [self.BpT[k], self.Bv[sub]], [Bp2])
                self.mm(o, self.qkT[:, h, tsl], self.Cin[:, h, :], False, True, [self.Bqk[h], self.BCin[h]], [Bp2])
            nums.append((ps2, Bp2))
        psd, Bpd = self.ps_next()
        for h in range(4):
            self.mm(psd[:, h:h + 1], self.pT[:, k, h, :], self.onecol[:], True, False, [self.BpT[k], self.Bconst], [Bpd])
            self.mm(psd[:, h:h + 1], self.qkT[:, h, tsl], self.nin[:, h:h + 1], False, True, [self.Bqk[h], self.Bnin], [Bpd])
        dC = []
        for hp in range(2):
            ps2, Bp2 = self.ps_next()
            for hh in range(2):
                h = hp * 2 + hh
                self.mm(ps2[:, hh * 256:(hh + 1) * 256], self.kw[:, kk, h, :], self.vml[:, sub, h * 256:(h + 1) * 256], True, True,
                        [self.Bkw[kk], self.Bv[sub]], [Bp2])
            dC.append((ps2, Bp2))
        psn, Bpn = self.ps_next()
        for h in range(4):
            self.mm(psn[:, h:h + 1], self.kw[:, kk, h, :], self.onecol[:], True, True, [self.Bkw[kk], self.Bconst], [Bpn])
        m = self.rr("msc", 2)
        M, Bm = self.msc, self.Bmsc[m]
        for h in range(4):
            ps2, Bp2 = nums[h // 2]
            hh = h % 2
            self.act(self.junk[:, m, :], ps2[:, hh * 256:(hh + 1) * 256], AF.Square, [Bp2], [self.Bjunk[m], Bm],
                     accum_out=M[:, m, 0, h:h + 1], scale=1.0 / 16)
        self.act(M[:, m, 1, :], psd[:, 0:4], AF.Square, [Bpd], [Bm])
        self.tt("dve", M[:, m, 2, :], M[:, m, 1, :], G[:, 9, sub * 4:(sub + 1) * 4], ALU.max, [Bm, Bg], [Bm])
        self.stt(M[:, m, 3, :], M[:, m, 2, :], EPS, M[:, m, 0, :], ALU.mult, ALU.add, [Bm], [Bm])
        self.tt("pool", M[:, m, 6, :], M[:, m, 3, :], self.mhalf[:, 0:4], ALU.pow, [Bm, self.Bconst], [Bm])
        self.mlstm_update(sub, dC, (psn, Bpn))
        cx["nums"], cx["m"] = nums, m
        return cx

    def mlstm_p3(self, cx):
        sub, nums, m = cx["sub"], cx["nums"], cx["m"]
        M, Bm = self.msc, self.Bmsc[m]
        ky = self.rr("ya", 2)
        for h in range(4):
            ps2, Bp2 = nums[h // 2]
            hh = h % 2
            self.stt(self.ya[:, ky, h * 256:(h + 1) * 256], ps2[:, hh * 256:(hh + 1) * 256], M[:, m, 6, h:h + 1],
                     self.to1[:, sub, h * 256:(h + 1) * 256], ALU.mult, ALU.mult, [Bp2, Bm, self.Bto[sub]], [self.Bya[ky]])
        for _, Bp2 in nums:
            self.ps_release(Bp2)
        cx["ky"] = ky

    def mlstm_p3b(self, cx):
        ky = cx["ky"]
        self.transpose_to(self.ya[:, ky, :], self.Bya[ky], 0, cx["sub"])

    def transpose_to(self, src, Bsrc, which, sub):
        ps, Bp = self.ps_next()
        psb = ps[:].bitcast(BF16)
        for kc in range(KC):
            self.tr(psb[:, kc * 128:(kc + 1) * 128], src[:, kc * 128:(kc + 1) * 128], [Bsrc], [Bp])
        self.cp("act", self.yT[:, which * 8:(which + 1) * 8, sub * 128:(sub + 1) * 128], psb.rearrange("p (k t) -> p k t", k=KC),
                [Bp], [self.ByT[which][sub]])

    def swa_p1(self, sub, j, first_block):
        mp = self.mpair[:, 0 if first_block else 1, :, :]
        mpb = mp.unsqueeze(2).to_broadcast([128, 2, 2, 128])
        ke = j
        banks = [self.ps_next(), self.ps_next()]
        for kb in range(2):
            ks = slice((sub + kb) * 128, (sub + kb + 1) * 128)
            for r in range(2):
                ps, Bp = banks[r]
                rs = slice(r * 64, (r + 1) * 64)
                self.mm(ps[:, kb * 256:(kb + 1) * 256].rearrange("p (c q) -> p c q", c=2),
                        self.ksT[rs, j, ks], self.qsT[rs, 2 * j:2 * j + 2, sub * 128:(sub + 1) * 128], True, True,
                        [self.Bks[j], self.Bqs[2 * j], self.Bqs[2 * j + 1]], [Bp])
        for r in range(2):
            ps, Bp = banks[r]
            Er = self.E[:, ke, r, :]
            self.act(Er, ps[:], AF.Exp, [Bp], [self.BE[ke][r]])
            Er4 = Er.rearrange("p (b c q) -> p b c q", b=2, c=2)
            self.tt("pool", Er4, Er4, mpb, ALU.mult, [self.BE[ke][r], self.Bconst], [self.BE[ke][r]])

    def swa_p2(self, sub, j, ky):
        ke = j
        pso, Bpo = self.ps_next()
        for g in range(4):
            c, r = g // 2, g % 2
            for kb in range(2):
                lhsT = self.E[:, ke, r, :].rearrange("p (b c q) -> p b c q", b=2, c=2)[:, kb, c, :]
                self.mm(pso[:, g * 65:(g + 1) * 65], lhsT, self.vs[:, sub + kb, j, :], kb == 0, kb == 1,
                        [self.BE[ke][r], self.Bvs[sub + kb]], [Bpo])
        s_ = self.rr("ssc", 2)
        SS, Bs = self.ssc, self.Bssc[s_]
        o3 = pso[:, 0:260].rearrange("p (g d) -> p g d", g=4)
        self.tt("dve", SS[:, s_, 0, :].unsqueeze(2), o3[:, :, 64:65], self.esink[:, 4 * j:4 * j + 4].unsqueeze(2), ALU.add,
                [Bpo, self.Bconst], [Bs])
        self.op("dve", lambda e, s_=s_: e.reciprocal(out=SS[:, s_, 1, :], in_=SS[:, s_, 0, :]), [Bs], [Bs])
        self.tt("dve", self.yb[:, ky, j * 256:(j + 1) * 256].rearrange("p (g d) -> p g d", g=4), o3[:, :, 0:64],
                SS[:, s_, 1, :].unsqueeze(2).to_broadcast([128, 4, 64]), ALU.mult, [Bpo, Bs], [self.Byb[ky]])

    def attention(self, sub, first_block, deferred):
        ky = self.rr("yb", 2)
        for f in deferred[:1]:
            f()
        cx = self.mlstm_p1(sub, use_s32=first_block)
        self.swa_p1(sub, 0, first_block)
        self.swa_p1(sub, 1, first_block)
        for f in deferred[1:]:
            f()
        del deferred[:]
        cx = self.mlstm_p2(cx)
        self.swa_p2(sub, 0, ky)
        self.swa_p2(sub, 1, ky)
        self.swa_p1(sub, 2, first_block)
        self.swa_p1(sub, 3, first_block)
        self.swa_p2(sub, 2, ky)
        self.swa_p2(sub, 3, ky)
        deferred.append(lambda: self.mlstm_p3(cx))
        deferred.append(lambda: self.mlstm_p3b(cx))
        deferred.append(lambda: self.transpose_to(self.yb[:, ky, :], self.Byb[ky], 1, sub))

    def swa_shift(self):
        for j in range(4):
            self.cp("pool", self.ksT[:, j, 0:128], self.ksT[:, j, MT:MT + 128], [self.Bks[j]], [self.Bks[j]])
        self.cp("pool", self.vs[:, 0, :, 0:64], self.vs[:, 4, :, 0:64], [self.Bvs[4]], [self.Bvs[0]])

    def merge(self):
        for half in range(2):
            wa, Bwa = self.w_next(G_WA0 + half)
            wbb, Bwb = self.w_next(G_WB0 + half, prev_live=True)
            wav = wa[:].rearrange("p (k n) -> p k n", k=KC)
            wbv = wbb[:].rearrange("p (k n) -> p k n", k=KC)
            for c in range(4):
                oc = half * 4 + c
                k = self.rr("tmpAB", 2)
                psa, Bpa = self.ps_next()
                for kc in range(KC):
                    self.mm(psa[:], wav[:, kc, c * 128:(c + 1) * 128], self.yT[:, kc, :], kc == 0, kc == KC - 1, [Bwa] + self.ByT[0], [Bpa])
                psb_, Bpb = self.ps_next()
                for kc in range(KC):
                    self.mm(psb_[:], wbv[:, kc, c * 128:(c + 1) * 128], self.yT[:, 8 + kc, :], kc == 0, kc == KC - 1, [Bwb] + self.ByT[1], [Bpb])
                self.stt(self.tmpA[:, k, :], self.tg[:, oc, :], 1.0, psa[:], ALU.add, ALU.mult, [self.Btg[oc], Bpa], [self.BtA[k]])
                self.stt(self.tmpB[:, k, :], self.tg[:, 8 + oc, :], 1.0, psb_[:], ALU.add, ALU.mult, [self.Btg[8 + oc], Bpb], [self.BtB[k]])
                self.tt("pool", self.hT[:, oc, :], self.tmpA[:, k, :], self.tmpB[:, k, :], ALU.add, [self.BtA[k], self.BtB[k]], self.BhT)

    def tm_residual(self, groups, lhs_src, Blhs_fn, nk, scale=1.0, after_sub=None):
        ws = [self.w_next(groups[0]), self.w_next(groups[1], prev_live=True)]
        for sub in range(NSUB):
            for half in range(2):
                w, Bw = ws[half]
                wv = w[:].rearrange("p (k n) -> p k n", k=nk)
                ps, Bp = self.ps_next()
                for kc in range(nk):
                    self.mm(ps[:], lhs_src[:, kc, sub * 128:(sub + 1) * 128], wv[:, kc, :], kc == 0, kc == nk - 1, [Bw] + Blhs_fn(sub), [Bp])
                xs = self.xs[:, sub, half * 512:(half + 1) * 512]
                self.stt(xs, ps[:], scale, xs, ALU.mult, ALU.add, [self.Bx[sub], Bp], [self.Bx[sub]])
            if after_sub is not None:
                after_sub(sub)

    def after_wout(self, sub):
        self.norm_stats(sub)
        if sub >= 2:
            self.norm_apply(sub - 2, 2)
        if sub == NSUB - 1:
            self.norm_apply(sub - 1, 2)
            self.norm_apply(sub, 2)

    def mlp(self):
        uT = []
        for c in range(8):
            uT.append((self.qsT[:, c, :], self.Bqs[c]))
        for c in range(8):
            uT.append((self.qkT[:, c, :], self.Bqk[c]))
        for c in range(16):
            uT.append((self.tg[:, c, :], self.Btg[c]))
        self.uT = uT
        for g in range(8):
            def cons(c, ps, Bp, g=g):
                cc = g * 4 + c
                k = self.rr("rl", 2)
                self.act(self.rl[:, k, :], ps[:], AF.Relu, [Bp], [self.Brl[k]])
                u, Bu = uT[cc]
                self.tt("pool", u, self.rl[:, k, :], self.rl[:, k, :], ALU.mult, [self.Brl[k]], [Bu])
            self.proj_fm(G_UP0 + g, cons)
        for half in range(2):
            pss = [self.ps_next() for _ in range(NSUB)]
            for kg in range(4):
                w, Bw = self.w_next(G_DN0 + half * 4 + kg)
                wv = w[:].rearrange("p (k n) -> p k n", k=KC)
                for sub in range(NSUB):
                    ps, Bp = pss[sub]
                    for kc in range(KC):
                        u, Bu = uT[kg * 8 + kc]
                        self.mm(ps[:], u[:, sub * 128:(sub + 1) * 128], wv[:, kc, :], kg == 0 and kc == 0, kg == 3 and kc == KC - 1, [Bw, Bu], [Bp])
            for sub in range(NSUB):
                ps, Bp = pss[sub]
                xs = self.xs[:, sub, half * 512:(half + 1) * 512]
                self.tt("dve", xs, xs, ps[:], ALU.add, [self.Bx[sub], Bp], [self.Bx[sub]])
                if half == 1:
                    self.norm_stats(sub)

    def ple_p(self, t):
        for sub in range(NSUB):
            r0 = t * MT + sub * 128
            self.dma(self.pld[:, sub, :], self.p_main[r0:r0 + 128, :], [], self.Bpld)
        self.tsc("pool", self.pbf[:], self.pld, 0.5, 0.0, ALU.mult, ALU.add, self.Bpld, [self.Bpbf])
        ps, Bp = self.ps_next()
        psb = ps[:].bitcast(BF16)
        for sub in range(NSUB):
            for c in range(2):
                self.tr(psb[:, (c * 4 + sub) * 128:(c * 4 + sub + 1) * 128], self.pbf[:, sub, c * 128:(c + 1) * 128], [self.Bpbf], [Bp])
        self.cp("act", self.ppT[:].rearrange("p c t -> p (c t)"), psb[:, 0:1024], [Bp], [self.BppT])

    def ple(self, t, nxt):
        self.norm_stage(3, stats_done=True)
        if nxt:
            for sub in range(NSUB):
                self.norm_stats(sub, tmp=True)
        wp = None
        for half in range(2):
            w, Bw = self.w_next(G_PG0 + half, prev_live=(half == 1))
            wv = w[:].rearrange("p (k n) -> p k n", k=KC)
            if half == 0:
                wp, Bwp = self.w_next(G_PP, prev_live=True)
                wpv = wp[:, 0:2048].rearrange("p (k n) -> p k n", k=2)
            for sub in range(NSUB):
                psg, Bpg = self.ps_next()
                for kc in range(KC):
                    self.mm(psg[:], self.hT[:, kc, sub * 128:(sub + 1) * 128], wv[:, kc, :], kc == 0, kc == KC - 1, [Bw, self.BhT[sub]], [Bpg])
                psp, Bpp = self.ps_next()
                for kc in range(2):
                    self.mm(psp[:], self.ppT[:, kc, sub * 128:(sub + 1) * 128], wpv[:, kc, half * 512:(half + 1) * 512], kc == 0, kc == 1,
                            [Bwp, self.BppT], [Bpp])
                k = self.rr("tgp", 2)
                self.act(self.tgp[:, k, :], psg[:], AF.Tanh, [Bpg], [self.Btgp[k]], scale=0.5)
                self.stt(self.ptmp[:, k, :], self.tgp[:, k, :], 1.0, psp[:], ALU.add, ALU.mult, [self.Btgp[k], Bpp], [self.Bptmp[k]])
                xs = self.xs[:, sub, half * 512:(half + 1) * 512]
                self.tt("pool", xs, xs, self.ptmp[:, k, :], ALU.add, [self.Bx[sub], self.Bptmp[k]], [self.Bx[sub]])

    def flush_out(self, n=None):
        while self.pending_out and (n is None or n > 0):
            o, src, Bs = self.pending_out.pop(0)
            self.dma(o, src, Bs, [self.Bout])
            if n is not None:
                n -= 1

    def final(self, t):
        for sub in range(NSUB):
            self.norm_stats(sub)
        for sub in range(NSUB):
            ko = self.rr("ot", 4)
            self.stt(self.ot[ko], self.xs[:, sub, :], self.nsc[:, sub, 2:3], self.gfin[:], ALU.mult, ALU.mult,
                     [self.Bx[sub], self.Bnsc[sub], self.Bconst], self.Bot[ko])
            r0 = t * MT + sub * 128
            self.pending_out.append((self.out_d[r0:r0 + 128, :], self.ot[ko], self.Bot[ko]))

    def prefix_proj(self, t, last):
        self.mark('prefix')
        if last:
            self.proj_tm(G_SMALL, self.cons_small, ncols=264)
        else:
            self.proj_tm(G_SMALL, self.cons_gates_only, ncols=8, col0=256)
        self.proj_conv(G_KML, 4)
        if not last:
            self.load_x(self.x_pre, t + 1)
        else:
            self.load_x(self.x_main, 0)
        self.gates()
        self.proj_tm(G_V0, self.cons_v(0))
        for sub in range(NSUB):
            self.norm_stats(sub)
        self.proj_tm(G_V1, self.cons_v(1))
        if last:
            self.proj_conv(G_QML, 0)
            self.proj_fm(G_KSW, self.cons_ksw)

    def prefix_state(self, t, last):
        for pair in range(2):
            st = [self.mlstm_kt(sub) for sub in (2 * pair, 2 * pair + 1)]
            for sub, (kk, ps, Bp) in zip((2 * pair, 2 * pair + 1), st):
                dC, dn = self.mlstm_dc(sub, kk, ps, Bp)
                self.mlstm_update(sub, dC, dn)
        if last:
            self.swa_shift()

    def main_tile(self, t):
        if t == 0:
            self.special_s0()
        self.mark('proj')
        self.proj_conv(G_KML, 4)
        self.proj_tm(G_V0, self.cons_v(0))
        self.proj_tm(G_V1, self.cons_v(1))
        self.proj_tm(G_SMALL, self.cons_small, ncols=264)
        self.proj_conv(G_QML, 0)
        self.proj_fm(G_KSW, self.cons_ksw)
        self.gates()
        self.proj_fm(G_QS0, self.cons_qs(0))
        self.proj_fm(G_QS1, self.cons_qs(1))
        self.proj_tm(G_O0, self.cons_o(0))
        self.proj_tm(G_O1, self.cons_o(1))
        self.proj_fm(G_GA0, self.cons_g(0))
        self.proj_fm(G_GA1, self.cons_g(4))
        self.proj_fm(G_GB0, self.cons_g(8))
        self.proj_fm(G_GB1, self.cons_g(12))
        if STOP == "proj":
            return
        self.flush_out()
        if self.copy_x_pending:
            for sub in range(NSUB):
                self.dma(self.xs[:, sub, :], self.xtmp[:, sub, :], self.Bxtmp[sub], [self.Bx[sub]])
            self.copy_x_pending = False
        self.mark('attn')
        deferred = []
        for sub in range(NSUB):
            self.attention(sub, (t == 0 and sub == 0), deferred)
        for f in deferred:
            f()
        self.swa_shift()
        if STOP == "mix":
            return
        self.mark('merge')
        self.ple_p(t)
        self.merge()
        if STOP == "merge":
            return
        nxt = t + 1 < self.nmain
        if nxt:
            for sub in range(NSUB):
                r0 = (t + 1) * MT + sub * 128
                self.dma(self.xtmp[:, sub, :], self.x_main[r0:r0 + 128, :], [], self.Bxtmp[sub])
        self.mark('wout')
        self.tm_residual([G_WO0, G_WO1], self.hT, lambda sub: [self.BhT[sub]], KC, scale=0.5, after_sub=self.after_wout)
        if STOP == "wout":
            return
        self.mark('mlp')
        self.mlp()
        if STOP == "mlp":
            return
        self.mark('ple')
        self.ple(t, nxt)
        if STOP == "ple":
            return
        if nxt:
            for sub in range(NSUB):
                self.norm_apply(sub, 0, tmp=True)
        self.mark('final')
        self.final(t)
        self.copy_x_pending = nxt

    def _build(self):
        self.w_init()
        self.setup()
        if STOP == "setup":
            return
        nfirst = len(PRE_LAST_GROUPS)
        self.convert_weights(CONV_ORDER[:nfirst])
        if STOP == "cvt0":
            return
        self.cvt_queue = list(CONV_ORDER[nfirst:])
        self.load_x(self.x_pre, 0)
        self.norm_stage(0)
        for t in range(self.npre):
            last = t == self.npre - 1
            self.prefix_proj(t, last)
            self.norm_stage(0, stats_done=True)
            self.prefix_state(t, last)
            self.cvt_one()
        while self.cvt_queue:
            self.cvt_one()
        if STOP == "cvt":
            return
        for t in range(self.nmain):
            self.main_tile(t)
        self.flush_out()


def _grp(w2d, nk=KC):
    ncols = w2d.shape[1]
    a = w2d.reshape(nk, 128, ncols).transpose(1, 0, 2).reshape(128, nk * ncols)
    if a.shape[1] < GSZ:
        a = np.concatenate([a, np.zeros((128, GSZ - a.shape[1]), np.float32)], axis=1)
    return a


def pack_weights(inp):
    w_in = np.asarray(inp["w_in"][0], np.float32)
    wg = np.zeros((NG, 128, GSZ), np.float32)
    wg[G_KML] = _grp(w_in[:, C_QK + 512:C_QK + 1024])
    wg[G_V0] = _grp(w_in[:, C_V:C_V + 512])
    wg[G_V1] = _grp(w_in[:, C_V + 512:C_V + 1024])
    small = np.zeros((D, 512), np.float32)
    small[:, 0:256] = w_in[:, C_VS:C_VS + 256]
    small[:, 256:264] = w_in[:, C_IF:C_IF + 8]
    wg[G_SMALL] = _grp(small)
    wg[G_QML] = _grp(w_in[:, C_QK:C_QK + 512])
    ksd = np.zeros((D, 512), np.float32)
    for j in range(4):
        kj = w_in[:, C_KS + j * 64:C_KS + (j + 1) * 64]
        ksd[:, j * 128:j * 128 + 64] = kj
        ksd[:, j * 128 + 64:(j + 1) * 128] = kj
    wg[G_KSW] = _grp(ksd)
    for i in range(2):
        wg[G_QS0 + i] = _grp(w_in[:, C_QS + i * 512:C_QS + (i + 1) * 512])
        wg[G_O0 + i] = _grp(w_in[:, C_O + i * 512:C_O + (i + 1) * 512])
        wg[G_GA0 + i] = _grp(w_in[:, C_GA + i * 512:C_GA + (i + 1) * 512])
        wg[G_GB0 + i] = _grp(w_in[:, C_GB + i * 512:C_GB + (i + 1) * 512])
        wg[G_WA0 + i] = _grp(np.asarray(inp["w_branch_a"][0])[:, i * 512:(i + 1) * 512])
        wg[G_WB0 + i] = _grp(np.asarray(inp["w_branch_b"][0])[:, i * 512:(i + 1) * 512])
        wg[G_WO0 + i] = _grp(np.asarray(inp["w_out"][0])[:, i * 512:(i + 1) * 512])
        wg[G_PG0 + i] = _grp(np.asarray(inp["w_ple_gate"][0])[:, i * 512:(i + 1) * 512])
    w_up = np.asarray(inp["w_up"][0])
    for g in range(8):
        wg[G_UP0 + g] = _grp(w_up[:, g * 512:(g + 1) * 512])
    w_dn = np.asarray(inp["w_down"][0])
    for half in range(2):
        for kg in range(4):
            wg[G_DN0 + half * 4 + kg] = _grp(w_dn[kg * 1024:(kg + 1) * 1024, half * 512:(half + 1) * 512])
    wg[G_PP] = _grp(np.asarray(inp["w_ple_proj"][0]), nk=2)
    return wg


def pack_common(inp):
    col = lambda v: np.ascontiguousarray(np.asarray(v, np.float32).reshape(KC, 128).T)
    gains = np.stack([col(inp["norm_mix_g"][0]), col(inp["mlstm_norm_g"][0]), col(inp["norm_mlp_g"][0]), col(inp["norm_ple_g"][0])], axis=1)
    convw = np.ascontiguousarray(np.asarray(inp["conv_qk"][0], np.float32).reshape(4, KC, 128).transpose(2, 1, 0))
    bif = np.ascontiguousarray(np.broadcast_to(np.asarray(inp["b_if"][0], np.float32)[None, :], (128, 8)))
    sinks = np.ascontiguousarray(np.broadcast_to(np.asarray(inp["sinks"][0], np.float32)[None, :], (128, 16)))
    gfin = np.ascontiguousarray(np.broadcast_to(np.asarray(inp["final_norm_g"], np.float32)[None, :], (128, D)))
    gmlb = np.ascontiguousarray(np.broadcast_to(np.asarray(inp["mlstm_norm_g"][0], np.float32)[None, :], (128, D)))
    ii = np.arange(128)
    mask = (ii[:, None] <= ii[None, :]).astype(np.float32)
    cmat = np.stack([np.eye(128, dtype=np.float32), mask, np.ones((128, 128), np.float32)], axis=1)
    return dict(wg=pack_weights(inp), gains=np.ascontiguousarray(gains), convw=convw, bif=bif, sinks=sinks, gfin=gfin, gmlb=gmlb,
                cmat=np.ascontiguousarray(cmat)), mask


def mpair_for(mask, first_half):
    mp = np.zeros((128, 2, 2, 128), np.float32)
    mp[:, 1, 0, :] = 1.0 - mask
    mp[:, 1, 1, :] = mask
    mp[:, 0, 1, :] = mask
    mp[:, 0, 0, :] = 0.0 if first_half else (1.0 - mask)
    return mp


_PROG_CACHE = {}


def get_prog(npre, nmain):
    key = (npre, nmain)
    if key not in _PROG_CACHE:
        _PROG_CACHE[key] = Prog(npre, nmain)
    return _PROG_CACHE[key]


def kernel(**inputs):
    x = np.asarray(inputs["x"], np.float32)
    p = np.asarray(inputs["p"], np.float32)[0]
    Bsz, S, _ = x.shape
    half = S // 2
    npre = nmain = half // MT
    common, mask = pack_common(inputs)
    prog = get_prog(npre, nmain)
    in_maps = []
    for c in range(8):
        b, h = c // 2, c % 2
        m = dict(common)
        m["x_pre"] = np.zeros((half, D), np.float32) if h == 0 else np.ascontiguousarray(x[b, 0:half])
        m["x_main"] = np.ascontiguousarray(x[b, h * half:(h + 1) * half])
        m["p_main"] = np.ascontiguousarray(p[b, h * half:(h + 1) * half])
        m["mpair"] = mpair_for(mask, h == 0)
        in_maps.append(m)
    res = run_bass_kernel_spmd(prog.nc, in_maps, core_ids=list(range(8)))
    out = np.empty((Bsz, S, D), np.float32)
    for c in range(8):
        b, h = c // 2, c % 2
        out[b, h * half:(h + 1) * half] = res.results[c]["out"]
    return out
```

```python
from contextlib import ExitStack
import numpy as np
import concourse.bass as bass
import concourse.mybir as mybir
from concourse.bass_utils import run_bass_kernel_spmd

F32 = mybir.dt.float32
BF16 = mybir.dt.bfloat16
AF = mybir.ActivationFunctionType
ALU = mybir.AluOpType

ENGS = ("pe", "act", "dve", "pool", "sp")
NDMASEM = 8


class Buf:
    __slots__ = ("name", "writers", "readers", "excl")

    def __init__(self, name, excl=False):
        self.name = name
        self.writers = {}
        self.readers = {}
        self.excl = excl


class Op:
    __slots__ = ("eng", "fn", "deps", "signal", "count", "idx", "is_dma", "dma_slot", "dma_val", "waits")

    def __init__(self, eng, fn, is_dma):
        self.eng = eng
        self.fn = fn
        self.deps = []
        self.signal = False
        self.count = 0
        self.is_dma = is_dma
        self.dma_slot = None
        self.dma_val = 0
        self.waits = []


class Sched:
    def __init__(self, nc, same_engine_sync=True):
        self.nc = nc
        self.ops = {e: [] for e in ENGS}
        self.all_ops = []
        self.same_engine_sync = same_engine_sync

    def _add(self, eng, fn, reads, writes, is_dma):
        op = Op(eng, fn, is_dma)
        op.idx = len(self.all_ops)
        deps = {}
        for b in reads:
            for w in b.writers.values():
                deps[id(w)] = w
            if b.excl:
                for r in b.readers.values():
                    if r.eng != eng:
                        deps[id(r)] = r
        for b in writes:
            for w in b.writers.values():
                deps[id(w)] = w
            for r in b.readers.values():
                deps[id(r)] = r
        op.deps = list(deps.values())
        key = eng if not is_dma else ("dma", op.idx)
        for b in reads:
            b.readers[key] = op
        for b in writes:
            b.writers = {key: op}
            b.readers = {}
        self.ops[eng].append(op)
        self.all_ops.append(op)
        return op

    def op(self, eng, fn, reads=(), writes=()):
        return self._add(eng, fn, reads, writes, False)

    def dma(self, eng, fn, reads=(), writes=()):
        return self._add(eng, fn, reads, writes, True)

    def _skip_same(self, d, op):
        return (d.eng == op.eng and not op.is_dma and not d.is_dma
                and (d.eng in ("pe", "sp") or not self.same_engine_sync))

    def finalize(self):
        dma_i = {e: 0 for e in ENGS}
        for op in self.all_ops:
            if op.is_dma:
                i = dma_i[op.eng]
                dma_i[op.eng] += 1
                op.dma_slot = (op.eng, i % NDMASEM)
                op.dma_val = 16 * (i // NDMASEM + 1)
        for op in self.all_ops:
            for d in op.deps:
                if d.is_dma or self._skip_same(d, op):
                    continue
                d.signal = True
        cnt = {e: 0 for e in ENGS}
        for op in self.all_ops:
            if op.signal and not op.is_dma:
                cnt[op.eng] += 1
                op.count = cnt[op.eng]
        waited = {e: {} for e in ENGS}
        prev_dma_on_slot = {}
        for op in self.all_ops:
            w = waited[op.eng]
            need = {}
            for d in op.deps:
                if d.is_dma:
                    key = ("dma",) + d.dma_slot
                    val = d.dma_val
                else:
                    if self._skip_same(d, op):
                        continue
                    key = ("eng", d.eng)
                    val = d.count
                if w.get(key, 0) >= val:
                    continue
                if need.get(key, 0) < val:
                    need[key] = val
            if op.is_dma:
                p = prev_dma_on_slot.get(op.dma_slot)
                if p is not None:
                    key = ("dma",) + p.dma_slot
                    if w.get(key, 0) < p.dma_val and need.get(key, 0) < p.dma_val:
                        need[key] = p.dma_val
                prev_dma_on_slot[op.dma_slot] = op
            for key, val in need.items():
                w[key] = val
            op.waits = list(need.items())
        self.final_dma = dict(prev_dma_on_slot)

    def emit(self):
        nc = self.nc
        self.finalize()
        with ExitStack() as es:
            esem = {e: es.enter_context(nc.semaphore("s_" + e)) for e in ENGS}
            dsem = {}
            for e in ENGS:
                if any(o.is_dma for o in self.ops[e]):
                    for k in range(NDMASEM):
                        dsem[(e, k)] = es.enter_context(nc.semaphore("d_%s%d" % (e, k)))
            block = es.enter_context(nc.Block())

            def run(eng_name, eng):
                for op in self.ops[eng_name]:
                    for key, val in op.waits:
                        if key[0] == "eng":
                            eng.wait_ge(esem[key[1]], val)
                        else:
                            eng.wait_ge(dsem[(key[1], key[2])], val)
                    ins = op.fn(eng)
                    if op.is_dma:
                        ins.then_inc(dsem[op.dma_slot], 16)
                    elif op.signal:
                        ins.then_inc(esem[eng_name], 1)
                if eng_name == "sp":
                    for slot, p in self.final_dma.items():
                        eng.wait_ge(dsem[slot], p.dma_val)

            @block.tensor
            def _(e):
                run("pe", e)

            @block.scalar
            def _(e):
                run("act", e)

            @block.vector
            def _(e):
                run("dve", e)

            @block.gpsimd
            def _(e):
                run("pool", e)

            @block.sync
            def _(e):
                run("sp", e)


D = 1024
KC = 8
MT = 512
NSUB = 4
EPS = 1e-6
DKS = 128 ** -0.5
NG = 39
GSZ = 4096
NWB = 4
STOP = None
DBG = 9

C_QK, C_V, C_O, C_IF, C_QS, C_KS, C_VS, C_GA, C_GB = 0, 1024, 2048, 3072, 3080, 4104, 4360, 4616, 5640

G_KML, G_V0, G_V1, G_SMALL, G_QML, G_KSW, G_QS0, G_QS1, G_O0, G_O1 = range(10)
G_GA0, G_GA1, G_GB0, G_GB1, G_WA0, G_WA1, G_WB0, G_WB1, G_WO0, G_WO1 = range(10, 20)
G_UP0 = 20
G_DN0 = 28
G_PG0, G_PG1, G_PP = 36, 37, 38

GN_MIX, GN_MIX8, GN_ML05, GN_ONE, GN_HALF, GN_MLP, GN_PLE = range(7)


def group_gain(g):
    if g in (G_QS0, G_QS1):
        return GN_MIX8
    if g <= G_GB1:
        return GN_MIX
    if g in (G_WA0, G_WA1):
        return GN_ML05
    if g in (G_WB0, G_WB1):
        return GN_ONE
    if g in (G_WO0, G_WO1):
        return GN_HALF
    if G_UP0 <= g < G_DN0:
        return GN_MLP
    if G_DN0 <= g < G_PG0:
        return GN_ONE
    if g in (G_PG0, G_PG1):
        return GN_PLE
    return GN_HALF


PRE_GROUPS = [G_SMALL, G_KML, G_V0, G_V1]
PRE_LAST_GROUPS = [G_SMALL, G_KML, G_V0, G_V1, G_QML, G_KSW]
MAIN_GROUPS = ([G_KML, G_V0, G_V1, G_SMALL, G_QML, G_KSW, G_QS0, G_QS1, G_O0, G_O1, G_GA0, G_GA1, G_GB0, G_GB1,
                G_WA0, G_WB0, G_WA1, G_WB1, G_WO0, G_WO1] + list(range(G_UP0, G_UP0 + 8))
               + list(range(G_DN0, G_DN0 + 8)) + [G_PG0, G_PP, G_PG1])
CONV_ORDER = PRE_LAST_GROUPS + [g for g in MAIN_GROUPS if g not in PRE_LAST_GROUPS]


class Prog:
    def __init__(self, npre, nmain, same_engine_sync=True):
        self.npre, self.nmain = npre, nmain
        nc = bass.Bass("TRN2", target_bir_lowering=False)
        self.nc = nc
        self.S = Sched(nc, same_engine_sync)
        self.es = ExitStack()
        self.ring_i = 0
        self.cnt = {}
        self.cvt_queue = []
        self.ps_reserved = set()
        self.pending_out = []
        self.copy_x_pending = False
        self.marks = []
        self._alloc()
        self._build()
        self.S.emit()
        self.es.close()

    def sb(self, name, shape, dt):
        return self.es.enter_context(self.nc.sbuf_tensor("sb_" + name, shape, dt))

    def rr(self, name, n):
        i = self.cnt.get(name, 0)
        self.cnt[name] = i + 1
        return i % n

    def mark(self, name):
        self.marks.append((name, len(self.S.ops['pe'])))

    def ps_next(self):
        while True:
            i = self.ring_i % 8
            self.ring_i += 1
            if i not in self.ps_reserved:
                return self.ps[i], self.Bps[i]

    def ps_reserve(self):
        ps, Bp = self.ps_next()
        self.ps_reserved.add(self.Bps.index(Bp))
        return ps, Bp

    def ps_release(self, Bp):
        self.ps_reserved.discard(self.Bps.index(Bp))

    def op(self, eng, fn, reads=(), writes=()):
        return self.S.op(eng, fn, reads, writes)

    def act(self, out, in_, func, reads, writes, **kw):
        self.S.op("act", lambda e: e.activation(out=out, in_=in_, func=func, **kw), reads, writes)

    def tsc(self, eng, out, in0, s1, s2, op0, op1, reads, writes):
        if op1 is None:
            self.S.op(eng, lambda e: e.tensor_scalar(out=out, in0=in0, scalar1=s1, scalar2=None, op0=op0), reads, writes)
        else:
            self.S.op(eng, lambda e: e.tensor_scalar(out=out, in0=in0, scalar1=s1, scalar2=s2, op0=op0, op1=op1), reads, writes)

    def stt(self, out, in0, scalar, in1, op0, op1, reads, writes):
        self.S.op("dve", lambda e: e.scalar_tensor_tensor(out=out, in0=in0, scalar=scalar, in1=in1, op0=op0, op1=op1), reads, writes)

    def tt(self, eng, out, in0, in1, op, reads, writes):
        self.S.op(eng, lambda e: e.tensor_tensor(out=out, in0=in0, in1=in1, op=op), reads, writes)

    def cp(self, eng, out, in_, reads, writes):
        if eng == "act":
            self.act(out, in_, AF.Copy, reads, writes)
        else:
            self.S.op(eng, lambda e: e.tensor_copy(out=out, in_=in_), reads, writes)

    def mm(self, out, lhsT, rhs, start, stop, reads, writes):
        self.S.op("pe", lambda e: e.matmul(out, lhsT=lhsT, rhs=rhs, start=start, stop=stop), reads, writes)

    def tr(self, out, in_, reads, writes):
        idb = self.identb
        self.S.op("pe", lambda e: e.transpose(out=out, in_=in_, identity=idb[:]), list(reads) + [self.Bconst], writes)

    def dma(self, out, in_, reads, writes, eng="sp"):
        self.S.dma(eng, lambda e: e.dma_start(out=out, in_=in_), reads, writes)

    def _alloc(self):
        nc = self.nc
        npre, nmain = self.npre, self.nmain
        dt_in = lambda name, shape: nc.dram_tensor(name, shape, F32, kind="ExternalInput").ap()
        self.x_pre = dt_in("x_pre", [npre * MT, D])
        self.x_main = dt_in("x_main", [nmain * MT, D])
        self.p_main = dt_in("p_main", [nmain * MT, 256])
        self.wg = dt_in("wg", [NG, 128, GSZ])
        self.gains_d = dt_in("gains", [128, 4, 8])
        self.convw_d = dt_in("convw", [128, 8, 4])
        self.bif_d = dt_in("bif", [128, 8])
        self.sinks_d = dt_in("sinks", [128, 16])
        self.gfin_d = dt_in("gfin", [128, D])
        self.gmlb_d = dt_in("gmlb", [128, D])
        self.cmat_d = dt_in("cmat", [128, 3, 128])
        self.mpair_d = dt_in("mpair", [128, 2, 2, 128])
        self.out_d = nc.dram_tensor("out", [nmain * MT, D], F32, kind="ExternalOutput").ap()
        self.scr = nc.dram_tensor("wscr", [NG, 128, GSZ], BF16, kind="Internal").ap()
        self.Bscr = [Buf("scr%d" % g) for g in range(NG)]

        sb = self.sb
        B = lambda n: Buf(n)
        self.cmat = sb("cmat", [128, 3, 128], F32)
        self.identb = sb("identb", [128, 128], BF16)
        self.maskb = sb("maskb", [128, 128], BF16)
        self.mpair = sb("mpair", [128, 2, 2, 128], BF16)
        self.gains = sb("gains", [128, 4, 8], F32)
        self.convw = sb("convw", [128, 8, 4], F32)
        self.bif = sb("bif", [128, 8], F32)
        self.sinks = sb("sinks", [128, 16], F32)
        self.esink = sb("esink", [128, 16], F32)
        self.gfin = sb("gfin", [128, D], F32)
        self.mhalf = sb("mhalf", [128, 16], F32)
        self.onecol = sb("onecol", [128, 1], BF16)
        self.Bconst = B("const")
        self.xs = sb("xs", [128, NSUB, D], F32)
        self.Bx = [B("x%d" % i) for i in range(NSUB)]
        self.hT = sb("hT", [128, KC, MT], BF16)
        self.BhT = [B("hT%d" % i) for i in range(NSUB)]
        self.hn = sb("hn", [128, 2, D], BF16)
        self.Bhn = [B("hn0"), B("hn1")]
        self.junk = sb("junk", [128, 2, 256], BF16)
        self.Bjunk = [B("junk0"), B("junk1")]
        self.nsc = sb("nsc", [128, 2 * NSUB, 4], F32)
        self.Bnsc = [B("nsc%d" % i) for i in range(2 * NSUB)]
        self.gmlb = sb("gmlb", [128, D], F32)
        self.raw = sb("raw", [128, 4, MT + 4], BF16)
        self.Braw = [B("raw%d" % i) for i in range(4)]
        self.dg = sb("dg", [128, 8, 4, 128], BF16)
        self.halo = sb("halo", [128, 8, 3], BF16)
        self.Bhalo = [B("halo%d" % i) for i in range(8)]
        self.acc = sb("acc", [128, 2, MT], F32)
        self.Bacc = [B("acc0"), B("acc1")]
        self.th = sb("th", [128, 2, MT], F32)
        self.Bth = [B("th0"), B("th1")]
        self.qkT = sb("qkT", [128, 8, MT], BF16)
        self.Bqk = [B("qk%d" % i) for i in range(8)]
        self.vml = sb("vml", [128, NSUB, D], BF16)
        self.Bv = [B("v%d" % i) for i in range(NSUB)]
        self.pld = self.vml[:, 0:2, :].bitcast(F32).rearrange("p a b -> p (a b)").rearrange("p (s f) -> p s f", s=NSUB)
        self.Bpld = [self.Bv[0], self.Bv[1]]
        self.to1 = sb("to1", [128, NSUB, D], BF16)
        self.Bto = [B("to%d" % i) for i in range(NSUB)]
        self.gsb = sb("gsb", [128, NSUB, 8], F32)
        self.Bgsb = B("gsb")
        self.gsc = sb("gsc", [128, 16, 16], F32)
        self.Bgsc = B("gsc")
        self.qsT = sb("qsT", [128, 8, MT], BF16)
        self.Bqs = [B("qs%d" % i) for i in range(8)]
        self.ksT = sb("ksT", [128, 4, MT + 128], BF16)
        self.Bks = [B("ks%d" % i) for i in range(4)]
        self.vs = sb("vs", [128, 5, 4, 65], BF16)
        self.Bvs = [B("vs%d" % i) for i in range(5)]
        self.tg = sb("tgab", [128, 16, MT], BF16)
        self.Btg = [B("tg%d" % i) for i in range(16)]
        self.yT = sb("yT", [128, 16, MT], BF16)
        self.ByT = [[B("yaT%d" % i) for i in range(NSUB)], [B("ybT%d" % i) for i in range(NSUB)]]
        self.xtmp = self.yT[:].bitcast(F32).rearrange("p a b -> p (a b)").rearrange("p (s f) -> p s f", s=NSUB)
        self.Bxtmp = [self.ByT[0], self.ByT[0], self.ByT[1], self.ByT[1]]
        self.ya = sb("ya", [128, 2, D], BF16)
        self.Bya = [B("ya0"), B("ya1")]
        self.yb = sb("yb", [128, 2, D], BF16)
        self.Byb = [B("yb0"), B("yb1")]
        self.pT = sb("pT", [128, 2, 4, 128], BF16)
        self.BpT = [B("pT0"), B("pT1")]
        self.kw = sb("kw", [128, 2, 4, 128], BF16)
        self.Bkw = [B("kw0"), B("kw1")]
        self.E = sb("E", [128, 4, 2, MT], BF16)
        self.BE = [[B("E%d0" % i), B("E%d1" % i)] for i in range(4)]
        self.tmpA = sb("tmpA", [128, 2, MT], F32)
        self.BtA = [B("tA0"), B("tA1")]
        self.tmpB = sb("tmpB", [128, 2, MT], F32)
        self.BtB = [B("tB0"), B("tB1")]
        self.ot = [self.tmpA[:].rearrange("p a b -> p (a b)"), self.tmpB[:].rearrange("p a b -> p (a b)")]
        self.mpair_f = self.ot[0][:, 0:512].rearrange("p (a b c) -> p a b c", a=2, b=2)
        self.Bot = [self.BtA, self.BtB]
        self.rl = sb("rl", [128, 2, MT], F32)
        self.Brl = [B("rl0"), B("rl1")]

        self.C = sb("C", [128, 4, 256], F32)
        self.BC = [B("C%d" % i) for i in range(4)]
        self.nst = sb("nst", [128, 8], F32)
        self.Bn = B("n")
        self.Cin = sb("Cin", [128, 4, 256], BF16)
        self.BCin = [B("Cin%d" % i) for i in range(4)]
        self.nin = sb("nin", [128, 4], BF16)
        self.Bnin = B("nin")
        self.msc = sb("msc", [128, 2, 8, 4], F32)
        self.Bmsc = [B("msc0"), B("msc1")]
        self.ssc = sb("ssc", [128, 2, 2, 4], F32)
        self.Bssc = [B("ssc0"), B("ssc1")]
        self.pbf = sb("pbf", [128, NSUB, 256], BF16)
        self.Bpbf = B("pbf")
        self.ppT = sb("ppT", [128, 2, MT], BF16)
        self.BppT = B("ppT")
        self.tgp, self.Btgp = self.acc, self.Bacc
        self.ot += [self.acc[:].rearrange("p a b -> p (a b)"), self.rl[:].rearrange("p a b -> p (a b)")]
        self.Bot += [self.Bacc, self.Brl]
        self.ptmp, self.Bptmp = self.th, self.Bth
        self.wb = [sb("wb%d" % i, [128, GSZ], BF16) for i in range(NWB)]
        self.Bwb = [B("wb%d" % i) for i in range(NWB)]
        self.Bout = B("outd")
        self.ps = [self.es.enter_context(nc.psum_tensor("ps%d" % i, [128, 512], F32)) for i in range(8)]
        self.Bps = [Buf("ps%d" % i, excl=True) for i in range(8)]

    def w_init(self):
        seq = []
        for t in range(self.npre):
            seq += PRE_LAST_GROUPS if t == self.npre - 1 else PRE_GROUPS
        for t in range(self.nmain):
            seq += MAIN_GROUPS
        self.wseq = seq
        self.w_i = 0
        self.w_issued = 0

    def w_next(self, g, prev_live=False):
        assert self.wseq[self.w_i] == g, (self.w_i, self.wseq[self.w_i], g)
        while self.w_issued < min(len(self.wseq), self.w_i + NWB - (1 if prev_live else 0)):
            k = self.w_issued
            gg = self.wseq[k]
            b = k % NWB
            self.dma(self.wb[b][:], self.scr[gg], [self.Bscr[gg]], [self.Bwb[b]])
            self.w_issued += 1
        b = self.w_i % NWB
        self.w_i += 1
        return self.wb[b], self.Bwb[b]

    def setup(self):
        n0 = len(self.S.all_ops)
        Bcm, Bgn, Bcw, Bbf, Bsk, Bgf, Bgm, Bes = (Buf(n) for n in ("c_cm", "c_gn", "c_cw", "c_bf", "c_sk", "c_gf", "c_gm", "c_es"))
        self.dma(self.cmat[:], self.cmat_d, [], [Bcm])
        self.dma(self.convw[:], self.convw_d, [], [Bcw])
        self.dma(self.mpair_f, self.mpair_d, [], self.BtA)
        self.dma(self.gains[:], self.gains_d, [], [Bgn])
        self.dma(self.bif[:], self.bif_d, [], [Bbf])
        self.dma(self.sinks[:], self.sinks_d, [], [Bsk])
        self.dma(self.gfin[:], self.gfin_d, [], [Bgf])
        self.dma(self.gmlb[:], self.gmlb_d, [], [Bgm])
        cm = self.cmat
        self.cp("dve", self.identb[:], cm[:, 0, :], [Bcm], [Buf("c_id")])
        self.cp("dve", self.maskb[:], cm[:, 1, :], [Bcm], [Buf("c_mk")])
        self.op("dve", lambda e: e.memset(self.mhalf[:], -0.5), [], [Buf("c_mh")])
        self.op("dve", lambda e: e.memset(self.onecol[:], 1.0), [], [Buf("c_oc")])
        self.tsc("dve", self.convw[:], self.convw[:], 0.5, None, ALU.mult, None, [Bcw], [Bcw])
        for rc in range(8):
            for j in range(4):
                self.tsc("dve", self.dg[:, rc, j, :], cm[:, 0, :], self.convw[:, rc, j:j + 1], None, ALU.mult, None, [Bcm, Bcw], [Buf("c_dg")])
        self.cp("dve", self.mpair[:], self.mpair_f, self.BtA, [Buf("c_mp")])
        self.tsc("dve", self.gmlb[:], self.gmlb[:], 0.5, None, ALU.mult, None, [Bgm], [Bgm])
        self.act(self.esink[:], self.sinks[:], AF.Exp, [Bsk], [Bes])
        self.Bconst.writers = {("setup", i): o for i, o in enumerate(self.S.all_ops[n0:])}
        for h in range(4):
            self.op("pool", lambda e, h=h: e.memset(self.C[:, h, :], 0.0), [], [self.BC[h]])
        self.op("pool", lambda e: e.memset(self.nst[:], 0.0), [], [self.Bn])
        for c in range(8):
            self.op("pool", lambda e, c=c: e.memset(self.halo[:, c, :], 0.0), [], [self.Bhalo[c]])
        for b in range(5):
            self.op("pool", lambda e, b=b: e.memset(self.vs[:, b, :, 64:65], 1.0), [], [self.Bvs[b]])

    def convert_weights(self, glist):
        for g in glist:
            self.dma(self.scr[g], self.wg[g], [], [self.Bscr[g]], eng="pool")

    def cvt_one(self):
        if self.cvt_queue:
            self.convert_weights([self.cvt_queue.pop(0)])

    def xsrc(self, sub, tmp=False):
        if tmp:
            return self.xtmp[:, sub, :], self.Bxtmp[sub]
        return self.xs[:, sub, :], [self.Bx[sub]]

    def norm_stats(self, sub, tmp=False):
        x, Bxl = self.xsrc(sub, tmp)
        sl = sub + (NSUB if tmp else 0)
        ns, Bn = self.nsc, self.Bnsc[sl]
        self.act(self.hn[:, sub % 2, :], x, AF.Square, Bxl, [Bn, self.Bhn[sub % 2]], accum_out=ns[:, sl, 0:1])
        self.tsc("dve", ns[:, sl, 1:2], ns[:, sl, 0:1], 1.0 / D, EPS, ALU.mult, ALU.add, [Bn], [Bn])
        self.tt("pool", ns[:, sl, 2:3], ns[:, sl, 1:2], self.mhalf[:, 0:1], ALU.pow, [Bn, self.Bconst], [Bn])
        self.cvt_one()

    def norm_apply(self, sub, gi, tmp=False):
        x, Bxl = self.xsrc(sub, tmp)
        sl = sub + (NSUB if tmp else 0)
        ns, Bn = self.nsc, self.Bnsc[sl]
        k = self.rr("hn", 2)
        self.act(self.hn[:, k, :], x, AF.Copy, Bxl + [Bn], [self.Bhn[k]], scale=ns[:, sl, 2:3])
        ps, Bp = self.ps_next()
        psb = ps[:].bitcast(BF16)
        for kc in range(KC):
            self.tr(psb[:, kc * 128:(kc + 1) * 128], self.hn[:, k, kc * 128:(kc + 1) * 128], [self.Bhn[k]], [Bp])
        self.tt("dve", self.hT[:, :, sub * 128:(sub + 1) * 128], psb.rearrange("p (k t) -> p k t", k=KC),
                self.gains[:, gi, :].unsqueeze(2).to_broadcast([128, KC, 128]), ALU.mult, [Bp, self.Bconst], [self.BhT[sub]])

    def norm_stage(self, gi, stats_done=False, tmp=False):
        if not stats_done:
            for sub in range(NSUB):
                self.norm_stats(sub, tmp)
        for sub in range(NSUB):
            self.norm_apply(sub, gi, tmp)

    def special_s0(self):
        F = F32
        tgf = self.tg[:].bitcast(F).rearrange("p a b -> p (a b)")
        ytf = self.yT[:].bitcast(F).rearrange("p a b -> p (a b)")
        hn32 = tgf[:, 0:1024]
        hT32 = tgf[:, 1024:2048].rearrange("p (k t) -> p k t", k=KC)
        w32 = [tgf[:, 2048:3072].rearrange("p (k n) -> p k n", k=KC), tgf[:, 3072:4096].rearrange("p (k n) -> p k n", k=KC)]
        qk32 = ytf[:, 0:1024].rearrange("p (c t) -> p c t", c=8)
        raw32 = [ytf[:, 1024:1024 + 131], ytf[:, 1280:1280 + 131]]
        acc32 = [ytf[:, 1536:1664], ytf[:, 1664:1792]]
        th32 = [ytf[:, 1792:1920], ytf[:, 1920:2048]]
        self.S32 = ytf[:, 2048:2560]
        Rt, Ry = list(self.Btg), self.ByT[0] + self.ByT[1]
        Bh, BhT, Bw, Bqk = Buf("hn32"), Buf("hT32"), [Buf("w32a"), Buf("w32b")], [Buf("qk32_%d" % i) for i in range(8)]
        Braw, Bacc, Bth, BS = [Buf("raw32a"), Buf("raw32b")], [Buf("acc32a"), Buf("acc32b")], [Buf("th32a"), Buf("th32b")], Buf("S32")
        self.BS32 = BS
        ns = self.nsc
        self.act(hn32, self.xs[:, 0, :], AF.Copy, [self.Bx[0], self.Bnsc[0]] + Rt, [Bh], scale=ns[:, 0, 2:3])
        idf = self.cmat[:, 0, :]
        for half in range(2):
            ps, Bp = self.ps_next()
            for q in range(4):
                kc = half * 4 + q
                self.S.op("pe", lambda e, o=ps[:, q * 128:(q + 1) * 128], i=hn32[:, kc * 128:(kc + 1) * 128]: e.transpose(out=o, in_=i, identity=idf),
                          [Bh, self.Bconst] + Rt, [Bp])
            self.tt("dve", hT32[:, half * 4:(half + 1) * 4, :], ps[:].rearrange("p (k t) -> p k t", k=4),
                    self.gains[:, 0, half * 4:(half + 1) * 4].unsqueeze(2).to_broadcast([128, 4, 128]), ALU.mult, [Bp, self.Bconst] + Rt, [BhT])
        cw = self.convw
        for rc in range(8):
            g, c = (G_QML, rc) if rc < 4 else (G_KML, rc - 4)
            k = rc % 2
            src = self.wg[g].rearrange("p (k n) -> p k n", k=KC)[:, :, c * 128:(c + 1) * 128]
            self.dma(w32[k], src, Rt, [Bw[k]])
            ps, Bp = self.ps_next()
            for kc in range(KC):
                self.mm(ps[:, 0:128], w32[k][:, kc, :], hT32[:, kc, :], kc == 0, kc == KC - 1, [Bw[k], BhT] + Rt, [Bp])
            self.cp("dve", raw32[k][:, 0:3], self.halo[:, rc, :], [self.Bhalo[rc]] + Ry, [Braw[k]])
            self.cp("act", raw32[k][:, 3:131], ps[:, 0:128], [Bp] + Ry, [Braw[k]])
            self.tsc("dve", acc32[k], raw32[k][:, 3:131], cw[:, rc, 3:4], None, ALU.mult, None, [Braw[k], self.Bconst] + Ry, [Bacc[k]])
            for j in (2, 1, 0):
                self.stt(acc32[k], raw32[k][:, j:j + 128], cw[:, rc, j:j + 1], acc32[k], ALU.mult, ALU.add, [Braw[k], Bacc[k], self.Bconst] + Ry, [Bacc[k]])
            self.act(th32[k], acc32[k], AF.Tanh, [Bacc[k]] + Ry, [Bth[k]])
            self.stt(qk32[:, rc, :], th32[k], 1.0, acc32[k], ALU.add, ALU.mult, [Bth[k], Bacc[k]] + Ry, [Bqk[rc]])
        ps, Bp = self.ps_next()
        for h in range(4):
            self.mm(ps[:, h * 128:(h + 1) * 128], qk32[:, 4 + h, :], qk32[:, h, :], True, True, [Bqk[4 + h], Bqk[h]] + Ry, [Bp])
        self.cp("act", self.S32, ps[:], [Bp] + Ry, [BS])

    def load_x(self, src, t):
        for sub in range(NSUB):
            r0 = t * MT + sub * 128
            self.dma(self.xs[:, sub, :], src[r0:r0 + 128, :], [], [self.Bx[sub]])

    def proj_fm(self, g, consumer):
        w, Bw = self.w_next(g)
        wv = w[:].rearrange("p (k n) -> p k n", k=KC)
        for c in range(4):
            ps, Bp = self.ps_next()
            for kc in range(KC):
                self.mm(ps[:], wv[:, kc, c * 128:(c + 1) * 128], self.hT[:, kc, :], kc == 0, kc == KC - 1,
                        [Bw] + self.BhT, [Bp])
            consumer(c, ps, Bp)

    def proj_tm(self, g, consumer, ncols=512, col0=0):
        w, Bw = self.w_next(g)
        wv = w[:].rearrange("p (k n) -> p k n", k=KC)
        for sub in range(NSUB):
            ps, Bp = self.ps_next()
            for kc in range(KC):
                self.mm(ps[:, 0:ncols], self.hT[:, kc, sub * 128:(sub + 1) * 128], wv[:, kc, col0:col0 + ncols], kc == 0, kc == KC - 1,
                        [Bw, self.BhT[sub]], [Bp])
            consumer(sub, ps, Bp)

    def conv_a(self, rc, ps, Bp):
        k = self.rr("raw", 4)
        raw, Br = self.raw[:, k, :], self.Braw[k]
        self.cp("pool", raw[:, 0:3], self.halo[:, rc, :], [self.Bhalo[rc]], [Br])
        self.cp("act", raw[:, 3:MT + 3], ps[:], [Bp], [Br])
        self.cp("pool", self.halo[:, rc, :], raw[:, MT:MT + 3], [Br], [self.Bhalo[rc]])
        return k

    def conv_b(self, rc, k):
        raw, Br = self.raw[:, k, :], self.Braw[k]
        ps, Bp = self.ps_next()
        for j in range(4):
            self.mm(ps[:], self.dg[:, rc, j, :], raw[:, j:j + MT], j == 0, j == 3, [Br, self.Bconst], [Bp])
        kt = self.rr("th", 2)
        th, Bt = self.th[:, kt, :], self.Bth[kt]
        self.act(th, ps[:], AF.Tanh, [Bp], [Bt])
        self.stt(self.qkT[:, rc, :], th, 1.0, ps[:], ALU.add, ALU.mult, [Bt, Bp], [self.Bqk[rc]])

    def proj_conv(self, g, base):
        ks = []
        self.proj_fm(g, lambda c, ps, Bp: ks.append(self.conv_a(base + c, ps, Bp)))
        for c in range(4):
            self.conv_b(base + c, ks[c])

    def cons_v(self, gi):
        def f(sub, ps, Bp):
            self.cp("act", self.vml[:, sub, gi * 512:(gi + 1) * 512], ps[:], [Bp], [self.Bv[sub]])
        return f

    def cons_small(self, sub, ps, Bp):
        self.cp("dve", self.vs[:, sub + 1, :, 0:64], ps[:, 0:256].rearrange("p (j d) -> p j d", j=4), [Bp], [self.Bvs[sub + 1]])
        self.cp("dve", self.gsb[:, sub, :], ps[:, 256:264], [Bp], [self.Bgsb])

    def cons_gates_only(self, sub, ps, Bp):
        self.cp("dve", self.gsb[:, sub, :], ps[:, 0:8], [Bp], [self.Bgsb])

    def cons_ksw(self, c, ps, Bp):
        self.cp("act", self.ksT[:, c, 128:MT + 128], ps[:], [Bp], [self.Bks[c]])

    def cons_qs(self, gi):
        def f(c, ps, Bp):
            cc = gi * 4 + c
            self.act(self.qsT[:, cc, :], ps[:], AF.Copy, [Bp], [self.Bqs[cc]], scale=0.125)
        return f

    def cons_o(self, gi):
        def f(sub, ps, Bp):
            o = self.to1[:, sub, gi * 512:(gi + 1) * 512]
            self.act(o, ps[:], AF.Tanh, [Bp], [self.Bto[sub]], scale=0.5)
            self.tsc("pool", o, o, 1.0, 1.0, ALU.add, ALU.mult, [self.Bto[sub]], [self.Bto[sub]])
            self.tt("pool", o, o, self.gmlb[:, gi * 512:(gi + 1) * 512], ALU.mult, [self.Bto[sub], self.Bconst], [self.Bto[sub]])
        return f

    def cons_g(self, base):
        def f(c, ps, Bp):
            cc = base + c
            self.act(self.tg[:, cc, :], ps[:], AF.Tanh, [Bp], [self.Btg[cc]], scale=0.5)
        return f

    def gates_pre(self):
        G = self.gsc
        Bg = self.Bgsc
        gp = G[:, 0:2, :].rearrange("p a b -> p (a b)").rearrange("p (s e) -> p s e", s=4)
        self.tt("dve", gp, self.gsb[:], self.bif[:].unsqueeze(1).to_broadcast([128, 4, 8]), ALU.add,
                [self.Bgsb, self.Bconst], [Bg])
        tf = G[:, 2, :].rearrange("p (s h) -> p s h", s=4)
        self.act(tf, gp[:, :, 4:8], AF.Tanh, [Bg], [Bg], scale=0.5)
        lf = G[:, 3, :]
        self.act(lf, G[:, 2, :], AF.Ln, [Bg], [Bg], scale=0.5, bias=0.5)

    def gates(self):
        G = self.gsc
        Bg = self.Bgsc
        gp = G[:, 0:2, :].rearrange("p a b -> p (a b)").rearrange("p (s e) -> p s e", s=4)
        lf = G[:, 3, :]
        ps, Bp = self.ps_next()
        cm = self.cmat
        self.mm(ps[:, 0:16], cm[:, 1, :], lf, True, True, [Bg, self.Bconst], [Bp])
        self.mm(ps[:, 16:32], cm[:, 2, :], lf, True, True, [Bg, self.Bconst], [Bp])
        bt = G[:, 4:6, :].rearrange("p a b -> p (a b)")
        self.cp("dve", bt, ps[:, 0:32], [Bp], [Bg])
        b_, tot = G[:, 4, :], G[:, 5, :]
        bmb = G[:, 6, :]
        self.stt(bmb, tot, -0.5, b_, ALU.mult, ALU.add, [Bg], [Bg])
        ap_ = G[:, 7, :]
        self.tt("dve", ap_.rearrange("p (s h) -> p s h", s=4), gp[:, :, 0:4], bmb.rearrange("p (s h) -> p s h", s=4),
                ALU.subtract, [Bg], [Bg])
        e1, dlow, ebeta, e2b = G[:, 8, :], G[:, 9, :], G[:, 10, :], G[:, 11, :]
        self.act(e1, ap_, AF.Exp, [Bg], [Bg])
        self.act(dlow, bmb, AF.Exp, [Bg], [Bg], scale=-2.0)
        self.act(ebeta, tot, AF.Exp, [Bg], [Bg], scale=0.5)
        self.act(e2b, tot, AF.Exp, [Bg], [Bg])
        self.tsc("dve", G[:, 12, :], e1, DKS, None, ALU.mult, None, [Bg], [Bg])
        self.tt("dve", G[:, 13, :], e1, ebeta, ALU.mult, [Bg], [Bg])
        self.tsc("dve", G[:, 14, :], ebeta, DKS, None, ALU.mult, None, [Bg], [Bg])

    def mlstm_kt(self, sub):
        kk = self.rr("kw", 2)
        ps, Bp = self.ps_next()
        psb = ps[:].bitcast(BF16)
        for h in range(4):
            self.tr(psb[:, h * 128:(h + 1) * 128], self.qkT[:, 4 + h, sub * 128:(sub + 1) * 128], [self.Bqk[4 + h]], [Bp])
        return kk, ps, Bp

    def mlstm_dc(self, sub, kk, ps, Bp):
        G, Bg = self.gsc, self.Bgsc
        psb = ps[:].bitcast(BF16)
        wsc = G[:, 13, sub * 4:(sub + 1) * 4]
        self.tt("dve", self.kw[:, kk, :, :], psb[:, 0:512].rearrange("p (h d) -> p h d", h=4),
                wsc.unsqueeze(2).to_broadcast([128, 4, 128]), ALU.mult, [Bp, Bg], [self.Bkw[kk]])
        dC = []
        for hp in range(2):
            ps2, Bp2 = self.ps_next()
            for hh in range(2):
                h = hp * 2 + hh
                self.mm(ps2[:, hh * 256:(hh + 1) * 256], self.kw[:, kk, h, :], self.vml[:, sub, h * 256:(h + 1) * 256], True, True,
                        [self.Bkw[kk], self.Bv[sub]], [Bp2])
            dC.append((ps2, Bp2))
        psn, Bpn = self.ps_next()
        for h in range(4):
            self.mm(psn[:, h:h + 1], self.kw[:, kk, h, :], self.onecol[:], True, True, [self.Bkw[kk], self.Bconst], [Bpn])
        return dC, (psn, Bpn)

    def mlstm_update(self, sub, dC, dn):
        G, Bg = self.gsc, self.Bgsc
        for h in range(4):
            ps2, Bp2 = dC[h // 2]
            hh = h % 2
            idx = sub * 4 + h
            self.stt(self.C[:, h, :], self.C[:, h, :], G[:, 11, idx:idx + 1], ps2[:, hh * 256:(hh + 1) * 256], ALU.mult, ALU.add,
                     [self.BC[h], Bg, Bp2], [self.BC[h]])
        psn, Bpn = dn
        n, tmp = self.nst[:, 0:4], self.nst[:, 4:8]
        self.tt("dve", tmp, n, G[:, 11, sub * 4:(sub + 1) * 4], ALU.mult, [self.Bn, Bg], [self.Bn])
        self.tt("dve", n, tmp, psn[:, 0:4], ALU.add, [self.Bn, Bpn], [self.Bn])

    def mlstm_p1(self, sub, use_s32=False):
        G, Bg = self.gsc, self.Bgsc
        tsl = slice(sub * 128, (sub + 1) * 128)
        for h in range(4):
            idx = sub * 4 + h
            self.act(self.Cin[:, h, :], self.C[:, h, :], AF.Copy, [self.BC[h], Bg], [self.BCin[h]], scale=G[:, 14, idx:idx + 1])
        self.tt("dve", self.nin[:], self.nst[:, 0:4], G[:, 14, sub * 4:(sub + 1) * 4], ALU.mult, [self.Bn, Bg], [self.Bnin])
        if use_s32:
            Ssrc, BSl = self.S32, [self.BS32] + self.ByT[0] + self.ByT[1]
        else:
            psS, BpS = self.ps_next()
            for h in range(4):
                self.mm(psS[:, h * 128:(h + 1) * 128], self.qkT[:, 4 + h, tsl], self.qkT[:, h, tsl], True, True,
                        [self.Bqk[4 + h], self.Bqk[h]], [BpS])
            Ssrc, BSl = psS, [BpS]
        k = self.rr("pT", 2)
        for h in range(4):
            idx = sub * 4 + h
            self.stt(self.pT[:, k, h, :], Ssrc[:, h * 128:(h + 1) * 128], G[:, 12, idx:idx + 1], self.maskb[:], ALU.mult, ALU.mult,
                     BSl + [Bg, self.Bconst], [self.BpT[k]])
        kk = self.rr("kw", 2)
        ps, Bp = self.ps_next()
        psb = ps[:].bitcast(BF16)
        for h in range(4):
            self.tr(psb[:, h * 128:(h + 1) * 128], self.qkT[:, 4 + h, tsl], [self.Bqk[4 + h]], [Bp])
        wsc = G[:, 13, sub * 4:(sub + 1) * 4]
        self.tt("dve", self.kw[:, kk, :, :], psb[:, 0:512].rearrange("p (h d) -> p h d", h=4),
                wsc.unsqueeze(2).to_broadcast([128, 4, 128]), ALU.mult, [Bp, Bg], [self.Bkw[kk]])
        return dict(sub=sub, k=k, kk=kk)

    def mlstm_p2(self, cx):
        G, Bg = self.gsc, self.Bgsc
        sub, k, kk = cx["sub"], cx["k"], cx["kk"]
        tsl = slice(sub * 128, (sub + 1) * 128)
        nums = []
        for hp in range(2):
            ps2, Bp2 = self.ps_reserve()
            for hh in range(2):
                h = hp * 2 + hh
                o = ps2[:, hh * 256:(hh + 1) * 256]
                self.mm(o, self.pT[:, k, h, :], self.vml[:, sub, h * 256:(h + 1) * 256], True, False, [self.BpT[k], self.Bv[sub]], [Bp2])
                self.mm(o, self.qkT[:, h, tsl], self.Cin[:, h, :], False, True, [self.Bqk[h], self.BCin[h]], [Bp2])
            nums.append((ps2, Bp2))
        psd, Bpd = self.ps_next()
        for h in range(4):
            self.mm(psd[:, h:h + 1], self.pT[:, k, h, :], self.onecol[:], True, False, [self.BpT[k], self.Bconst], [Bpd])
            self.mm(psd[:, h:h + 1], self.qkT[:, h, tsl], self.nin[:, h:h + 1], False, True, [self.Bqk[h], self.Bnin], [Bpd])
        dC = []
        for hp in range(2):
            ps2, Bp2 = self.ps_next()
            for hh in range(2):
                h = hp * 2 + hh
                self.mm(ps2[:, hh * 256:(hh + 1) * 256], self.kw[:, kk, h, :], self.vml[:, sub, h * 256:(h + 1) * 256], True, True,
                        [self.Bkw[kk], self.Bv[sub]], [Bp2])
            dC.append((ps2, Bp2))
        psn, Bpn = self.ps_next()
        for h in range(4):
            self.mm(psn[:, h:h + 1], self.kw[:, kk, h, :], self.onecol[:], True, True, [self.Bkw[kk], self.Bconst], [Bpn])
        m = self.rr("msc", 2)
        M, Bm = self.msc, self.Bmsc[m]
        for h in range(4):
            ps2, Bp2 = nums[h // 2]
            hh = h % 2
            self.act(self.junk[:, m, :], ps2[:, hh * 256:(hh + 1) * 256], AF.Square, [Bp2], [self.Bjunk[m], Bm],
                     accum_out=M[:, m, 0, h:h + 1], scale=1.0 / 16)
        self.act(M[:, m, 1, :], psd[:, 0:4], AF.Square, [Bpd], [Bm])
        self.tt("dve", M[:, m, 2, :], M[:, m, 1, :], G[:, 9, sub * 4:(sub + 1) * 4], ALU.max, [Bm, Bg], [Bm])
        self.stt(M[:, m, 3, :], M[:, m, 2, :], EPS, M[:, m, 0, :], ALU.mult, ALU.add, [Bm], [Bm])
        self.tt("pool", M[:, m, 6, :], M[:, m, 3, :], self.mhalf[:, 0:4], ALU.pow, [Bm, self.Bconst], [Bm])
        self.mlstm_update(sub, dC, (psn, Bpn))
        cx["nums"], cx["m"] = nums, m
        return cx

    def mlstm_p3(self, cx):
        sub, nums, m = cx["sub"], cx["nums"], cx["m"]
        M, Bm = self.msc, self.Bmsc[m]
        ky = self.rr("ya", 2)
        for h in range(4):
            ps2, Bp2 = nums[h // 2]
            hh = h % 2
            self.stt(self.ya[:, ky, h * 256:(h + 1) * 256], ps2[:, hh * 256:(hh + 1) * 256], M[:, m, 6, h:h + 1],
                     self.to1[:, sub, h * 256:(h + 1) * 256], ALU.mult, ALU.mult, [Bp2, Bm, self.Bto[sub]], [self.Bya[ky]])
        for _, Bp2 in nums:
            self.ps_release(Bp2)
        cx["ky"] = ky

    def mlstm_p3b(self, cx):
        ky = cx["ky"]
        self.transpose_to(self.ya[:, ky, :], self.Bya[ky], 0, cx["sub"])

    def transpose_to(self, src, Bsrc, which, sub):
        ps, Bp = self.ps_next()
        psb = ps[:].bitcast(BF16)
        for kc in range(KC):
            self.tr(psb[:, kc * 128:(kc + 1) * 128], src[:, kc * 128:(kc + 1) * 128], [Bsrc], [Bp])
        self.cp("act", self.yT[:, which * 8:(which + 1) * 8, sub * 128:(sub + 1) * 128], psb.rearrange("p (k t) -> p k t", k=KC),
                [Bp], [self.ByT[which][sub]])

    def swa_p1(self, sub, j, first_block):
        mp = self.mpair[:, 0 if first_block else 1, :, :]
        mpb = mp.unsqueeze(2).to_broadcast([128, 2, 2, 128])
        ke = j
        banks = [self.ps_next(), self.ps_next()]
        for kb in range(2):
            ks = slice((sub + kb) * 128, (sub + kb + 1) * 128)
            for r in range(2):
                ps, Bp = banks[r]
                rs = slice(r * 64, (r + 1) * 64)
                self.mm(ps[:, kb * 256:(kb + 1) * 256].rearrange("p (c q) -> p c q", c=2),
                        self.ksT[rs, j, ks], self.qsT[rs, 2 * j:2 * j + 2, sub * 128:(sub + 1) * 128], True, True,
                        [self.Bks[j], self.Bqs[2 * j], self.Bqs[2 * j + 1]], [Bp])
        for r in range(2):
            ps, Bp = banks[r]
            Er = self.E[:, ke, r, :]
            self.act(Er, ps[:], AF.Exp, [Bp], [self.BE[ke][r]])
            Er4 = Er.rearrange("p (b c q) -> p b c q", b=2, c=2)
            self.tt("pool", Er4, Er4, mpb, ALU.mult, [self.BE[ke][r], self.Bconst], [self.BE[ke][r]])

    def swa_p2(self, sub, j, ky):
        ke = j
        pso, Bpo = self.ps_next()
        for g in range(4):
            c, r = g // 2, g % 2
            for kb in range(2):
                lhsT = self.E[:, ke, r, :].rearrange("p (b c q) -> p b c q", b=2, c=2)[:, kb, c, :]
                self.mm(pso[:, g * 65:(g + 1) * 65], lhsT, self.vs[:, sub + kb, j, :], kb == 0, kb == 1,
                        [self.BE[ke][r], self.Bvs[sub + kb]], [Bpo])
        s_ = self.rr("ssc", 2)
        SS, Bs = self.ssc, self.Bssc[s_]
        o3 = pso[:, 0:260].rearrange("p (g d) -> p g d", g=4)
        self.tt("dve", SS[:, s_, 0, :].unsqueeze(2), o3[:, :, 64:65], self.esink[:, 4 * j:4 * j + 4].unsqueeze(2), ALU.add,
                [Bpo, self.Bconst], [Bs])
        self.op("dve", lambda e, s_=s_: e.reciprocal(out=SS[:, s_, 1, :], in_=SS[:, s_, 0, :]), [Bs], [Bs])
        self.tt("dve", self.yb[:, ky, j * 256:(j + 1) * 256].rearrange("p (g d) -> p g d", g=4), o3[:, :, 0:64],
                SS[:, s_, 1, :].unsqueeze(2).to_broadcast([128, 4, 64]), ALU.mult, [Bpo, Bs], [self.Byb[ky]])

    def attention(self, sub, first_block, deferred):
        ky = self.rr("yb", 2)
        for f in deferred[:1]:
            f()
        cx = self.mlstm_p1(sub, use_s32=first_block)
        self.swa_p1(sub, 0, first_block)
        self.swa_p1(sub, 1, first_block)
        for f in deferred[1:]:
            f()
        del deferred[:]
        cx = self.mlstm_p2(cx)
        self.swa_p2(sub, 0, ky)
        self.swa_p2(sub, 1, ky)
        self.swa_p1(sub, 2, first_block)
        self.swa_p1(sub, 3, first_block)
        self.swa_p2(sub, 2, ky)
        self.swa_p2(sub, 3, ky)
        deferred.append(lambda: self.mlstm_p3(cx))
        deferred.append(lambda: self.mlstm_p3b(cx))
        deferred.append(lambda: self.transpose_to(self.yb[:, ky, :], self.Byb[ky], 1, sub))

    def swa_shift(self):
        for j in range(4):
            self.cp("pool", self.ksT[:, j, 0:128], self.ksT[:, j, MT:MT + 128], [self.Bks[j]], [self.Bks[j]])
        self.cp("pool", self.vs[:, 0, :, 0:64], self.vs[:, 4, :, 0:64], [self.Bvs[4]], [self.Bvs[0]])

    def merge(self):
        for half in range(2):
            wa, Bwa = self.w_next(G_WA0 + half)
            wbb, Bwb = self.w_next(G_WB0 + half, prev_live=True)
            wav = wa[:].rearrange("p (k n) -> p k n", k=KC)
            wbv = wbb[:].rearrange("p (k n) -> p k n", k=KC)
            for c in range(4):
                oc = half * 4 + c
                k = self.rr("tmpAB", 2)
                psa, Bpa = self.ps_next()
                for kc in range(KC):
                    self.mm(psa[:], wav[:, kc, c * 128:(c + 1) * 128], self.yT[:, kc, :], kc == 0, kc == KC - 1, [Bwa] + self.ByT[0], [Bpa])
                psb_, Bpb = self.ps_next()
                for kc in range(KC):
                    self.mm(psb_[:], wbv[:, kc, c * 128:(c + 1) * 128], self.yT[:, 8 + kc, :], kc == 0, kc == KC - 1, [Bwb] + self.ByT[1], [Bpb])
                self.stt(self.tmpA[:, k, :], self.tg[:, oc, :], 1.0, psa[:], ALU.add, ALU.mult, [self.Btg[oc], Bpa], [self.BtA[k]])
                self.stt(self.tmpB[:, k, :], self.tg[:, 8 + oc, :], 1.0, psb_[:], ALU.add, ALU.mult, [self.Btg[8 + oc], Bpb], [self.BtB[k]])
                self.tt("pool", self.hT[:, oc, :], self.tmpA[:, k, :], self.tmpB[:, k, :], ALU.add, [self.BtA[k], self.BtB[k]], self.BhT)

    def tm_residual(self, groups, lhs_src, Blhs_fn, nk, scale=1.0, after_sub=None):
        ws = [self.w_next(groups[0]), self.w_next(groups[1], prev_live=True)]
        for sub in range(NSUB):
            for half in range(2):
                w, Bw = ws[half]
                wv = w[:].rearrange("p (k n) -> p k n", k=nk)
                ps, Bp = self.ps_next()
                for kc in range(nk):
                    self.mm(ps[:], lhs_src[:, kc, sub * 128:(sub + 1) * 128], wv[:, kc, :], kc == 0, kc == nk - 1, [Bw] + Blhs_fn(sub), [Bp])
                xs = self.xs[:, sub, half * 512:(half + 1) * 512]
                self.stt(xs, ps[:], scale, xs, ALU.mult, ALU.add, [self.Bx[sub], Bp], [self.Bx[sub]])
            if after_sub is not None:
                after_sub(sub)

    def after_wout(self, sub):
        self.norm_stats(sub)
        if sub >= 2:
            self.norm_apply(sub - 2, 2)
        if sub == NSUB - 1:
            self.norm_apply(sub - 1, 2)
            self.norm_apply(sub, 2)

    def mlp(self):
        uT = []
        for c in range(8):
            uT.append((self.qsT[:, c, :], self.Bqs[c]))
        for c in range(8):
            uT.append((self.qkT[:, c, :], self.Bqk[c]))
        for c in range(16):
            uT.append((self.tg[:, c, :], self.Btg[c]))
        self.uT = uT
        for g in range(8):
            def cons(c, ps, Bp, g=g):
                cc = g * 4 + c
                k = self.rr("rl", 2)
                self.act(self.rl[:, k, :], ps[:], AF.Relu, [Bp], [self.Brl[k]])
                u, Bu = uT[cc]
                self.tt("pool", u, self.rl[:, k, :], self.rl[:, k, :], ALU.mult, [self.Brl[k]], [Bu])
            self.proj_fm(G_UP0 + g, cons)
        for half in range(2):
            pss = [self.ps_next() for _ in range(NSUB)]
            for kg in range(4):
                w, Bw = self.w_next(G_DN0 + half * 4 + kg)
                wv = w[:].rearrange("p (k n) -> p k n", k=KC)
                for sub in range(NSUB):
                    ps, Bp = pss[sub]
                    for kc in range(KC):
                        u, Bu = uT[kg * 8 + kc]
                        self.mm(ps[:], u[:, sub * 128:(sub + 1) * 128], wv[:, kc, :], kg == 0 and kc == 0, kg == 3 and kc == KC - 1, [Bw, Bu], [Bp])
            for sub in range(NSUB):
                ps, Bp = pss[sub]
                xs = self.xs[:, sub, half * 512:(half + 1) * 512]
                self.tt("dve", xs, xs, ps[:], ALU.add, [self.Bx[sub], Bp], [self.Bx[sub]])
                if half == 1:
                    self.norm_stats(sub)

    def ple_p(self, t):
        for sub in range(NSUB):
            r0 = t * MT + sub * 128
            self.dma(self.pld[:, sub, :], self.p_main[r0:r0 + 128, :], [], self.Bpld)
        self.tsc("pool", self.pbf[:], self.pld, 0.5, 0.0, ALU.mult, ALU.add, self.Bpld, [self.Bpbf])
        ps, Bp = self.ps_next()
        psb = ps[:].bitcast(BF16)
        for sub in range(NSUB):
            for c in range(2):
                self.tr(psb[:, (c * 4 + sub) * 128:(c * 4 + sub + 1) * 128], self.pbf[:, sub, c * 128:(c + 1) * 128], [self.Bpbf], [Bp])
        self.cp("act", self.ppT[:].rearrange("p c t -> p (c t)"), psb[:, 0:1024], [Bp], [self.BppT])

    def ple(self, t, nxt):
        self.norm_stage(3, stats_done=True)
        if nxt:
            for sub in range(NSUB):
                self.norm_stats(sub, tmp=True)
        wp = None
        for half in range(2):
            w, Bw = self.w_next(G_PG0 + half, prev_live=(half == 1))
            wv = w[:].rearrange("p (k n) -> p k n", k=KC)
            if half == 0:
                wp, Bwp = self.w_next(G_PP, prev_live=True)
                wpv = wp[:, 0:2048].rearrange("p (k n) -> p k n", k=2)
            for sub in range(NSUB):
                psg, Bpg = self.ps_next()
                for kc in range(KC):
                    self.mm(psg[:], self.hT[:, kc, sub * 128:(sub + 1) * 128], wv[:, kc, :], kc == 0, kc == KC - 1, [Bw, self.BhT[sub]], [Bpg])
                psp, Bpp = self.ps_next()
                for kc in range(2):
                    self.mm(psp[:], self.ppT[:, kc, sub * 128:(sub + 1) * 128], wpv[:, kc, half * 512:(half + 1) * 512], kc == 0, kc == 1,
                            [Bwp, self.BppT], [Bpp])
                k = self.rr("tgp", 2)
                self.act(self.tgp[:, k, :], psg[:], AF.Tanh, [Bpg], [self.Btgp[k]], scale=0.5)
                self.stt(self.ptmp[:, k, :], self.tgp[:, k, :], 1.0, psp[:], ALU.add, ALU.mult, [self.Btgp[k], Bpp], [self.Bptmp[k]])
                xs = self.xs[:, sub, half * 512:(half + 1) * 512]
                self.tt("pool", xs, xs, self.ptmp[:, k, :], ALU.add, [self.Bx[sub], self.Bptmp[k]], [self.Bx[sub]])

    def flush_out(self, n=None):
        while self.pending_out and (n is None or n > 0):
            o, src, Bs = self.pending_out.pop(0)
            self.dma(o, src, Bs, [self.Bout])
            if n is not None:
                n -= 1

    def final(self, t):
        for sub in range(NSUB):
            self.norm_stats(sub)
        for sub in range(NSUB):
            ko = self.rr("ot", 4)
            self.stt(self.ot[ko], self.xs[:, sub, :], self.nsc[:, sub, 2:3], self.gfin[:], ALU.mult, ALU.mult,
                     [self.Bx[sub], self.Bnsc[sub], self.Bconst], self.Bot[ko])
            r0 = t * MT + sub * 128
            self.pending_out.append((self.out_d[r0:r0 + 128, :], self.ot[ko], self.Bot[ko]))

    def prefix_proj(self, t, last):
        self.mark('prefix')
        if last:
            self.proj_tm(G_SMALL, self.cons_small, ncols=264)
        else:
            self.proj_tm(G_SMALL, self.cons_gates_only, ncols=8, col0=256)
        self.gates_pre()
        self.proj_conv(G_KML, 4)
        if not last:
            self.load_x(self.x_pre, t + 1)
        else:
            self.load_x(self.x_main, 0)
        self.gates()
        self.proj_tm(G_V0, self.cons_v(0))
        for sub in range(NSUB):
            self.norm_stats(sub)
        self.proj_tm(G_V1, self.cons_v(1))
        if last:
            self.proj_conv(G_QML, 0)
            self.proj_fm(G_KSW, self.cons_ksw)

    def prefix_state(self, t, last):
        for pair in range(2):
            st = [self.mlstm_kt(sub) for sub in (2 * pair, 2 * pair + 1)]
            for sub, (kk, ps, Bp) in zip((2 * pair, 2 * pair + 1), st):
                dC, dn = self.mlstm_dc(sub, kk, ps, Bp)
                self.mlstm_update(sub, dC, dn)
        if last:
            self.swa_shift()

    def main_tile(self, t):
        if t == 0:
            self.special_s0()
        self.mark('proj')
        self.proj_conv(G_KML, 4)
        self.proj_tm(G_V0, self.cons_v(0))
        self.proj_tm(G_V1, self.cons_v(1))
        self.proj_tm(G_SMALL, self.cons_small, ncols=264)
        self.gates_pre()
        self.proj_conv(G_QML, 0)
        self.proj_fm(G_KSW, self.cons_ksw)
        self.gates()
        self.proj_fm(G_QS0, self.cons_qs(0))
        self.proj_fm(G_QS1, self.cons_qs(1))
        self.proj_tm(G_O0, self.cons_o(0))
        self.proj_tm(G_O1, self.cons_o(1))
        self.proj_fm(G_GA0, self.cons_g(0))
        self.proj_fm(G_GA1, self.cons_g(4))
        self.proj_fm(G_GB0, self.cons_g(8))
        self.proj_fm(G_GB1, self.cons_g(12))
        if STOP == "proj":
            return
        self.flush_out()
        if self.copy_x_pending:
            for sub in range(NSUB):
                self.dma(self.xs[:, sub, :], self.xtmp[:, sub, :], self.Bxtmp[sub], [self.Bx[sub]])
            self.copy_x_pending = False
        self.mark('attn')
        deferred = []
        for sub in range(NSUB):
            self.attention(sub, (t == 0 and sub == 0), deferred)
        for f in deferred:
            f()
        self.swa_shift()
        if STOP == "mix":
            return
        self.mark('merge')
        self.ple_p(t)
        self.merge()
        if STOP == "merge":
            return
        nxt = t + 1 < self.nmain
        if nxt:
            for sub in range(NSUB):
                r0 = (t + 1) * MT + sub * 128
                self.dma(self.xtmp[:, sub, :], self.x_main[r0:r0 + 128, :], [], self.Bxtmp[sub])
        self.mark('wout')
        self.tm_residual([G_WO0, G_WO1], self.hT, lambda sub: [self.BhT[sub]], KC, scale=0.5, after_sub=self.after_wout)
        if STOP == "wout":
            return
        self.mark('mlp')
        self.mlp()
        if STOP == "mlp":
            return
        self.mark('ple')
        self.ple(t, nxt)
        if STOP == "ple":
            return
        if nxt:
            for sub in range(NSUB):
                self.norm_apply(sub, 0, tmp=True)
        self.mark('final')
        self.final(t)
        self.copy_x_pending = nxt

    def _build(self):
        self.w_init()
        self.setup()
        if STOP == "setup":
            return
        nfirst = len(PRE_LAST_GROUPS)
        self.convert_weights(CONV_ORDER[:nfirst])
        if STOP == "cvt0":
            return
        self.cvt_queue = list(CONV_ORDER[nfirst:])
        self.load_x(self.x_pre, 0)
        self.norm_stage(0)
        for t in range(self.npre):
            last = t == self.npre - 1
            self.prefix_proj(t, last)
            self.norm_stage(0, stats_done=True)
            self.prefix_state(t, last)
            self.cvt_one()
        while self.cvt_queue:
            self.cvt_one()
        if STOP == "cvt":
            return
        for t in range(self.nmain):
            self.main_tile(t)
        self.flush_out()


def _grp(w2d, nk=KC):
    ncols = w2d.shape[1]
    a = w2d.reshape(nk, 128, ncols).transpose(1, 0, 2).reshape(128, nk * ncols)
    if a.shape[1] < GSZ:
        a = np.concatenate([a, np.zeros((128, GSZ - a.shape[1]), np.float32)], axis=1)
    return a


def pack_weights(inp):
    w_in = np.asarray(inp["w_in"][0], np.float32)
    wg = np.zeros((NG, 128, GSZ), np.float32)
    wg[G_KML] = _grp(w_in[:, C_QK + 512:C_QK + 1024])
    wg[G_V0] = _grp(w_in[:, C_V:C_V + 512])
    wg[G_V1] = _grp(w_in[:, C_V + 512:C_V + 1024])
    small = np.zeros((D, 512), np.float32)
    small[:, 0:256] = w_in[:, C_VS:C_VS + 256]
    small[:, 256:264] = w_in[:, C_IF:C_IF + 8]
    wg[G_SMALL] = _grp(small)
    wg[G_QML] = _grp(w_in[:, C_QK:C_QK + 512])
    ksd = np.zeros((D, 512), np.float32)
    for j in range(4):
        kj = w_in[:, C_KS + j * 64:C_KS + (j + 1) * 64]
        ksd[:, j * 128:j * 128 + 64] = kj
        ksd[:, j * 128 + 64:(j + 1) * 128] = kj
    wg[G_KSW] = _grp(ksd)
    for i in range(2):
        wg[G_QS0 + i] = _grp(w_in[:, C_QS + i * 512:C_QS + (i + 1) * 512])
        wg[G_O0 + i] = _grp(w_in[:, C_O + i * 512:C_O + (i + 1) * 512])
        wg[G_GA0 + i] = _grp(w_in[:, C_GA + i * 512:C_GA + (i + 1) * 512])
        wg[G_GB0 + i] = _grp(w_in[:, C_GB + i * 512:C_GB + (i + 1) * 512])
        wg[G_WA0 + i] = _grp(np.asarray(inp["w_branch_a"][0])[:, i * 512:(i + 1) * 512])
        wg[G_WB0 + i] = _grp(np.asarray(inp["w_branch_b"][0])[:, i * 512:(i + 1) * 512])
        wg[G_WO0 + i] = _grp(np.asarray(inp["w_out"][0])[:, i * 512:(i + 1) * 512])
        wg[G_PG0 + i] = _grp(np.asarray(inp["w_ple_gate"][0])[:, i * 512:(i + 1) * 512])
    w_up = np.asarray(inp["w_up"][0])
    for g in range(8):
        wg[G_UP0 + g] = _grp(w_up[:, g * 512:(g + 1) * 512])
    w_dn = np.asarray(inp["w_down"][0])
    for half in range(2):
        for kg in range(4):
            wg[G_DN0 + half * 4 + kg] = _grp(w_dn[kg * 1024:(kg + 1) * 1024, half * 512:(half + 1) * 512])
    wg[G_PP] = _grp(np.asarray(inp["w_ple_proj"][0]), nk=2)
    return wg


def pack_common(inp):
    col = lambda v: np.ascontiguousarray(np.asarray(v, np.float32).reshape(KC, 128).T)
    gains = np.stack([col(inp["norm_mix_g"][0]), col(inp["mlstm_norm_g"][0]), col(inp["norm_mlp_g"][0]), col(inp["norm_ple_g"][0])], axis=1)
    convw = np.ascontiguousarray(np.asarray(inp["conv_qk"][0], np.float32).reshape(4, KC, 128).transpose(2, 1, 0))
    bif = np.ascontiguousarray(np.broadcast_to(np.asarray(inp["b_if"][0], np.float32)[None, :], (128, 8)))
    sinks = np.ascontiguousarray(np.broadcast_to(np.asarray(inp["sinks"][0], np.float32)[None, :], (128, 16)))
    gfin = np.ascontiguousarray(np.broadcast_to(np.asarray(inp["final_norm_g"], np.float32)[None, :], (128, D)))
    gmlb = np.ascontiguousarray(np.broadcast_to(np.asarray(inp["mlstm_norm_g"][0], np.float32)[None, :], (128, D)))
    ii = np.arange(128)
    mask = (ii[:, None] <= ii[None, :]).astype(np.float32)
    cmat = np.stack([np.eye(128, dtype=np.float32), mask, np.ones((128, 128), np.float32)], axis=1)
    return dict(wg=pack_weights(inp), gains=np.ascontiguousarray(gains), convw=convw, bif=bif, sinks=sinks, gfin=gfin, gmlb=gmlb,
                cmat=np.ascontiguousarray(cmat)), mask


def mpair_for(mask, first_half):
    mp = np.zeros((128, 2, 2, 128), np.float32)
    mp[:, 1, 0, :] = 1.0 - mask
    mp[:, 1, 1, :] = mask
    mp[:, 0, 1, :] = mask
    mp[:, 0, 0, :] = 0.0 if first_half else (1.0 - mask)
    return mp


_PROG_CACHE = {}


def get_prog(npre, nmain):
    key = (npre, nmain)
    if key not in _PROG_CACHE:
        _PROG_CACHE[key] = Prog(npre, nmain)
    return _PROG_CACHE[key]


def kernel(**inputs):
    x = np.asarray(inputs["x"], np.float32)
    p = np.asarray(inputs["p"], np.float32)[0]
    Bsz, S, _ = x.shape
    half = S // 2
    npre = nmain = half // MT
    common, mask = pack_common(inputs)
    prog = get_prog(npre, nmain)
    in_maps = []
    for c in range(8):
        b, h = c // 2, c % 2
        m = dict(common)
        m["x_pre"] = np.zeros((half, D), np.float32) if h == 0 else np.ascontiguousarray(x[b, 0:half])
        m["x_main"] = np.ascontiguousarray(x[b, h * half:(h + 1) * half])
        m["p_main"] = np.ascontiguousarray(p[b, h * half:(h + 1) * half])
        m["mpair"] = mpair_for(mask, h == 0)
        in_maps.append(m)
    res = run_bass_kernel_spmd(prog.nc, in_maps, core_ids=list(range(8)))
    out = np.empty((Bsz, S, D), np.float32)
    for c in range(8):
        b, h = c // 2, c % 2
        out[b, h * half:(h + 1) * half] = res.results[c]["out"]
    return out
```
